# Optimizing a Trainium2 kernel written in Bass

```python
import math
import jax, jax.numpy as jnp
from jax import lax
import numpy as np

D_MODEL = 1024
BATCH = 2
SEQ = 16384
DEPTH = 2

GRID_W = 64
CTX_LEN = 256
NORM_EPS = 1e-6

HEAD_DIM = 64
DIFF_HEADS = 4
DIFF_VDIM = 2 * HEAD_DIM
DIFF_WIDTH = DIFF_HEADS * DIFF_VDIM
DIFF_PROJ = 3 * DIFF_WIDTH
DIFF_SUBLN_EPS = 1e-5
Q_BLOCK = 128
ROPE_BASE = 10000.0
ROPE_AXIS_DIM = HEAD_DIM // 2

RWKV_HEADS = 8
RWKV_WIDTH = RWKV_HEADS * HEAD_DIM
DECAY_LORA = 64
ICLR_LORA = 64
GATE_LORA = 128
RWKV_PROJ = 3 * RWKV_WIDTH + DECAY_LORA + ICLR_LORA + GATE_LORA
RWKV_SPLITS = (RWKV_WIDTH, 2 * RWKV_WIDTH, 3 * RWKV_WIDTH, 3 * RWKV_WIDTH + DECAY_LORA, 3 * RWKV_WIDTH + DECAY_LORA + ICLR_LORA)
RWKV_GN_EPS = 64e-5

EVEN_PROJ = DIFF_PROJ + RWKV_PROJ
MIX_WIDTH = DIFF_WIDTH + RWKV_WIDTH

SSM_INNER = 2 * D_MODEL
SSM_HEAD_DIM = 64
SSM_HEADS = SSM_INNER // SSM_HEAD_DIM
SSM_GROUPS = 4
SSM_HEADS_PER_GROUP = SSM_HEADS // SSM_GROUPS
SSM_STATE = 128
SSM_CONV = 5
SSM_CHUNK = 128
SSM_CONV_CH = SSM_INNER + 2 * SSM_GROUPS * SSM_STATE
SSM_PROJ = SSM_INNER + SSM_CONV_CH + 2 * SSM_HEADS
SSM_NORM_EPS = 1e-5

FFN_HIDDEN = 2816
FFN_CONV = 3

N_EVEN = (DEPTH + 1) // 2
N_ODD = DEPTH // 2

kernel_name = 'hybrid_diffattn_rwkv7_mamba2_dit'


def rmsnorm(x, w, eps=NORM_EPS):
    xf = x.astype(jnp.float32)
    y = xf * lax.rsqrt(jnp.mean(xf * xf, axis=-1, keepdims=True) + eps)
    return (y * w.astype(jnp.float32)).astype(x.dtype)


def modulate(h, shift, scale):
    return h * (1 + scale) + shift


def dwconv_centred(x, w, b):
    k = w.shape[0]
    pad = k // 2
    y = lax.conv_general_dilated(x, w[:, None, :].astype(x.dtype), window_strides=(1,), padding=[(pad, pad)],
                                 dimension_numbers=('NWC', 'WIO', 'NWC'), feature_group_count=x.shape[-1])
    return y + b


def token_shift_centred(p, mu_prev, mu_next):
    prev = jnp.pad(p[:, :-1], ((0, 0), (1, 0), (0, 0)))
    nxt = jnp.pad(p[:, 1:], ((0, 0), (0, 1), (0, 0)))
    return p + mu_prev * (prev - p) + mu_next * (nxt - p)


def axial_rope_angles(rows):
    row = jnp.repeat(jnp.arange(rows, dtype=jnp.float32), GRID_W)
    col = (jnp.arange(rows * GRID_W) % GRID_W).astype(jnp.float32)
    inv = ROPE_BASE ** (-jnp.arange(0, ROPE_AXIS_DIM, 2, dtype=jnp.float32) / ROPE_AXIS_DIM)
    return (row[:, None] * inv)[None, :, None, None, :], (col[:, None] * inv)[None, :, None, None, :]


def rope_1d(x, ang):
    cos = jnp.cos(ang).astype(x.dtype)
    sin = jnp.sin(ang).astype(x.dtype)
    x1, x2 = jnp.split(x, 2, axis=-1)
    return jnp.concatenate([x1 * cos - x2 * sin, x2 * cos + x1 * sin], axis=-1)


def rope_axial(x, ang_row, ang_col):
    xr, xc = jnp.split(x, 2, axis=-1)
    return jnp.concatenate([rope_1d(xr, ang_row), rope_1d(xc, ang_col)], axis=-1)


def split_diff(d):
    b, t, _ = d.shape
    q, k, v = jnp.split(d, 3, axis=-1)
    return (q.reshape(b, t, DIFF_HEADS, 2, HEAD_DIM), k.reshape(b, t, DIFF_HEADS, 2, HEAD_DIM),
            v.reshape(b, t, DIFF_HEADS, DIFF_VDIM))


def diff_attention(q, k_all, v_all, lam):
    b, t = q.shape[:2]
    nb = t // Q_BLOCK
    qb = jnp.moveaxis(q.reshape(b, nb, Q_BLOCK, DIFF_HEADS, 2, HEAD_DIM), 1, 0)
    scale = HEAD_DIM ** -0.5

    def block(q_blk):
        s = jnp.einsum('bqhmd,bkhmd->bhmqk', q_blk, k_all).astype(jnp.float32) * scale
        p = jax.nn.softmax(s, axis=-1)
        a = (p[:, :, 0] - lam * p[:, :, 1]).astype(v_all.dtype)
        return jnp.einsum('bhqk,bkhv->bqhv', a, v_all)

    out = lax.map(block, qb)
    return jnp.moveaxis(out, 0, 1).reshape(b, t, DIFF_HEADS, DIFF_VDIM)


def diff_output(a, subln_w, lam_init):
    b, t = a.shape[:2]
    return (rmsnorm(a, subln_w, DIFF_SUBLN_EPS) * (1 - lam_init)).reshape(b, t, DIFF_WIDTH)


def rwkv7_scan(r, decay, k, v, kk, a, state0, reverse):
    def step(s, inp):
        r_t, w_t, k_t, v_t, kk_t, a_t = inp
        sa = jnp.einsum('bhij,bhj->bhi', s, kk_t)
        s = (s * w_t[:, :, None, :] - sa[..., None] * (kk_t * a_t)[:, :, None, :]
             + v_t[..., None] * k_t[:, :, None, :])
        return s, jnp.einsum('bhij,bhj->bhi', s, r_t)

    xs = tuple(jnp.moveaxis(z, 1, 0) for z in (r, decay, k, v, kk, a))
    s, y = lax.scan(step, state0, xs, reverse=reverse)
    return jnp.moveaxis(y, 0, 1), s


def rwkv_mixer(pb, s0_f, s0_b, rw):
    b, t, _ = pb.shape

    def heads(z):
        return z.reshape(b, t, RWKV_HEADS, HEAD_DIM)

    pb = token_shift_centred(pb, rw['mu'][0], rw['mu'][1])
    r, k, v, w_in, a_in, g_in = jnp.split(pb, RWKV_SPLITS, axis=-1)
    kk = heads(k * rw['k_k']).astype(jnp.float32)
    kk = (kk * lax.rsqrt(jnp.maximum(jnp.sum(kk * kk, axis=-1, keepdims=True), 1e-12))).astype(pb.dtype)
    r_h, v_h = heads(r), heads(v)
    ys, ks, states = [], [], []
    for d, reverse, s0 in ((0, False, s0_f), (1, True, s0_b)):
        logw = -jax.nn.softplus(-(rw['w0'][d] + jnp.tanh(w_in) @ rw['w_up'][d])) - 0.5
        decay = jnp.exp(-jnp.exp(logw.astype(jnp.float32))).astype(pb.dtype)
        a = jax.nn.sigmoid(rw['a0'][d] + a_in @ rw['a_up'][d])
        k_d = heads(k * (1 + (a - 1) * rw['k_a']))
        y, s = rwkv7_scan(r_h, heads(decay), k_d, v_h, kk, heads(a), s0, reverse)
        ys.append(y)
        ks.append(k_d)
        states.append(s)
    y = (ys[0] + ys[1]).astype(jnp.float32)
    mu = jnp.mean(y, axis=-1, keepdims=True)
    var = jnp.mean(jnp.square(y - mu), axis=-1, keepdims=True)
    yn = ((y - mu) * lax.rsqrt(var + RWKV_GN_EPS)).reshape(b, t, RWKV_WIDTH) * rw['ln_w'] + rw['ln_b']
    bonus = jnp.sum(r_h * (ks[0] + ks[1]) * rw['r_k'], axis=-1, keepdims=True) * v_h
    g = jax.nn.sigmoid(g_in) @ rw['g_up']
    out = (yn.astype(pb.dtype) + bonus.reshape(b, t, RWKV_WIDTH)) * g
    return out, states[0], states[1]


def even_mixer(h_lat, h_ctx, ang_row, ang_col, layer_idx, w_in, w_out, lam_vecs, subln_w, rw, ctx_out):
    b, t, _ = h_lat.shape
    p_lat = h_lat @ w_in
    p_ctx = h_ctx @ w_in
    lam_init = 0.8 - 0.6 * math.exp(-0.3 * layer_idx)
    lv = lam_vecs.astype(jnp.float32)
    lam = jnp.exp(jnp.sum(lv[0] * lv[1])) - jnp.exp(jnp.sum(lv[2] * lv[3])) + lam_init
    q_l, k_l, v_l = split_diff(p_lat[..., :DIFF_PROJ])
    q_c, k_c, v_c = split_diff(p_ctx[..., :DIFF_PROJ])
    q_l = rope_axial(q_l, ang_row, ang_col)
    k_l = rope_axial(k_l, ang_row, ang_col)
    k_all = jnp.concatenate([k_c, k_l], axis=1)
    v_all = jnp.concatenate([v_c, v_l], axis=1)
    att_l = diff_output(diff_attention(q_l, k_all, v_all, lam), subln_w, lam_init)
    zero_state = jnp.zeros((h_ctx.shape[0], RWKV_HEADS, HEAD_DIM, HEAD_DIM), h_lat.dtype)
    rw_c, s_f, s_b = rwkv_mixer(p_ctx[..., DIFF_PROJ:], zero_state, zero_state, rw)
    rw_l, _, _ = rwkv_mixer(p_lat[..., DIFF_PROJ:], s_f, s_b, rw)
    o_lat = jnp.concatenate([att_l, rw_l], axis=-1) @ w_out
    o_ctx = None
    if ctx_out:
        att_c = diff_output(diff_attention(q_c, k_c, v_c, lam), subln_w, lam_init)
        o_ctx = jnp.concatenate([att_c, rw_c], axis=-1) @ w_out
    return o_lat, o_ctx


def mamba_inputs(h, w_in, conv_w, conv_b):
    b, t, _ = h.shape
    p = h @ w_in
    z, xbc, dt = jnp.split(p, [SSM_INNER, SSM_INNER + SSM_CONV_CH], axis=-1)
    xbc = jax.nn.silu(dwconv_centred(xbc, conv_w, conv_b))
    xs, bm, cm = jnp.split(xbc, [SSM_INNER, SSM_INNER + SSM_GROUPS * SSM_STATE], axis=-1)
    shp = (b, t, SSM_GROUPS, SSM_HEADS_PER_GROUP, SSM_HEAD_DIM)
    return (z.reshape(shp), xs.reshape(shp), bm.reshape(b, t, SSM_GROUPS, SSM_STATE),
            cm.reshape(b, t, SSM_GROUPS, SSM_STATE), dt.reshape(b, t, 2, SSM_HEADS))


def ssm_direction(xs, dt_raw, dt_bias, a_log):
    b, t = xs.shape[:2]
    dt = jax.nn.softplus(dt_raw.astype(jnp.float32) + dt_bias)
    da = (dt * -jnp.exp(a_log.astype(jnp.float32))).reshape(b, t, SSM_GROUPS, SSM_HEADS_PER_GROUP)
    xdt = xs * dt.reshape(b, t, SSM_GROUPS, SSM_HEADS_PER_GROUP)[..., None].astype(xs.dtype)
    return xdt, da


def ssd_scan(xdt, da, bm, cm, h0):
    b, t, g, e, p = xdt.shape
    nc = t // SSM_CHUNK
    dt = xdt.dtype
    causal = jnp.tril(jnp.ones((SSM_CHUNK, SSM_CHUNK), dtype=bool))[:, :, None, None]

    def chunks(z):
        return jnp.moveaxis(z.reshape((b, nc, SSM_CHUNK) + z.shape[2:]), 1, 0)

    def step(h, inp):
        x_c, a_c, b_c, c_c = inp
        cum = jnp.cumsum(a_c, axis=1)
        seg = cum[:, :, None] - cum[:, None, :]
        decay = jnp.exp(jnp.where(causal, seg, -jnp.inf)).astype(dt)
        cb = jnp.einsum('blgn,bsgn->blsg', c_c, b_c)
        y = jnp.einsum('blsge,bsgep->blgep', cb[..., None] * decay, x_c)
        y = y + jnp.einsum('blgn,bgepn->blgep', c_c, h) * jnp.exp(cum).astype(dt)[..., None]
        to_end = jnp.exp(cum[:, -1:] - cum).astype(dt)
        h = (h * jnp.exp(cum[:, -1]).astype(dt)[..., None, None]
             + jnp.einsum('bsgn,bsge,bsgep->bgepn', b_c, to_end, x_c)).astype(h.dtype)
        return h, y

    h, ys = lax.scan(step, h0, (chunks(xdt), chunks(da), chunks(bm), chunks(cm)))
    return jnp.moveaxis(ys, 0, 1).reshape(b, t, g, e, p), h


def ssd_state(xdt, da, bm):
    cum = jnp.cumsum(da, axis=1)
    to_end = jnp.exp(cum[:, -1:] - cum).astype(xdt.dtype)
    return jnp.einsum('btgn,btge,btgep->bgepn', bm, to_end, xdt)


def mamba_out(y, z, norm_w, w_out):
    b, t = y.shape[:2]
    gy = (y * jax.nn.silu(z)).astype(jnp.float32).reshape(b, t, SSM_GROUPS, SSM_INNER // SSM_GROUPS)
    gy = gy * lax.rsqrt(jnp.mean(gy * gy, axis=-1, keepdims=True) + SSM_NORM_EPS)
    gy = gy.reshape(b, t, SSM_INNER) * norm_w.astype(jnp.float32)
    return gy.astype(y.dtype) @ w_out


def odd_mixer(h_lat, h_ctx, w_in, conv_w, conv_b, dt_bias, a_log, d_skip, norm_w, w_out, ctx_out):
    z_l, xs_l, b_l, c_l, dt_l = mamba_inputs(h_lat, w_in, conv_w, conv_b)
    z_c, xs_c, b_c, c_c, dt_c = mamba_inputs(h_ctx, w_in, conv_w, conv_b)
    d_gep = d_skip.reshape(SSM_GROUPS, SSM_HEADS_PER_GROUP, 1).astype(xs_l.dtype)
    y_l = d_gep * xs_l
    y_c = d_gep * xs_c if ctx_out else None
    h0 = jnp.zeros((h_ctx.shape[0], SSM_GROUPS, SSM_HEADS_PER_GROUP, SSM_HEAD_DIM, SSM_STATE), xs_l.dtype)
    for d in range(2):
        xdt_l, da_l = ssm_direction(xs_l, dt_l[:, :, d], dt_bias[d], a_log[d])
        xdt_c, da_c = ssm_direction(xs_c, dt_c[:, :, d], dt_bias[d], a_log[d])
        seq_l = (xdt_l, da_l, b_l, c_l)
        seq_c = (xdt_c, da_c, b_c, c_c)
        if d == 1:
            seq_l = tuple(jnp.flip(z, axis=1) for z in seq_l)
            seq_c = tuple(jnp.flip(z, axis=1) for z in seq_c)
        if ctx_out:
            yc, h_ctx_final = ssd_scan(seq_c[0], seq_c[1], seq_c[2], seq_c[3], h0)
            y_c = y_c + (jnp.flip(yc, axis=1) if d == 1 else yc)
        else:
            h_ctx_final = ssd_state(seq_c[0], seq_c[1], seq_c[2])
        yl, _ = ssd_scan(seq_l[0], seq_l[1], seq_l[2], seq_l[3], h_ctx_final)
        y_l = y_l + (jnp.flip(yl, axis=1) if d == 1 else yl)
    o_lat = mamba_out(y_l, z_l, norm_w, w_out)
    o_ctx = mamba_out(y_c, z_c, norm_w, w_out) if ctx_out else None
    return o_lat, o_ctx


def conv_ffn(h, w_up, conv_w, conv_b, w_down):
    val, gate = jnp.split(h @ w_up, 2, axis=-1)
    gate = dwconv_centred(gate, conv_w, conv_b)
    return (jax.nn.silu(gate) * val) @ w_down


def setup_inputs(seed: int = 0) -> dict:
    key = jax.random.key(seed)
    ks = iter(jax.random.split(key, 48))

    def nrm(shape, scale):
        return jax.random.normal(next(ks), shape, jnp.float32) * scale

    def unif(shape, lo, hi):
        return jax.random.uniform(next(ks), shape, jnp.float32, lo, hi)

    D = D_MODEL
    dt0 = jnp.exp(unif((N_ODD, 2, SSM_HEADS), math.log(1e-3), math.log(1e-1)))
    return {
        'x': nrm((BATCH, SEQ, D), 1.0),
        'c': nrm((BATCH, D), 1.0),
        'ctx': nrm((BATCH, CTX_LEN, D), 1.0),
        'c_ctx': nrm((D,), 1.0),
        'ada_w': nrm((DEPTH, D, 6 * D), 0.5 * D ** -0.5),
        'ada_b': nrm((DEPTH, 6 * D), 0.02),
        'norm_mix': 1.0 + nrm((DEPTH, D), 0.02),
        'norm_ffn': 1.0 + nrm((DEPTH, D), 0.02),
        'ev_w_in': nrm((N_EVEN, D, EVEN_PROJ), D ** -0.5),
        'ev_w_out': nrm((N_EVEN, MIX_WIDTH, D), MIX_WIDTH ** -0.5),
        'diff_lambda': nrm((N_EVEN, 4, HEAD_DIM), 0.1),
        'diff_subln': 1.0 + nrm((N_EVEN, DIFF_VDIM), 0.02),
        'rwkv_mu': unif((N_EVEN, 2, RWKV_PROJ), 0.0, 0.5),
        'rwkv_w0': unif((N_EVEN, 2, RWKV_WIDTH), -6.0, 0.0),
        'rwkv_w_up': nrm((N_EVEN, 2, DECAY_LORA, RWKV_WIDTH), 0.1),
        'rwkv_a0': nrm((N_EVEN, 2, RWKV_WIDTH), 0.1),
        'rwkv_a_up': nrm((N_EVEN, 2, ICLR_LORA, RWKV_WIDTH), 0.1),
        'rwkv_g_up': nrm((N_EVEN, GATE_LORA, RWKV_WIDTH), GATE_LORA ** -0.5),
        'rwkv_k_k': 0.85 + nrm((N_EVEN, RWKV_WIDTH), 0.05),
        'rwkv_k_a': 1.0 + nrm((N_EVEN, RWKV_WIDTH), 0.05),
        'rwkv_r_k': nrm((N_EVEN, RWKV_HEADS, HEAD_DIM), 0.1),
        'rwkv_ln_w': 1.0 + nrm((N_EVEN, RWKV_WIDTH), 0.02),
        'rwkv_ln_b': nrm((N_EVEN, RWKV_WIDTH), 0.02),
        'ssm_w_in': nrm((N_ODD, D, SSM_PROJ), D ** -0.5),
        'ssm_conv_w': nrm((N_ODD, SSM_CONV, SSM_CONV_CH), SSM_CONV ** -0.5),
        'ssm_conv_b': nrm((N_ODD, SSM_CONV_CH), 0.02),
        'ssm_dt_bias': dt0 + jnp.log(-jnp.expm1(-dt0)),
        'ssm_a_log': jnp.log(unif((N_ODD, 2, SSM_HEADS), 1.0, 16.0)),
        'ssm_d': 1.0 + nrm((N_ODD, SSM_HEADS), 0.1),
        'ssm_norm_w': 1.0 + nrm((N_ODD, SSM_INNER), 0.02),
        'ssm_w_out': nrm((N_ODD, SSM_INNER, D), SSM_INNER ** -0.5),
        'ffn_w_up': nrm((DEPTH, D, 2 * FFN_HIDDEN), D ** -0.5),
        'ffn_conv_w': nrm((DEPTH, FFN_CONV, FFN_HIDDEN), FFN_CONV ** -0.5),
        'ffn_conv_b': nrm((DEPTH, FFN_HIDDEN), 0.02),
        'ffn_w_down': nrm((DEPTH, FFN_HIDDEN, D), FFN_HIDDEN ** -0.5),
        'final_norm': 1.0 + nrm((D,), 0.02),
    }


def reference(x, c, ctx, c_ctx, ada_w, ada_b, norm_mix, norm_ffn, ev_w_in, ev_w_out, diff_lambda, diff_subln,
              rwkv_mu, rwkv_w0, rwkv_w_up, rwkv_a0, rwkv_a_up, rwkv_g_up, rwkv_k_k, rwkv_k_a, rwkv_r_k,
              rwkv_ln_w, rwkv_ln_b, ssm_w_in, ssm_conv_w, ssm_conv_b, ssm_dt_bias, ssm_a_log, ssm_d,
              ssm_norm_w, ssm_w_out, ffn_w_up, ffn_conv_w, ffn_conv_b, ffn_w_down, final_norm):
    rows = x.shape[1] // GRID_W
    ang_row, ang_col = axial_rope_angles(rows)
    x_lat, x_ctx = x, ctx
    for i in range(DEPTH):
        j = i // 2
        ctx_out = i < DEPTH - 1
        mod_lat = (jax.nn.silu(c) @ ada_w[i] + ada_b[i])[:, None, :]
        mod_ctx = (jax.nn.silu(c_ctx) @ ada_w[i] + ada_b[i])[None, None, :]
        sh1_l, sc1_l, g1_l, sh2_l, sc2_l, g2_l = jnp.split(mod_lat, 6, axis=-1)
        sh1_c, sc1_c, g1_c, sh2_c, sc2_c, g2_c = jnp.split(mod_ctx, 6, axis=-1)
        h_lat = modulate(rmsnorm(x_lat, norm_mix[i]), sh1_l, sc1_l)
        h_ctx = modulate(rmsnorm(x_ctx, norm_mix[i]), sh1_c, sc1_c)
        if i % 2 == 0:
            rw = {'mu': rwkv_mu[j], 'w0': rwkv_w0[j], 'w_up': rwkv_w_up[j], 'a0': rwkv_a0[j],
                  'a_up': rwkv_a_up[j], 'g_up': rwkv_g_up[j], 'k_k': rwkv_k_k[j], 'k_a': rwkv_k_a[j],
                  'r_k': rwkv_r_k[j], 'ln_w': rwkv_ln_w[j], 'ln_b': rwkv_ln_b[j]}
            o_lat, o_ctx = even_mixer(h_lat, h_ctx, ang_row, ang_col, i, ev_w_in[j], ev_w_out[j],
                                      diff_lambda[j], diff_subln[j], rw, ctx_out)
        else:
            o_lat, o_ctx = odd_mixer(h_lat, h_ctx, ssm_w_in[j], ssm_conv_w[j], ssm_conv_b[j], ssm_dt_bias[j],
                                     ssm_a_log[j], ssm_d[j], ssm_norm_w[j], ssm_w_out[j], ctx_out)
        x_lat = x_lat + g1_l * o_lat
        h_lat = modulate(rmsnorm(x_lat, norm_ffn[i]), sh2_l, sc2_l)
        x_lat = x_lat + g2_l * conv_ffn(h_lat, ffn_w_up[i], ffn_conv_w[i], ffn_conv_b[i], ffn_w_down[i])
        if ctx_out:
            x_ctx = x_ctx + g1_c * o_ctx
            h_ctx = modulate(rmsnorm(x_ctx, norm_ffn[i]), sh2_c, sc2_c)
            x_ctx = x_ctx + g2_c * conv_ffn(h_ctx, ffn_w_up[i], ffn_conv_w[i], ffn_conv_b[i], ffn_w_down[i])
    return rmsnorm(x_lat, final_norm)
```

```python
import numpy as np
import concourse.bass as bass
import concourse.mybir as mybir
from concourse.bass_utils import run_bass_kernel_spmd

F32 = mybir.dt.float32
BF16 = mybir.dt.bfloat16
AF = mybir.ActivationFunctionType
ALU = mybir.AluOpType
AX = mybir.AxisListType

EPOCH = 30000
NDMASEM = 6


class Sched:
    ENGS = ("pe", "act", "dve", "pool", "sp")

    def __init__(self, nc, same_engine_sync=True):
        self.nc = nc
        self.ops = {e: [] for e in self.ENGS}
        self.cnt = {e: 0 for e in self.ENGS}
        self.sems = {}
        self.dma_n = {e: 0 for e in self.ENGS}
        self.dma_last = {}
        self.waited = {e: {} for e in self.ENGS}
        self.last_w = {}
        self.readers = {}
        self.same = same_engine_sync
        self.final_tokens = []
        self.last_tok = {}

    def sem(self, name):
        if name not in self.sems:
            self.sems[name] = self.nc.alloc_semaphore(name) if hasattr(self.nc, "alloc_semaphore") else None
        return self.sems[name]

    def _need(self, eng, tok, waits):
        if tok is None:
            return
        name, val, src = tok
        if src == eng and (eng == "pe" or not self.same) and not name.startswith("dma"):
            return
        if self.waited[eng].get(name, 0) >= val:
            return
        waits[name] = max(waits.get(name, 0), val)

    def barrier(self):
        toks = list(self.last_tok.values()) + list(self.dma_last.values())
        for e in self.ENGS:
            self.op(e, lambda en: en.nop(), extra=toks)

    def op(self, eng, fn, reads=(), writes=(), dma=False, extra=()):
        waits = {}
        for tok in extra:
            self._need(eng, tok, waits)
        for b in reads:
            self._need(eng, self.last_w.get(b), waits)
        for b in writes:
            self._need(eng, self.last_w.get(b), waits)
            for tok in self.readers.get(b, {}).values():
                self._need(eng, tok, waits)
        if dma:
            i = self.dma_n[eng]
            self.dma_n[eng] += 1
            name = f"dma_{eng}_{i % NDMASEM}"
            val = 16 * (i // NDMASEM + 1)
            prev = self.dma_last.get(name)
            if prev is not None:
                self._need(eng, prev, waits)
            tok = (name, val, eng)
            self.dma_last[name] = tok
            inc = 16
        else:
            self.cnt[eng] += 1
            ep, v = divmod(self.cnt[eng] - 1, EPOCH)
            name = f"c_{eng}_{ep}"
            val = v + 1
            tok = (name, val, eng)
            self.last_tok[eng] = tok
            inc = 1
        for n, v in waits.items():
            self.waited[eng][n] = v
        self.ops[eng].append((list(waits.items()), fn, name, inc))
        for b in reads:
            self.readers.setdefault(b, {})[eng + ("d%d" % self.dma_n[eng] if dma else "")] = tok
        for b in writes:
            self.last_w[b] = tok
            self.readers[b] = {}
        return tok

    def emit(self, final_wait_tokens=()):
        nc = self.nc
        names = set()
        for e in self.ENGS:
            for waits, fn, name, inc in self.ops[e]:
                names.add(name)
                for n, _ in waits:
                    names.add(n)
        import contextlib
        with contextlib.ExitStack() as st:
            semh = {n: st.enter_context(nc.semaphore(n)) for n in sorted(names)}
            block = st.enter_context(nc.Block())

            def body(eng_name):
                def f(engine):
                    for waits, fn, name, inc in self.ops[eng_name]:
                        for n, v in waits:
                            engine.wait_ge(semh[n], v)
                        fn(engine).then_inc(semh[name], inc)
                    if eng_name == "sp":
                        for (n, v, _) in final_wait_tokens:
                            engine.wait_ge(semh[n], v)
                return f

            block.tensor(body("pe"))
            block.scalar(body("act"))
            block.vector(body("dve"))
            block.gpsimd(body("pool"))
            block.sync(body("sp"))


def new_nc():
    return bass.Bass("TRN2", target_bir_lowering=False)

import math

D = 1024
FH = 2816
NCH = 22
TL = 4096
CTXL = 256
WOUT = 510


def cast_weights(nc, S, dram, dst, nk, ncols, stages, piece):
    i = 0
    for k in range(nk):
        for c0 in range(0, ncols, piece):
            c1 = min(ncols, c0 + piece)
            st, key = stages[i % len(stages)]
            S.op("sp", lambda e, st=st, k=k, c0=c0, c1=c1: e.dma_start(out=st[:, 0:c1 - c0], in_=dram[:, k, c0:c1]),
                 writes=[key], dma=True)
            eng = "pool" if i % 2 == 0 else "act"
            if eng == "pool":
                S.op("pool", lambda e, st=st, k=k, c0=c0, c1=c1: e.tensor_copy(out=dst[:, k, c0:c1], in_=st[:, 0:c1 - c0]),
                     reads=[key], writes=[("w", dst.name)])
            else:
                S.op("act", lambda e, st=st, k=k, c0=c0, c1=c1: e.copy(out=dst[:, k, c0:c1], in_=st[:, 0:c1 - c0]),
                     reads=[key], writes=[("w", dst.name)])
            i += 1


def mod_alloc(nc, ns):
    return (nc.alloc_sbuf_tensor("c_sb", [128, 8, 2], F32), nc.alloc_sbuf_tensor("sc_sb", [128, 8, 2], F32),
            nc.alloc_sbuf_tensor("ab_sb", [128, 48], F32), nc.alloc_sbuf_tensor("mod_sb", [128, ns * 8, 2], F32))


def mod_vectors(nc, S, ada_w, ada_b, c2, sections, modps, stage, stage_key, tiles):
    ns = len(sections)
    c_sb, sc_sb, ab_sb, mod = tiles
    S.op("sp", lambda e: e.dma_start(out=c_sb[:], in_=c2[:, :, :]), writes=["c_sb"], dma=True)
    S.op("sp", lambda e: e.dma_start(out=ab_sb[:], in_=ada_b[:, :]), writes=["ab_sb"], dma=True)
    S.op("act", lambda e: e.activation(out=sc_sb[:], in_=c_sb[:], func=AF.Silu), reads=["c_sb"], writes=["sc_sb"])
    for si, s in enumerate(sections):
        for half in range(2):
            n0 = s * 1024 + half * 512
            S.op("sp", lambda e, n0=n0: e.dma_start(out=stage[:, 0:4096].rearrange("p (k n) -> p k n", k=8),
                                                    in_=ada_w[:, :, n0:n0 + 512]),
                 writes=[stage_key], dma=True)
            for nn in range(4):
                col = (si * 8 + half * 4 + nn) * 2
                for k in range(8):
                    S.op("pe", lambda e, nn=nn, k=k, col=col: e.matmul(
                        modps[:, col:col + 2], stage[:, k * 512 + nn * 128:k * 512 + (nn + 1) * 128], sc_sb[:, k, :],
                        start=(k == 0), stop=(k == 7)),
                        reads=[stage_key, "sc_sb"], writes=["modps"])
    for si, s in enumerate(sections):
        S.op("dve", lambda e, si=si, s=s: e.tensor_tensor(
            out=mod[:, si * 8:(si + 1) * 8, :],
            in0=modps[:, si * 16:(si + 1) * 16].rearrange("p (j v) -> p j v", v=2),
            in1=ab_sb[:, s * 8:(s + 1) * 8].unsqueeze(2).to_broadcast([128, 8, 2]), op=ALU.add),
            reads=["modps", "ab_sb"], writes=["mod_sb"])
    return mod


def build_ffn(KM, ctx_out, final, debug=None):
    nc = new_nc()
    S = Sched(nc)
    TP = TL + 2
    CP = CTXL + 2

    def din(name, shape):
        return nc.dram_tensor(name, shape, F32, kind="ExternalInput").ap()

    xT = din("xT", [128, 8, TP])
    mixT = din("mixT", [128, KM, TP])
    w_out = din("w_out", [128, KM, D])
    w_up = din("w_up", [128, 8, 2 * FH])
    w_dn = din("w_dn", [128, NCH, D])
    ada_w = din("ada_w", [128, 8, 6 * D])
    ada_b = din("ada_b", [128, 48])
    c2 = din("c2", [128, 8, 2])
    nrm = din("nrm", [128, 8])
    cw = din("cw", [128, NCH, 3])
    cb = din("cb", [128, NCH])
    emask = din("emask", [128, 2])
    if ctx_out:
        xcT = din("xcT", [128, 8, CP])
        mixcT = din("mixcT", [128, KM, CP])
        xcoT = nc.dram_tensor("xcoT", [128, 8, CTXL], F32, kind="ExternalOutput").ap()
    if final:
        fnw = din("fnw", [128, 8])
    xoT = nc.dram_tensor("xoT", [128, 8, TL], F32, kind="ExternalOutput").ap()
    xmid = nc.dram_tensor("xmid", [128, 8, TP], F32, kind="Internal").ap()
    xmidc = nc.dram_tensor("xmidc", [128, 8, CP], F32, kind="Internal").ap()

    ps_acc = [nc.alloc_psum_tensor(f"ps_acc{i}", [128, 512], F32) for i in range(2)]
    ps_ss = nc.alloc_psum_tensor("ps_ss", [128, 512], F32)
    ps_val = [nc.alloc_psum_tensor(f"ps_val{i}", [128, 512], F32) for i in range(2)]
    ps_gate = [nc.alloc_psum_tensor(f"ps_gate{i}", [128, 512], F32) for i in range(2)]
    modps = nc.alloc_psum_tensor("modps", [128, 512], F32)

    ones_bf = nc.alloc_sbuf_tensor("ones_bf", [128, 128], BF16)
    S.op("pool", lambda e: e.memset(ones_bf[:], 1.0), writes=["ones_bf"])
    small = {}
    for name, src, shape in (("nrm", nrm, [128, 8]), ("cw", cw, [128, NCH, 3]), ("cb", cb, [128, NCH]),
                             ("emask", emask, [128, 2])) + ((("fnw", fnw, [128, 8]),) if final else ()):
        t = nc.alloc_sbuf_tensor(name + "_sb", shape, F32)
        S.op("sp", lambda e, t=t, src=src: e.dma_start(out=t[:], in_=src), writes=[name], dma=True)
        small[name] = t

    import contextlib
    mtiles = mod_alloc(nc, 4)
    A2 = nc.alloc_sbuf_tensor("A2", [128, 8, 2], F32)
    x_t = nc.alloc_sbuf_tensor("x_t", [128, 8, 512], F32)
    eps_sb = nc.alloc_sbuf_tensor("eps_sb", [128, 1], F32)
    est = contextlib.ExitStack()
    stA = est.enter_context(nc.sbuf_tensor("stA", [128, 4096], F32))
    stB = est.enter_context(nc.sbuf_tensor("stB", [128, 2816], F32))
    mod = mod_vectors(nc, S, ada_w, ada_b, c2, [2, 3, 4, 5], modps, stA, "stA", mtiles)
    S.op("dve", lambda e: e.scalar_tensor_tensor(
        out=A2[:], in0=mod[:, 16:24, :], scalar=1.0, in1=small["nrm"][:].unsqueeze(2).to_broadcast([128, 8, 2]),
        op0=ALU.add, op1=ALU.mult), reads=["mod_sb", "nrm"], writes=["A2"])

    if debug == "mod":
        dbg = nc.dram_tensor("dbg", [128, 32, 2], F32, kind="ExternalOutput").ap()
        tok = S.op("sp", lambda e: e.dma_start(out=dbg[:, :, :], in_=mod[:]), reads=["mod_sb"], dma=True)
        S.emit([tok])
        return nc

    def g1(j, v):
        return mod[:, j, v:v + 1]

    def sh2(j, v):
        return mod[:, 8 + j, v:v + 1]

    def g2(j, v):
        return mod[:, 24 + j, v:v + 1]

    est1 = est
    wout_bf = est1.enter_context(nc.sbuf_tensor("wout_bf", [128, KM, D], BF16))
    cast_weights(nc, S, w_out, wout_bf, KM, D, [(stA, "stA"), (stB, "stB")], D)
    mix_f = est1.enter_context(nc.sbuf_tensor("mix_f", [128, KM, 512], F32))
    mix_b = est1.enter_context(nc.sbuf_tensor("mix_b", [128, KM, 512], BF16))
    acc_i = [0]

    def stage1(x_src, mix_src, dst, ncols_total, v):
        for c0 in range(0, ncols_total, 512):
            n = min(512, ncols_total - c0)
            S.op("sp", lambda e, c0=c0, n=n: e.dma_start(out=mix_f[:, :, 0:n], in_=mix_src[:, :, c0:c0 + n]),
                 writes=["mix_f"], dma=True)
            S.op("sp", lambda e, c0=c0, n=n: e.dma_start(out=x_t[:, :, 0:n], in_=x_src[:, :, c0:c0 + n]),
                 writes=["x_t"], dma=True)
            h = KM // 2
            S.op("pool", lambda e, n=n, h=h: e.tensor_copy(out=mix_b[:, 0:h, 0:n], in_=mix_f[:, 0:h, 0:n]),
                 reads=["mix_f"], writes=["mix_b0"])
            S.op("act", lambda e, n=n, h=h: e.copy(out=mix_b[:, h:KM, 0:n], in_=mix_f[:, h:KM, 0:n]),
                 reads=["mix_f"], writes=["mix_b1"])
            for j in range(8):
                ai = acc_i[0] % 2
                acc_i[0] += 1
                ps = ps_acc[ai]
                for k in range(KM):
                    S.op("pe", lambda e, ps=ps, k=k, j=j, n=n: e.matmul(
                        ps[:, 0:n], wout_bf[:, k, j * 128:(j + 1) * 128], mix_b[:, k, 0:n],
                        start=(k == 0), stop=(k == KM - 1)),
                        reads=[("w", "wout_bf"), "mix_b0", "mix_b1"], writes=[f"ps_acc{ai}"])
                S.op("dve", lambda e, ps=ps, j=j, n=n, v=v: e.scalar_tensor_tensor(
                    out=x_t[:, j, 0:n], in0=ps[:, 0:n], scalar=g1(j, v), in1=x_t[:, j, 0:n],
                    op0=ALU.mult, op1=ALU.add),
                    reads=[f"ps_acc{ai}", "mod_sb", "x_t"], writes=["x_t"])
            S.op("sp", lambda e, c0=c0, n=n: e.dma_start(out=dst[:, :, c0:c0 + n], in_=x_t[:, :, 0:n]),
                 reads=["x_t"], writes=["xmid_dram"], dma=True)

    stage1(xT, mixT, xmid, TP, 0)
    if ctx_out:
        stage1(xcT, mixcT, xmidc, CP, 1)

    if debug == "stage1":
        S.barrier()
        tok = S.op("sp", lambda e: e.dma_start(out=xoT[:, :, 0:512], in_=x_t[:]), reads=["x_t"], dma=True)
        S.emit([tok])
        return nc
    S.barrier()
    est.close()
    wup_bf = nc.alloc_sbuf_tensor("wup_bf", [128, 8, 2 * FH], BF16)
    wdn_bf = nc.alloc_sbuf_tensor("wdn_bf", [128, NCH, D], BF16)
    est = contextlib.ExitStack()
    stA = est.enter_context(nc.sbuf_tensor("stA2", [128, 2816], F32))
    stB = est.enter_context(nc.sbuf_tensor("stB2", [128, 2816], F32))
    cast_weights(nc, S, w_up, wup_bf, 8, 2 * FH, [(stA, "stA"), (stB, "stB")], FH)
    cast_weights(nc, S, w_dn, wdn_bf, NCH, D, [(stA, "stA"), (stB, "stB")], D)
    S.barrier()
    est.close()
    h_bf = nc.alloc_sbuf_tensor("h_bf", [128, 8, 512], BF16)
    sq_bf = [nc.alloc_sbuf_tensor(f"sq_bf{i}", [128, 512], BF16) for i in range(2)]
    rs = nc.alloc_sbuf_tensor("rs", [128, 512], F32)
    t1 = [nc.alloc_sbuf_tensor(f"t1_{i}", [128, 512], F32) for i in range(2)]
    u_bf = nc.alloc_sbuf_tensor("u_bf", [128, NCH, 512], BF16)
    print("sbuf remaining after ffn alloc", nc.sbuf_bytes_remaining)

    def rms_stats(src_tile, ncol, off):
        for k in range(8):
            b = k % 2
            S.op("act", lambda e, k=k, b=b: e.activation(out=sq_bf[b][:, 0:ncol], in_=src_tile[:, k, off:off + ncol],
                                                         func=AF.Square),
                 reads=["x_t"], writes=[f"sq{b}"])
            S.op("pe", lambda e, k=k, b=b: e.matmul(ps_ss[:, 0:ncol], ones_bf[:], sq_bf[b][:, 0:ncol],
                                                   start=(k == 0), stop=(k == 7)),
                 reads=[f"sq{b}", "ones_bf"], writes=["ps_ss"])
        S.op("act", lambda e: e.activation(out=rs[:, 0:ncol], in_=ps_ss[:, 0:ncol], func=AF.Sqrt,
                                           bias=eps_sb[:, 0:1], scale=1.0 / D),
             reads=["ps_ss", "eps"], writes=["rs"])
        S.op("dve", lambda e: e.reciprocal(out=rs[:, 0:ncol], in_=rs[:, 0:ncol]), reads=["rs"], writes=["rs"])

    S.op("pool", lambda e: e.memset(eps_sb[:], 1e-6), writes=["eps"])
    vg_i = [0]

    def stage2(src, dst, n_total, v, is_ctx):
        ntile = (n_total + WOUT - 1) // WOUT
        for ti in range(ntile):
            o0 = ti * WOUT
            no = min(WOUT, n_total - o0)
            n = no + 2
            S.op("sp", lambda e, o0=o0, n=n: e.dma_start(out=x_t[:, :, 0:n], in_=src[:, :, o0:o0 + n]),
                 reads=["xmid_dram"], writes=["x_t"], dma=True)
            rms_stats(x_t, n, 0)
            for k in range(8):
                b = k % 2
                S.op("dve", lambda e, k=k, b=b, n=n: e.tensor_tensor(out=t1[b][:, 0:n], in0=x_t[:, k, 0:n],
                                                                    in1=rs[:, 0:n], op=ALU.mult),
                     reads=["x_t", "rs"], writes=[f"t1_{b}"])
                S.op("act", lambda e, k=k, b=b, n=n, v=v: e.activation(
                    out=h_bf[:, k, 0:n], in_=t1[b][:, 0:n], func=AF.Identity,
                    bias=sh2(k, v), scale=A2[:, k, v:v + 1]),
                    reads=[f"t1_{b}", "A2", "mod_sb"], writes=["h_bf"])
            if is_ctx:
                S.op("dve", lambda e: e.tensor_scalar(out=h_bf[:, :, 0:1], in0=h_bf[:, :, 0:1], scalar1=0.0,
                                                      scalar2=None, op0=ALU.mult), reads=["h_bf"], writes=["h_bf"])
                S.op("dve", lambda e, n=n: e.tensor_scalar(out=h_bf[:, :, n - 1:n], in0=h_bf[:, :, n - 1:n], scalar1=0.0,
                                                           scalar2=None, op0=ALU.mult), reads=["h_bf"], writes=["h_bf"])
            else:
                if ti == 0:
                    S.op("dve", lambda e: e.tensor_scalar(out=h_bf[:, :, 0:1], in0=h_bf[:, :, 0:1],
                                                          scalar1=small["emask"][:, 0:1], scalar2=None, op0=ALU.mult),
                         reads=["emask", "h_bf"], writes=["h_bf"])
                if ti == ntile - 1:
                    S.op("dve", lambda e, n=n: e.tensor_scalar(out=h_bf[:, :, n - 1:n], in0=h_bf[:, :, n - 1:n],
                                                               scalar1=small["emask"][:, 1:2], scalar2=None, op0=ALU.mult),
                         reads=["emask", "h_bf"], writes=["h_bf"])
            if debug == "s2a":
                return
            for c in range(NCH if debug != "s2b" else 1):
                bi = vg_i[0] % 2
                vg_i[0] += 1
                pv, pg = ps_val[bi], ps_gate[bi]
                for k in range(8):
                    S.op("pe", lambda e, pv=pv, k=k, c=c, n=n: e.matmul(
                        pv[:, 0:n], wup_bf[:, k, c * 128:(c + 1) * 128], h_bf[:, k, 0:n], start=(k == 0), stop=(k == 7)),
                        reads=[("w", "wup_bf"), "h_bf"], writes=[f"ps_val{bi}"])
                for k in range(8):
                    S.op("pe", lambda e, pg=pg, k=k, c=c, n=n: e.matmul(
                        pg[:, 0:n], wup_bf[:, k, FH + c * 128:FH + (c + 1) * 128], h_bf[:, k, 0:n], start=(k == 0), stop=(k == 7)),
                        reads=[("w", "wup_bf"), "h_bf"], writes=[f"ps_gate{bi}"])
                ta, tb = t1[0], t1[1]
                S.op("act", lambda e, pg=pg, c=c, no=no: e.activation(out=ta[:, 0:no], in_=pg[:, 0:no], func=AF.Identity,
                                                                     scale=small["cw"][:, c, 0:1]),
                     reads=[f"ps_gate{bi}", "cw"], writes=["t1_0"])
                S.op("dve", lambda e, pg=pg, c=c, no=no: e.scalar_tensor_tensor(
                    out=ta[:, 0:no], in0=pg[:, 1:no + 1], scalar=small["cw"][:, c, 1:2], in1=ta[:, 0:no],
                    op0=ALU.mult, op1=ALU.add), reads=[f"ps_gate{bi}", "cw", "t1_0"], writes=["t1_0"])
                S.op("dve", lambda e, pg=pg, c=c, no=no: e.scalar_tensor_tensor(
                    out=ta[:, 0:no], in0=pg[:, 2:no + 2], scalar=small["cw"][:, c, 2:3], in1=ta[:, 0:no],
                    op0=ALU.mult, op1=ALU.add), reads=[f"ps_gate{bi}", "cw", "t1_0"], writes=["t1_0"])
                S.op("act", lambda e, c=c, no=no: e.activation(out=tb[:, 0:no], in_=ta[:, 0:no], func=AF.Silu,
                                                              bias=small["cb"][:, c:c + 1]),
                     reads=["t1_0", "cb"], writes=["t1_1"])
                S.op("dve", lambda e, pv=pv, c=c, no=no: e.tensor_tensor(out=u_bf[:, c, 0:no], in0=tb[:, 0:no],
                                                                        in1=pv[:, 1:no + 1], op=ALU.mult),
                     reads=["t1_1", f"ps_val{bi}"], writes=["u_bf"])
            if debug == "s2b":
                return
            for j in range(8):
                ai = acc_i[0] % 2
                acc_i[0] += 1
                ps = ps_acc[ai]
                for c in range(NCH):
                    S.op("pe", lambda e, ps=ps, c=c, j=j, no=no: e.matmul(
                        ps[:, 0:no], wdn_bf[:, c, j * 128:(j + 1) * 128], u_bf[:, c, 0:no],
                        start=(c == 0), stop=(c == NCH - 1)),
                        reads=[("w", "wdn_bf"), "u_bf"], writes=[f"ps_acc{ai}"])
                S.op("dve", lambda e, ps=ps, j=j, no=no, v=v: e.scalar_tensor_tensor(
                    out=x_t[:, j, 1:no + 1], in0=ps[:, 0:no], scalar=g2(j, v), in1=x_t[:, j, 1:no + 1],
                    op0=ALU.mult, op1=ALU.add),
                    reads=[f"ps_acc{ai}", "mod_sb", "x_t"], writes=["x_t"])
            if debug == "s2c":
                return
            if final and not is_ctx:
                rms_stats(x_t, no, 1)
                for j in range(8):
                    S.op("dve", lambda e, j=j, no=no: e.scalar_tensor_tensor(
                        out=x_t[:, j, 1:no + 1], in0=x_t[:, j, 1:no + 1], scalar=small["fnw"][:, j:j + 1],
                        in1=rs[:, 0:no], op0=ALU.mult, op1=ALU.mult),
                        reads=["x_t", "fnw", "rs"], writes=["x_t"])
            tok = S.op("sp", lambda e, o0=o0, no=no: e.dma_start(out=dst[:, :, o0:o0 + no], in_=x_t[:, :, 1:no + 1]),
                       reads=["x_t"], writes=["out_dram"], dma=True)
            S.final_tokens.append(tok)

    if debug in ("s2a", "s2b", "s2c", "s2d"):
        stage2(xmid, xoT, TL, 0, False)
        S.barrier()
        tok = S.op("sp", lambda e: e.dma_start(out=xoT[:, :, 0:512], in_=x_t[:]), reads=["x_t"], dma=True)
        S.emit([tok])
        return nc
    with nc.allow_low_precision("bf16 matmul operands, fp32 accumulation"):
        stage2(xmid, xoT, TL, 0, False)
        if ctx_out:
            stage2(xmidc, xcoT, CTXL, 1, True)
        S.emit(S.final_tokens)
    return nc


def fm(a):
    T, F = a.shape
    return np.ascontiguousarray(a.reshape(T, F // 128, 128).transpose(2, 1, 0))


def fm_inv(a):
    p, nch, T = a.shape
    return np.ascontiguousarray(a.transpose(2, 1, 0).reshape(T, nch * 128))


def kch(w):
    K, N = w.shape
    return np.ascontiguousarray(w.reshape(K // 128, 128, N).transpose(1, 0, 2))


def vec_fm(v):
    return np.ascontiguousarray(v.reshape(-1, 128).T)


def halo_slice(a, start, end):
    T, F = a.shape
    out = np.zeros((end - start + 2, F), a.dtype)
    lo, hi = max(start - 1, 0), min(end + 1, T)
    out[lo - (start - 1):hi - (start - 1)] = a[lo:hi]
    return out


def ffn_inputs(core, x, mix, xc, mixc, c, c_ctx, ada_w, ada_b, nrm, w_out, w_up, cw, cb, w_dn, ctx_out, fnw=None):
    b, q = divmod(core, 4)
    s, e = q * TL, (q + 1) * TL
    m = {
        "xT": fm(halo_slice(x[b], s, e)),
        "mixT": fm(halo_slice(mix[b], s, e)),
        "w_out": kch(w_out), "w_up": kch(w_up), "w_dn": kch(w_dn),
        "ada_w": kch(ada_w), "ada_b": vec_fm(ada_b),
        "c2": np.ascontiguousarray(np.stack([vec_fm(c[b]), vec_fm(c_ctx)], axis=-1)),
        "nrm": vec_fm(nrm),
        "cw": np.ascontiguousarray(cw.T.reshape(NCH, 128, 3).transpose(1, 0, 2)),
        "cb": vec_fm(cb),
        "emask": np.ascontiguousarray(np.broadcast_to(
            np.array([0.0 if q == 0 else 1.0, 0.0 if q == 3 else 1.0], np.float32), (128, 2))),
    }
    if ctx_out:
        m["xcT"] = fm(halo_slice(xc[b], 0, CTXL))
        m["mixcT"] = fm(halo_slice(mixc[b], 0, CTXL))
    if fnw is not None:
        m["fnw"] = vec_fm(fnw)
    return m


SEQ = 16384
NKT = (SEQ + CTXL) // 128
NPROJ = 10
LAM_INIT = 0.8 - 0.6 * math.exp(-0.3 * 0)


def build_even(debug=None, nq_tiles=32):
    nc = new_nc()
    S = Sched(nc)

    def din(name, shape):
        return nc.dram_tensor(name, shape, F32, kind="ExternalInput").ap()

    xT = din("xT", [128, 8, SEQ])
    cT = din("cT", [128, 8, CTXL])
    w_in = din("w_in", [128, 8, NPROJ * 128])
    ada_w = din("ada_w", [128, 8, 6 * D])
    ada_b = din("ada_b", [128, 48])
    c2 = din("c2", [128, 8, 2])
    nrm = din("nrm", [128, 8])
    lamv = din("lamv", [128, 4, 64])
    subw = din("subw", [128, 1])
    rfreq = din("rfreq", [128, 3])
    rpos = din("rpos", [128, 256])
    attT = nc.dram_tensor("attT", [128, SEQ], F32, kind="ExternalOutput").ap()
    attcT = nc.dram_tensor("attcT", [128, CTXL], F32, kind="ExternalOutput").ap()

    bank = [nc.alloc_psum_tensor(f"bank{i}", [128, 512], F32) for i in range(8)]

    ones_bf = nc.alloc_sbuf_tensor("ones_bf", [128, 128], BF16)
    S.op("pool", lambda e: e.memset(ones_bf[:], 1.0), writes=["ones_bf"])
    eps6 = nc.alloc_sbuf_tensor("eps6", [128, 1], F32)
    S.op("pool", lambda e: e.memset(eps6[:], 1e-6), writes=["eps6"])
    eps5 = nc.alloc_sbuf_tensor("eps5", [128, 1], F32)
    S.op("pool", lambda e: e.memset(eps5[:], 1e-5), writes=["eps5"])
    small = {}
    for name, src, shape in (("nrm", nrm, [128, 8]), ("lamv", lamv, [128, 4, 64]), ("subw", subw, [128, 1]),
                             ("rfreq", rfreq, [128, 3]), ("rpos", rpos, [128, 256])):
        t = nc.alloc_sbuf_tensor(name + "_sb", shape, F32)
        S.op("sp", lambda e, t=t, src=src: e.dma_start(out=t[:], in_=src), writes=[name], dma=True)
        small[name] = t
    mtiles = mod_alloc(nc, 2)
    A1 = nc.alloc_sbuf_tensor("A1", [128, 8, 2], F32)
    lam_t = nc.alloc_sbuf_tensor("lam_t", [128, 8], F32)
    subs = nc.alloc_sbuf_tensor("subs", [128, 1], F32)
    Crow = nc.alloc_sbuf_tensor("Crow", [128, 256], F32)
    Srow = nc.alloc_sbuf_tensor("Srow", [128, 256], F32)
    Ccol = nc.alloc_sbuf_tensor("Ccol", [128, 64], F32)
    Scol = nc.alloc_sbuf_tensor("Scol", [128, 64], F32)
    QT = nc.alloc_sbuf_tensor("QT", [128, SEQ], BF16)
    QcT = nc.alloc_sbuf_tensor("QcT", [128, CTXL], BF16)
    KT = nc.alloc_sbuf_tensor("KT", [128, SEQ + CTXL], BF16)
    Vt = nc.alloc_sbuf_tensor("Vt", [128, NKT, 128], BF16)
    win_bf = nc.alloc_sbuf_tensor("win_bf", [128, 8, NPROJ * 128], BF16)
    x_t = nc.alloc_sbuf_tensor("x_t", [128, 8, 512], F32)
    h_bf = nc.alloc_sbuf_tensor("h_bf", [128, 8, 512], BF16)
    sq_bf = [nc.alloc_sbuf_tensor(f"sq_bf{i}", [128, 512], BF16) for i in range(2)]
    rs = nc.alloc_sbuf_tensor("rs", [128, 512], F32)
    t1 = [nc.alloc_sbuf_tensor(f"t1_{i}", [128, 512], F32) for i in range(3)]
    cs_t = [nc.alloc_sbuf_tensor(f"cs_{i}", [128, 512], F32) for i in range(2)]
    PT = [nc.alloc_sbuf_tensor(f"PT{i}", [128, 512], BF16) for i in range(4)]
    stA = nc.alloc_sbuf_tensor("stA", [128, 4096], F32)
    Qpad = [[nc.alloc_sbuf_tensor(f"Qpad{a}{m}", [128, 512], BF16) for m in range(2)] for a in range(2)]
    accL = [nc.alloc_sbuf_tensor(f"accL{m}", [128, 512], F32) for m in range(2)]
    ones_f = nc.alloc_sbuf_tensor("ones_f", [128, 128], F32)
    S.op("pool", lambda e: e.memset(ones_f[:], 1.0), writes=["ones_f"])
    for a in range(2):
        for m in range(2):
            S.op("pool", lambda e, a=a, m=m: e.memset(Qpad[a][m][:], 0.0), writes=[f"Qpad{a}{m}"])
    print("sbuf remaining (even)", nc.sbuf_bytes_remaining)

    mod = mod_vectors(nc, S, ada_w, ada_b, c2, [0, 1], bank[7], stA, "stA", mtiles)
    S.op("dve", lambda e: e.scalar_tensor_tensor(
        out=A1[:], in0=mod[:, 8:16, :], scalar=1.0, in1=small["nrm"][:].unsqueeze(2).to_broadcast([128, 8, 2]),
        op0=ALU.add, op1=ALU.mult), reads=["mod_sb", "nrm"], writes=["A1"])
    cast_weights(nc, S, w_in, win_bf, 8, NPROJ * 128, [(stA, "stA")], NPROJ * 128)

    lv = small["lamv"]
    tmpl = t1[0]
    lv4 = lv[:].rearrange("p (a t) d -> p a t d", t=2)
    S.op("dve", lambda e: e.tensor_tensor(out=tmpl[:, 0:128].rearrange("p (a d) -> p a d", a=2),
                                          in0=lv4[:, :, 0, :], in1=lv4[:, :, 1, :], op=ALU.mult),
         reads=["lamv"], writes=["t1_0"])
    S.op("dve", lambda e: e.reduce_sum(out=lam_t[:, 0:2], in_=tmpl[:, 0:128].rearrange("p (a d) -> p a d", a=2),
                                       axis=AX.X), reads=["t1_0"], writes=["lam_t"])
    S.op("act", lambda e: e.activation(out=lam_t[:, 2:4], in_=lam_t[:, 0:2], func=AF.Exp), reads=["lam_t"], writes=["lam_t"])
    S.op("dve", lambda e: e.tensor_tensor(out=lam_t[:, 4:5], in0=lam_t[:, 3:4], in1=lam_t[:, 2:3], op=ALU.subtract),
         reads=["lam_t"], writes=["lam_t"])
    S.op("dve", lambda e: e.tensor_scalar(out=lam_t[:, 4:5], in0=lam_t[:, 4:5], scalar1=-LAM_INIT, scalar2=None, op0=ALU.add),
         reads=["lam_t"], writes=["lam_t"])
    S.op("dve", lambda e: e.tensor_scalar(out=subs[:], in0=small["subw"][:], scalar1=(1.0 - LAM_INIT), scalar2=None, op0=ALU.mult),
         reads=["subw"], writes=["subs"])

    TWO_PI = 2.0 * math.pi
    ti32 = nc.alloc_sbuf_tensor("ti32", [128, 256], mybir.dt.int32)

    def trig(dst, npos, fcol, phase, signed):
        ang = t1[1]
        k_f = t1[2]
        a = ang[:, 0:npos]
        S.op("dve", lambda e: e.tensor_scalar(out=a, in0=small["rpos"][:, 0:npos], scalar1=small["rfreq"][:, fcol:fcol + 1],
                                              scalar2=phase, op0=ALU.mult, op1=ALU.add),
             reads=["rpos", "rfreq"], writes=["t1_1"])
        S.op("dve", lambda e: e.tensor_scalar(out=k_f[:, 0:npos], in0=a, scalar1=1.0 / TWO_PI, scalar2=None, op0=ALU.mult),
             reads=["t1_1"], writes=["t1_2"])
        S.op("dve", lambda e: e.tensor_copy(out=ti32[:, 0:npos], in_=k_f[:, 0:npos]), reads=["t1_2"], writes=["ti32"])
        S.op("dve", lambda e: e.tensor_copy(out=k_f[:, 0:npos], in_=ti32[:, 0:npos]), reads=["ti32"], writes=["t1_2"])
        S.op("dve", lambda e: e.scalar_tensor_tensor(out=a, in0=k_f[:, 0:npos], scalar=-TWO_PI, in1=a, op0=ALU.mult, op1=ALU.add),
             reads=["t1_1", "t1_2"], writes=["t1_1"])
        S.op("dve", lambda e: e.tensor_scalar(out=k_f[:, 0:npos], in0=a, scalar1=math.pi, scalar2=-TWO_PI, op0=ALU.is_gt, op1=ALU.mult),
             reads=["t1_1"], writes=["t1_2"])
        S.op("dve", lambda e: e.tensor_tensor(out=a, in0=a, in1=k_f[:, 0:npos], op=ALU.add), reads=["t1_1", "t1_2"], writes=["t1_1"])
        S.op("dve", lambda e: e.tensor_scalar(out=k_f[:, 0:npos], in0=a, scalar1=-math.pi, scalar2=TWO_PI, op0=ALU.is_lt, op1=ALU.mult),
             reads=["t1_1"], writes=["t1_2"])
        S.op("dve", lambda e: e.tensor_tensor(out=a, in0=a, in1=k_f[:, 0:npos], op=ALU.add), reads=["t1_1", "t1_2"], writes=["t1_1"])
        S.op("act", lambda e: e.activation(out=dst[:, 0:npos], in_=a, func=AF.Sin), reads=["t1_1"], writes=[("tab", dst.name)])
        if signed:
            S.op("dve", lambda e: e.tensor_scalar(out=dst[:, 0:npos], in0=dst[:, 0:npos], scalar1=small["rfreq"][:, 2:3],
                                                  scalar2=None, op0=ALU.mult), reads=[("tab", dst.name), "rfreq"],
                 writes=[("tab", dst.name)])

    trig(Srow, 256, 0, 0.0, True)
    trig(Crow, 256, 0, math.pi / 2, False)
    trig(Scol, 64, 1, 0.0, True)
    trig(Ccol, 64, 1, math.pi / 2, False)
    tabs = [("tab", "Srow"), ("tab", "Crow"), ("tab", "Scol"), ("tab", "Ccol")]

    if debug == "tabs":
        dbg = nc.dram_tensor("dbg", [128, 1024], F32, kind="ExternalOutput").ap()
        toks = []
        for i, (t, n) in enumerate(((Crow, 256), (Srow, 256), (Ccol, 64), (Scol, 64))):
            toks.append(S.op("sp", lambda e, t=t, n=n, i=i: e.dma_start(out=dbg[:, i * 256:i * 256 + n], in_=t[:, 0:n]),
                             reads=tabs, dma=True))
        toks.append(S.op("sp", lambda e: e.dma_start(out=dbg[:, 1000:1008], in_=lam_t[:]), reads=["lam_t"], dma=True))
        S.emit(toks)
        return nc

    S.barrier()
    def rms_stats(ncol):
        for k in range(8):
            b = k % 2
            S.op("act", lambda e, k=k, b=b: e.activation(out=sq_bf[b][:, 0:ncol], in_=x_t[:, k, 0:ncol], func=AF.Square),
                 reads=["x_t"], writes=[f"sq{b}"])
            S.op("pe", lambda e, k=k, b=b: e.matmul(bank[7][:, 0:ncol], ones_bf[:], sq_bf[b][:, 0:ncol], start=(k == 0), stop=(k == 7)),
                 reads=[f"sq{b}", "ones_bf"], writes=["bank7"])
        S.op("act", lambda e: e.activation(out=rs[:, 0:ncol], in_=bank[7][:, 0:ncol], func=AF.Sqrt, bias=eps6[:, 0:1], scale=1.0 / D),
             reads=["bank7", "eps6"], writes=["rs"])
        S.op("dve", lambda e: e.reciprocal(out=rs[:, 0:ncol], in_=rs[:, 0:ncol]), reads=["rs"], writes=["rs"])

    pi = [0]

    def proj_chunk(ci, n):
        bi = pi[0] % 4
        pi[0] += 1
        ps = bank[bi]
        for k in range(8):
            S.op("pe", lambda e, ps=ps, k=k, ci=ci, n=n: e.matmul(ps[:, 0:n], win_bf[:, k, ci * 128:(ci + 1) * 128], h_bf[:, k, 0:n],
                                                             start=(k == 0), stop=(k == 7)),
                 reads=[("w", "win_bf"), "h_bf"], writes=[f"bank{bi}"])
        return ps, f"bank{bi}"

    def pass1_tile(src, c0, n, v, is_ctx, tok0):
        S.op("sp", lambda e: e.dma_start(out=x_t[:, :, 0:n], in_=src[:, :, c0:c0 + n]), writes=["x_t"], dma=True)
        rms_stats(n)
        for k in range(8):
            b = k % 2
            S.op("dve", lambda e, k=k, b=b: e.tensor_tensor(out=t1[b][:, 0:n], in0=x_t[:, k, 0:n], in1=rs[:, 0:n], op=ALU.mult),
                 reads=["x_t", "rs"], writes=[f"t1_{b}"])
            S.op("act", lambda e, k=k, b=b: e.activation(out=h_bf[:, k, 0:n], in_=t1[b][:, 0:n], func=AF.Identity,
                                                         bias=mod[:, k, v:v + 1], scale=A1[:, k, v:v + 1]),
                 reads=[f"t1_{b}", "A1", "mod_sb"], writes=["h_bf"])
        if not is_ctx:
            r0 = c0 // 64
            S.op("pool", lambda e: e.tensor_tensor(
                out=cs_t[0][:, 0:n].rearrange("p (r c) -> p r c", c=64),
                in0=Crow[:, r0:r0 + n // 64].unsqueeze(2).to_broadcast([128, n // 64, 64]),
                in1=Ccol[:, :].unsqueeze(1).to_broadcast([128, n // 64, 64]), op=ALU.mult), reads=tabs, writes=["cs0"])
            S.op("pool", lambda e: e.tensor_tensor(
                out=cs_t[1][:, 0:n].rearrange("p (r c) -> p r c", c=64),
                in0=Srow[:, r0:r0 + n // 64].unsqueeze(2).to_broadcast([128, n // 64, 64]),
                in1=Scol[:, :].unsqueeze(1).to_broadcast([128, n // 64, 64]), op=ALU.add), reads=tabs, writes=["cs1"])
        for qi, (ci, dst, dc0) in enumerate(((0, QcT if is_ctx else QT, c0), (2, KT, tok0))):
            ps, key = proj_chunk(ci, n)
            if is_ctx:
                S.op("act", lambda e, ps=ps, dst=dst, dc0=dc0: e.copy(out=dst[:, dc0:dc0 + n], in_=ps[:, 0:n]),
                     reads=[key], writes=[("qk", dst.name)])
            else:
                ps2, key2 = proj_chunk(ci + 1, n)
                S.op("dve", lambda e, ps=ps: e.tensor_tensor(out=t1[0][:, 0:n], in0=ps[:, 0:n], in1=cs_t[0][:, 0:n], op=ALU.mult),
                     reads=[key, "cs0"], writes=["t1_0"])
                S.op("dve", lambda e, ps2=ps2: e.tensor_tensor(out=t1[1][:, 0:n], in0=ps2[:, 0:n], in1=cs_t[1][:, 0:n], op=ALU.mult),
                     reads=[key2, "cs1"], writes=["t1_1"])
                S.op("dve", lambda e, dst=dst, dc0=dc0: e.tensor_tensor(out=dst[:, dc0:dc0 + n], in0=t1[0][:, 0:n], in1=t1[1][:, 0:n], op=ALU.add),
                     reads=["t1_0", "t1_1"], writes=[("qk", dst.name)])
        for tt in range(n // 128):
            bi = pi[0] % 4
            pi[0] += 1
            ps = bank[bi]
            for k in range(8):
                S.op("pe", lambda e, ps=ps, k=k, tt=tt: e.matmul(ps[:, 0:128], h_bf[:, k, tt * 128:(tt + 1) * 128], win_bf[:, k, 4 * 128:5 * 128],
                                                              start=(k == 0), stop=(k == 7)),
                     reads=[("w", "win_bf"), "h_bf"], writes=[f"bank{bi}"])
            kt = (tok0 + tt * 128) // 128
            S.op("act", lambda e, ps=ps, kt=kt: e.copy(out=Vt[:, kt, :], in_=ps[:, 0:128]), reads=[f"bank{bi}"], writes=["Vt"])

    with nc.allow_low_precision("bf16 matmul operands, fp32 accumulation"):
        pass1_tile(cT, 0, CTXL, 1, True, 0)
        for ti in range(SEQ // 512):
            pass1_tile(xT, ti * 512, 512, 0, False, CTXL + ti * 512)

        if debug == "qk":
            dbg = nc.dram_tensor("dbg", [128, 2048], BF16, kind="ExternalOutput").ap()
            toks = [S.op("sp", lambda e: e.dma_start(out=dbg[:, 0:512], in_=QT[:, 0:512]), reads=[("qk", "QT")], dma=True),
                    S.op("sp", lambda e: e.dma_start(out=dbg[:, 512:1024], in_=KT[:, 0:512]), reads=[("qk", "KT")], dma=True),
                    S.op("sp", lambda e: e.dma_start(out=dbg[:, 1024:1536], in_=Vt[:, 0:4, :]), reads=["Vt"], dma=True),
                    S.op("sp", lambda e: e.dma_start(out=dbg[:, 1536:2048], in_=QT[:, SEQ - 512:SEQ]), reads=[("qk", "QT")], dma=True)]
            S.emit(toks)
            return nc

        S.barrier()
        si = [0]
        pti = [0]
        qsi = [0]

        def attend(Qsrc, q0, nq, kt0, kt1, dst, d0):
            O = [bank[4], bank[5]]
            L = [bank[6], bank[7]]
            units = [(kt, m) for kt in range(kt0, kt1) for m in range(2)]
            PRE = 3
            slots = {}
            qs = qsi[0] % 2
            qsi[0] += 1
            S.op("dve", lambda e: e.tensor_copy(out=Qpad[qs][0][0:64, 0:nq], in_=Qsrc[0:64, q0:q0 + nq]),
                 reads=[("qk", Qsrc.name)], writes=[f"Qpad{qs}0"])
            S.op("act", lambda e: e.copy(out=Qpad[qs][1][64:128, 0:nq], in_=Qsrc[64:128, q0:q0 + nq]),
                 reads=[("qk", Qsrc.name)], writes=[f"Qpad{qs}1"])

            def emit_S(i):
                kt, m = units[i]
                sb = si[0] % 4
                si[0] += 1
                sps = bank[sb]
                S.op("pe", lambda e: e.matmul(
                    sps[:, 0:nq], KT[:, kt * 128:(kt + 1) * 128], Qpad[qs][m][:, 0:nq],
                    start=True, stop=True), reads=[("qk", "KT"), f"Qpad{qs}{m}"], writes=[f"bank{sb}"])
                pb = pti[0] % 4
                pti[0] += 1
                S.op("act", lambda e: e.activation(out=PT[pb][:, 0:nq], in_=sps[:, 0:nq], func=AF.Exp, scale=0.125),
                     reads=[f"bank{sb}"], writes=[f"PT{pb}"])
                slots[i] = pb

            def emit_AV(i):
                kt, m = units[i]
                pb = slots[i]
                S.op("pe", lambda e: e.matmul(O[m][:, 0:nq], Vt[:, kt, :], PT[pb][:, 0:nq],
                                              start=(kt == kt0), stop=(kt == kt1 - 1)),
                     reads=["Vt", f"PT{pb}"], writes=[f"bank{4 + m}"])
                if kt == kt0:
                    S.op("dve", lambda e: e.tensor_copy(out=accL[m][:, 0:nq], in_=PT[pb][:, 0:nq]),
                         reads=[f"PT{pb}"], writes=[f"accL{m}"])
                else:
                    S.op("dve", lambda e: e.tensor_tensor(out=accL[m][:, 0:nq], in0=accL[m][:, 0:nq], in1=PT[pb][:, 0:nq], op=ALU.add),
                         reads=[f"PT{pb}", f"accL{m}"], writes=[f"accL{m}"])

            for i in range(len(units) + PRE):
                if i < len(units):
                    emit_S(i)
                if i - PRE >= 0:
                    emit_AV(i - PRE)
            for m in range(2):
                S.op("pe", lambda e, m=m: e.matmul(L[m][:, 0:nq], ones_f[:], accL[m][:, 0:nq], start=True, stop=True),
                     reads=["ones_f", f"accL{m}"], writes=[f"bank{6 + m}"])
            for m in range(2):
                S.op("dve", lambda e, m=m: e.reciprocal(out=t1[2][:, 0:nq], in_=L[m][:, 0:nq]), reads=[f"bank{6 + m}"], writes=["t1_2"])
                S.op("dve", lambda e, m=m: e.tensor_tensor(out=t1[m][:, 0:nq], in0=O[m][:, 0:nq], in1=t1[2][:, 0:nq], op=ALU.mult),
                     reads=[f"bank{4 + m}", "t1_2"], writes=[f"t1_{m}"])
            S.op("dve", lambda e: e.scalar_tensor_tensor(out=t1[0][:, 0:nq], in0=t1[1][:, 0:nq], scalar=lam_t[:, 4:5], in1=t1[0][:, 0:nq],
                                                         op0=ALU.mult, op1=ALU.add), reads=["t1_0", "t1_1", "lam_t"], writes=["t1_0"])
            S.op("act", lambda e: e.activation(out=sq_bf[0][:, 0:nq], in_=t1[0][:, 0:nq], func=AF.Square), reads=["t1_0"], writes=["sq0"])
            S.op("pe", lambda e: e.matmul(bank[6][:, 0:nq], ones_bf[:], sq_bf[0][:, 0:nq], start=True, stop=True),
                 reads=["sq0", "ones_bf"], writes=["bank6"])
            S.op("act", lambda e: e.activation(out=rs[:, 0:nq], in_=bank[6][:, 0:nq], func=AF.Sqrt, bias=eps5[:, 0:1], scale=1.0 / 128),
                 reads=["bank6", "eps5"], writes=["rs"])
            S.op("dve", lambda e: e.reciprocal(out=rs[:, 0:nq], in_=rs[:, 0:nq]), reads=["rs"], writes=["rs"])
            S.op("dve", lambda e: e.scalar_tensor_tensor(out=t1[1][:, 0:nq], in0=t1[0][:, 0:nq], scalar=subs[:, 0:1], in1=rs[:, 0:nq],
                                                         op0=ALU.mult, op1=ALU.mult), reads=["t1_0", "subs", "rs"], writes=["t1_1"])
            tok = S.op("sp", lambda e: e.dma_start(out=dst[:, d0:d0 + nq], in_=t1[1][:, 0:nq]), reads=["t1_1"], dma=True)
            S.final_tokens.append(tok)

        attend(QcT, 0, CTXL, 0, 2, attcT, 0)
        for qt in range(nq_tiles):
            attend(QT, qt * 512, 512, 0, NKT, attT, qt * 512)
        S.emit(S.final_tokens)
    return nc


def even_inputs(core, z, layer=0):
    b, g = divmod(core, 4)
    w = z["ev_w_in"][0]
    qc = np.arange(g * 128, (g + 1) * 128)
    d = qc % 32
    partner = np.where(d < 16, qc + 16, qc - 16)
    cols = np.concatenate([qc, partner, 512 + qc, 512 + partner, 1024 + qc,
                           1536 + qc, 1536 + 512 + qc, 1536 + 1024 + qc,
                           1536 + 1536 + np.arange(128), 1536 + 1536 + 128 + np.arange(128)])
    pd = np.arange(128) % 64
    inv = (10000.0 ** (-(np.arange(16, dtype=np.float32)) / 16.0)).astype(np.float32)
    f = inv[pd % 16]
    rfreq = np.stack([np.where(pd < 32, f, 0.0), np.where(pd >= 32, f, 0.0), np.where(pd % 32 < 16, -1.0, 1.0)], 1).astype(np.float32)
    return {
        "xT": fm(z["x"][b]), "cT": fm(z["ctx"][b]),
        "w_in": kch(np.ascontiguousarray(w[:, cols])),
        "ada_w": kch(z["ada_w"][layer]), "ada_b": vec_fm(z["ada_b"][layer]),
        "c2": np.ascontiguousarray(np.stack([vec_fm(z["c"][b]), vec_fm(z["c_ctx"])], axis=-1)),
        "nrm": vec_fm(z["norm_mix"][layer]),
        "lamv": np.ascontiguousarray(np.broadcast_to(z["diff_lambda"][0], (128, 4, 64))),
        "subw": np.ascontiguousarray(z["diff_subln"][0].reshape(128, 1)),
        "rfreq": rfreq,
        "rpos": np.ascontiguousarray(np.broadcast_to(np.arange(256, dtype=np.float32), (128, 256))),
    }


NTOK = SEQ + CTXL
MT = 256


def build_mamba(debug=None, nlat_chunks=128):
    nc = new_nc()
    S = Sched(nc)

    def din(name, shape):
        return nc.dram_tensor(name, shape, F32, kind="ExternalInput").ap()

    def dscr(name, shape):
        return nc.dram_tensor(name, shape, F32, kind="Internal").ap()

    xT = din("xT", [128, 8, SEQ + 4])
    cT = din("cT", [128, 8, CTXL + 4])
    w_xbc = din("w_xbc", [128, 8, 768])
    w_z = din("w_z", [128, 8, 512])
    w_dt = din("w_dt", [128, 8, 16])
    ada_w = din("ada_w", [128, 8, 6 * D])
    ada_b = din("ada_b", [128, 48])
    c2 = din("c2", [128, 8, 2])
    nrm = din("nrm", [128, 8])
    cw = din("cw", [128, 6, 5])
    cb = din("cb", [128, 6])
    dtb = din("dtb", [128, 16])
    alog = din("alog", [128, 16])
    dsk = din("dsk", [128, 8])
    nw = din("nw", [128, 512])
    ident = din("ident", [128, 128])
    trif = din("trif", [128, 128])
    trib = din("trib", [128, 128])
    gy_out = nc.dram_tensor("gy_out", [SEQ, 512], F32, kind="ExternalOutput").ap()
    xs_tok = dscr("xs_tok", [NTOK, 512])
    B_tok = dscr("B_tok", [NTOK, 128])
    BT = dscr("BT", [128, NTOK])
    CT = dscr("CT", [128, NTOK])
    dt_tok = dscr("dt_tok", [NTOK, 16])
    sz_tok = dscr("sz_tok", [SEQ, 512])
    y_tok = dscr("y_tok", [SEQ, 512])

    bank = [nc.alloc_psum_tensor(f"bank{i}", [128, 512], F32) for i in range(8)]

    ones_bf = nc.alloc_sbuf_tensor("ones_bf", [128, 128], BF16)
    ones_f = nc.alloc_sbuf_tensor("ones_f", [128, 128], F32)
    S.op("pool", lambda e: e.memset(ones_bf[:], 1.0), writes=["ones_bf"])
    S.op("pool", lambda e: e.memset(ones_f[:], 1.0), writes=["ones_f"])
    eps6 = nc.alloc_sbuf_tensor("eps6", [128, 1], F32)
    S.op("pool", lambda e: e.memset(eps6[:], 1e-6), writes=["eps6"])
    eps5 = nc.alloc_sbuf_tensor("eps5", [128, 1], F32)
    S.op("pool", lambda e: e.memset(eps5[:], 1e-5), writes=["eps5"])
    small = {}
    for name, src, shape in (("nrm", nrm, [128, 8]), ("cw", cw, [128, 6, 5]), ("cb", cb, [128, 6]), ("dtb", dtb, [128, 16]),
                             ("alog", alog, [128, 16]), ("dsk", dsk, [128, 8]), ("nw", nw, [128, 512]),
                             ("ident", ident, [128, 128]), ("trif", trif, [128, 128]), ("trib", trib, [128, 128])):
        t = nc.alloc_sbuf_tensor(name + "_sb", shape, F32)
        S.op("sp", lambda e, t=t, src=src: e.dma_start(out=t[:], in_=src), writes=[name], dma=True)
        small[name] = t
    negA = nc.alloc_sbuf_tensor("negA", [128, 16], F32)
    S.op("act", lambda e: e.activation(out=negA[:], in_=small["alog"][:], func=AF.Exp), reads=["alog"], writes=["negA"])
    S.op("dve", lambda e: e.tensor_scalar(out=negA[:], in0=negA[:], scalar1=-1.0, scalar2=None, op0=ALU.mult),
         reads=["negA"], writes=["negA"])
    mtiles = mod_alloc(nc, 2)
    A1 = nc.alloc_sbuf_tensor("A1", [128, 8, 2], F32)
    wx_bf = nc.alloc_sbuf_tensor("wx_bf", [128, 8, 768], BF16)
    wz_bf = nc.alloc_sbuf_tensor("wz_bf", [128, 8, 512], BF16)
    wdt_bf = nc.alloc_sbuf_tensor("wdt_bf", [128, 8, 16], BF16)
    x_t = nc.alloc_sbuf_tensor("x_t", [128, 8, 512], F32)
    h_bf = nc.alloc_sbuf_tensor("h_bf", [128, 8, 512], BF16)
    sq_bf = [nc.alloc_sbuf_tensor(f"sq_bf{i}", [128, 512], BF16) for i in range(2)]
    rs = nc.alloc_sbuf_tensor("rs", [128, 512], F32)
    t1 = [nc.alloc_sbuf_tensor(f"t1_{i}", [128, 512], F32) for i in range(3)]
    xbcT = nc.alloc_sbuf_tensor("xbcT", [128, 6, MT], F32)
    tok_sb = [nc.alloc_sbuf_tensor(f"tok_sb{i}", [128, 512], F32) for i in range(2)]
    btok_sb = nc.alloc_sbuf_tensor("btok_sb", [128, 128], F32)
    sz_sb = nc.alloc_sbuf_tensor("sz_sb", [128, 512], F32)
    dt_sb = nc.alloc_sbuf_tensor("dt_sb", [128, 16], F32)
    stA = nc.alloc_sbuf_tensor("stA", [128, 4096], F32)
    xs_cs = [nc.alloc_sbuf_tensor(f"xs_c{i}", [128, 512], F32) for i in range(2)]
    bt_fs = [nc.alloc_sbuf_tensor(f"bt_f{i}", [128, 3, 128], F32) for i in range(2)]
    scn = [0]
    bt_b = nc.alloc_sbuf_tensor("bt_b", [128, 3, 128], BF16)
    dt_cs = [nc.alloc_sbuf_tensor(f"dt_c{i}", [128, 16], F32) for i in range(2)]
    da = nc.alloc_sbuf_tensor("da", [128, 8], F32)
    da_bc = nc.alloc_sbuf_tensor("da_bc", [128, 8, 128], F32)
    cum_sb = nc.alloc_sbuf_tensor("cum_sb", [128, 24], F32)
    e_all = nc.alloc_sbuf_tensor("e_all", [128, 24], F32)
    xdt_f = nc.alloc_sbuf_tensor("xdt_f", [128, 512], F32)
    xdt_b = nc.alloc_sbuf_tensor("xdt_b", [128, 512], BF16)
    xw_b = nc.alloc_sbuf_tensor("xw_b", [128, 512], BF16)
    mcb = nc.alloc_sbuf_tensor("mcb", [128, 128], F32)
    seg = nc.alloc_sbuf_tensor("seg", [128, 8, 128], F32)
    G_b = nc.alloc_sbuf_tensor("G_b", [128, 8, 128], BF16)
    ysc = nc.alloc_sbuf_tensor("ysc", [128, 512], F32)
    ytmp = nc.alloc_sbuf_tensor("ytmp", [128, 512], F32)
    yprevs = [nc.alloc_sbuf_tensor(f"yprev{i}", [128, 512], F32) for i in range(2)]
    szcs = [nc.alloc_sbuf_tensor(f"szc{i}", [128, 512], F32) for i in range(2)]
    h_f = nc.alloc_sbuf_tensor("h_f", [128, 512], F32)
    h_b = nc.alloc_sbuf_tensor("h_b", [128, 512], BF16)
    ss1 = nc.alloc_sbuf_tensor("ss1", [128, 2], F32)
    print("sbuf remaining (mamba)", nc.sbuf_bytes_remaining)

    mod = mod_vectors(nc, S, ada_w, ada_b, c2, [0, 1], bank[7], stA, "stA", mtiles)
    S.op("dve", lambda e: e.scalar_tensor_tensor(
        out=A1[:], in0=mod[:, 8:16, :], scalar=1.0, in1=small["nrm"][:].unsqueeze(2).to_broadcast([128, 8, 2]),
        op0=ALU.add, op1=ALU.mult), reads=["mod_sb", "nrm"], writes=["A1"])
    cast_weights(nc, S, w_xbc, wx_bf, 8, 768, [(stA, "stA")], 768)
    cast_weights(nc, S, w_z, wz_bf, 8, 512, [(stA, "stA")], 512)
    cast_weights(nc, S, w_dt, wdt_bf, 8, 16, [(stA, "stA")], 16)
    S.barrier()

    def rms_stats(ncol):
        for k in range(8):
            b = k % 2
            S.op("act", lambda e, k=k, b=b: e.activation(out=sq_bf[b][:, 0:ncol], in_=x_t[:, k, 0:ncol], func=AF.Square),
                 reads=["x_t"], writes=[f"sq{b}"])
            S.op("pe", lambda e, k=k, b=b: e.matmul(bank[7][:, 0:ncol], ones_bf[:], sq_bf[b][:, 0:ncol], start=(k == 0), stop=(k == 7)),
                 reads=[f"sq{b}", "ones_bf"], writes=["bank7"])
        S.op("act", lambda e: e.activation(out=rs[:, 0:ncol], in_=bank[7][:, 0:ncol], func=AF.Sqrt, bias=eps6[:, 0:1], scale=1.0 / D),
             reads=["bank7", "eps6"], writes=["rs"])
        S.op("dve", lambda e: e.reciprocal(out=rs[:, 0:ncol], in_=rs[:, 0:ncol]), reads=["rs"], writes=["rs"])

    pi = [0]

    def nbank():
        bi = pi[0] % 6
        pi[0] += 1
        return bank[bi], f"bank{bi}"

    def pass1_tile(src, c0, tok0, lat0, v, first, last):
        n = MT + 4
        S.op("sp", lambda e: e.dma_start(out=x_t[:, :, 0:n], in_=src[:, :, c0:c0 + n]), writes=["x_t"], dma=True)
        rms_stats(n)
        for k in range(8):
            b = k % 2
            S.op("dve", lambda e, k=k, b=b: e.tensor_tensor(out=t1[b][:, 0:n], in0=x_t[:, k, 0:n], in1=rs[:, 0:n], op=ALU.mult),
                 reads=["x_t", "rs"], writes=[f"t1_{b}"])
            S.op("act", lambda e, k=k, b=b: e.activation(out=h_bf[:, k, 0:n], in_=t1[b][:, 0:n], func=AF.Identity,
                                                         bias=mod[:, k, v:v + 1], scale=A1[:, k, v:v + 1]),
                 reads=[f"t1_{b}", "A1", "mod_sb"], writes=["h_bf"])
        if first:
            S.op("dve", lambda e: e.tensor_scalar(out=h_bf[:, :, 0:2], in0=h_bf[:, :, 0:2], scalar1=0.0, scalar2=None, op0=ALU.mult),
                 reads=["h_bf"], writes=["h_bf"])
        if last:
            S.op("dve", lambda e: e.tensor_scalar(out=h_bf[:, :, n - 2:n], in0=h_bf[:, :, n - 2:n], scalar1=0.0, scalar2=None, op0=ALU.mult),
                 reads=["h_bf"], writes=["h_bf"])
        for j in range(6):
            ps, key = nbank()
            for k in range(8):
                S.op("pe", lambda e, ps=ps, k=k, j=j: e.matmul(ps[:, 0:n], wx_bf[:, k, j * 128:(j + 1) * 128], h_bf[:, k, 0:n],
                                                              start=(k == 0), stop=(k == 7)),
                     reads=[("w", "wx_bf"), "h_bf"], writes=[key])
            ta = t1[2]
            S.op("act", lambda e, ps=ps, j=j: e.activation(out=ta[:, 0:MT], in_=ps[:, 0:MT], func=AF.Identity, scale=small["cw"][:, j, 0:1]),
                 reads=[key, "cw"], writes=["t1_2"])
            for kk in range(1, 5):
                S.op("dve", lambda e, ps=ps, j=j, kk=kk: e.scalar_tensor_tensor(
                    out=ta[:, 0:MT], in0=ps[:, kk:kk + MT], scalar=small["cw"][:, j, kk:kk + 1], in1=ta[:, 0:MT],
                    op0=ALU.mult, op1=ALU.add), reads=[key, "cw", "t1_2"], writes=["t1_2"])
            S.op("act", lambda e, j=j: e.activation(out=xbcT[:, j, :], in_=ta[:, 0:MT], func=AF.Silu, bias=small["cb"][:, j:j + 1]),
                 reads=["t1_2", "cb"], writes=[("xbc", j)])
        S.op("sp", lambda e: e.dma_start(out=BT[:, tok0:tok0 + MT], in_=xbcT[:, 4, :]), reads=[("xbc", 4)], dma=True)
        S.op("sp", lambda e: e.dma_start(out=CT[:, tok0:tok0 + MT], in_=xbcT[:, 5, :]), reads=[("xbc", 5)], dma=True)
        for tt in range(MT // 128):
            r0 = tok0 + tt * 128
            ps, key = nbank()
            for j in range(4):
                S.op("pe", lambda e, ps=ps, j=j, tt=tt: e.matmul(ps[:, j * 128:(j + 1) * 128], xbcT[:, j, tt * 128:(tt + 1) * 128],
                                                              small["ident"][:], start=True, stop=True),
                     reads=[("xbc", j), "ident"], writes=[key])
            tb = tok_sb[tt % 2]
            S.op("act", lambda e, ps=ps, tb=tb: e.copy(out=tb[:], in_=ps[:, 0:512]), reads=[key], writes=[f"tok_sb{tt % 2}"])
            S.op("sp", lambda e, tb=tb, r0=r0: e.dma_start(out=xs_tok[r0:r0 + 128, :], in_=tb[:]), reads=[f"tok_sb{tt % 2}"], dma=True)
            ps, key = nbank()
            S.op("pe", lambda e, ps=ps, tt=tt: e.matmul(ps[:, 0:128], xbcT[:, 4, tt * 128:(tt + 1) * 128], small["ident"][:],
                                                        start=True, stop=True), reads=[("xbc", 4), "ident"], writes=[key])
            S.op("act", lambda e, ps=ps: e.copy(out=btok_sb[:], in_=ps[:, 0:128]), reads=[key], writes=["btok_sb"])
            S.op("sp", lambda e, r0=r0: e.dma_start(out=B_tok[r0:r0 + 128, :], in_=btok_sb[:]), reads=["btok_sb"], dma=True)
            if lat0 is not None:
                ps, key = nbank()
                for k in range(8):
                    S.op("pe", lambda e, ps=ps, k=k, tt=tt: e.matmul(ps[:, 0:512], h_bf[:, k, 2 + tt * 128:2 + (tt + 1) * 128], wz_bf[:, k, :],
                                                                  start=(k == 0), stop=(k == 7)),
                         reads=[("w", "wz_bf"), "h_bf"], writes=[key])
                S.op("act", lambda e, ps=ps: e.activation(out=sz_sb[:], in_=ps[:, 0:512], func=AF.Silu), reads=[key], writes=["sz_sb"])
                l0 = lat0 + tt * 128
                S.op("sp", lambda e, l0=l0: e.dma_start(out=sz_tok[l0:l0 + 128, :], in_=sz_sb[:]), reads=["sz_sb"], dma=True)
            ps, key = nbank()
            for k in range(8):
                S.op("pe", lambda e, ps=ps, k=k, tt=tt: e.matmul(ps[:, 0:16], h_bf[:, k, 2 + tt * 128:2 + (tt + 1) * 128], wdt_bf[:, k, :],
                                                              start=(k == 0), stop=(k == 7)),
                     reads=[("w", "wdt_bf"), "h_bf"], writes=[key])
            S.op("dve", lambda e, ps=ps: e.tensor_tensor(out=dt_sb[:], in0=ps[:, 0:16], in1=small["dtb"][:], op=ALU.add),
                 reads=[key, "dtb"], writes=["dt_sb"])
            S.op("act", lambda e: e.activation(out=dt_sb[:], in_=dt_sb[:], func=AF.Exp), reads=["dt_sb"], writes=["dt_sb"])
            S.op("act", lambda e: e.activation(out=dt_sb[:], in_=dt_sb[:], func=AF.Ln, bias=ones_f[:, 0:1]), reads=["dt_sb"], writes=["dt_sb"])
            S.op("sp", lambda e, r0=r0: e.dma_start(out=dt_tok[r0:r0 + 128, :], in_=dt_sb[:]), reads=["dt_sb"], dma=True)

    def scan_chunk(d, r0, lat_r0, tri, trikey):
        is_ctx = lat_r0 is None
        par = scn[0] % 2
        scn[0] += 1
        xs_c, bt_f, dt_c, yprev, szc = xs_cs[par], bt_fs[par], dt_cs[par], yprevs[par], szcs[par]
        S.op("sp", lambda e: e.dma_start(out=xs_c[:], in_=xs_tok[r0:r0 + 128, :]), writes=[f"xs_c{par}"], dma=True)
        S.op("sp", lambda e: e.dma_start(out=bt_f[:, 0, :], in_=B_tok[r0:r0 + 128, :]), writes=[f"bt_f0{par}"], dma=True)
        S.op("sp", lambda e: e.dma_start(out=bt_f[:, 1, :], in_=BT[:, r0:r0 + 128]), writes=[f"bt_f1{par}"], dma=True)
        S.op("sp", lambda e: e.dma_start(out=bt_f[:, 2, :], in_=CT[:, r0:r0 + 128]), writes=[f"bt_f2{par}"], dma=True)
        S.op("sp", lambda e: e.dma_start(out=dt_c[:], in_=dt_tok[r0:r0 + 128, :]), writes=[f"dt_c{par}"], dma=True)
        S.op("act", lambda e: e.copy(out=bt_b[:], in_=bt_f[:]), reads=[f"bt_f0{par}", f"bt_f1{par}", f"bt_f2{par}"], writes=["bt_b"])
        S.op("dve", lambda e: e.tensor_tensor(out=da[:], in0=dt_c[:, d * 8:(d + 1) * 8], in1=negA[:, d * 8:(d + 1) * 8], op=ALU.mult),
             reads=[f"dt_c{par}", "negA"], writes=["da"])
        S.op("dve", lambda e: e.tensor_tensor(out=xdt_f[:].rearrange("p (e q) -> p e q", e=8), in0=xs_c[:].rearrange("p (e q) -> p e q", e=8),
                                              in1=dt_c[:, d * 8:(d + 1) * 8].unsqueeze(2).to_broadcast([128, 8, 64]), op=ALU.mult),
             reads=[f"xs_c{par}", f"dt_c{par}"], writes=["xdt_f"])
        S.op("pe", lambda e: e.matmul(bank[0][:, 0:8], tri[:], da[:], start=True, stop=True), reads=[trikey, "da"], writes=["bank0"])
        S.op("pe", lambda e: e.matmul(bank[0][:, 8:16], ones_f[:], da[:], start=True, stop=True), reads=["ones_f", "da"], writes=["bank0"])
        S.op("dve", lambda e: e.tensor_copy(out=cum_sb[:, 0:16], in_=bank[0][:, 0:16]), reads=["bank0"], writes=["cum_sb"])
        S.op("dve", lambda e: e.tensor_tensor(out=cum_sb[:, 16:24], in0=cum_sb[:, 8:16], in1=cum_sb[:, 0:8], op=ALU.subtract),
             reads=["cum_sb"], writes=["cum_sb"])
        S.op("act", lambda e: e.activation(out=e_all[:], in_=cum_sb[:], func=AF.Exp), reads=["cum_sb"], writes=["e_all"])
        if not is_ctx:
            S.op("pool", lambda e: e.tensor_copy(out=xdt_b[:], in_=xdt_f[:]), reads=["xdt_f"], writes=["xdt_b"])
            S.op("dve", lambda e: e.tensor_copy(out=da_bc[:], in_=da[:].unsqueeze(2).to_broadcast([128, 8, 128])), reads=["da"], writes=["da_bc"])
            for e_ in range(8):
                bk = 1 + e_ // 4
                S.op("pe", lambda e, e_=e_, bk=bk: e.matmul(bank[bk][:, (e_ % 4) * 128:(e_ % 4 + 1) * 128], da_bc[:, e_, :], tri[:],
                                                            start=True, stop=True), reads=["da_bc", trikey], writes=[f"bank{bk}"])
            S.op("pe", lambda e: e.matmul(bank[3][:, 0:128], bt_b[:, 1, :], bt_b[:, 2, :], start=True, stop=True),
                 reads=["bt_b"], writes=["bank3"])
            S.op("dve", lambda e: e.tensor_tensor(out=mcb[:], in0=bank[3][:, 0:128], in1=tri[:], op=ALU.mult),
                 reads=["bank3", trikey], writes=["mcb"])
            for half in range(2):
                S.op("dve", lambda e, half=half: e.tensor_tensor(
                    out=seg[:, half * 4:(half + 1) * 4, :], in0=bank[1 + half][:, 0:512].rearrange("p (e l) -> p e l", e=4),
                    in1=cum_sb[:, half * 4:(half + 1) * 4].unsqueeze(2).to_broadcast([128, 4, 128]), op=ALU.subtract),
                    reads=[f"bank{1 + half}", "cum_sb"], writes=["seg"])
            S.op("dve", lambda e: e.tensor_scalar(out=seg[:], in0=seg[:], scalar1=0.0, scalar2=None, op0=ALU.min), reads=["seg"], writes=["seg"])
            S.op("act", lambda e: e.activation(out=seg[:], in_=seg[:], func=AF.Exp), reads=["seg"], writes=["seg"])
            S.op("dve", lambda e: e.tensor_tensor(out=G_b[:], in0=seg[:], in1=mcb[:].unsqueeze(1).to_broadcast([128, 8, 128]), op=ALU.mult),
                 reads=["seg", "mcb"], writes=["G_b"])
            for e_ in range(8):
                S.op("pe", lambda e, e_=e_: e.matmul(bank[4][:, e_ * 64:(e_ + 1) * 64], G_b[:, e_, :], xdt_b[:, e_ * 64:(e_ + 1) * 64],
                                                     start=True, stop=True), reads=["G_b", "xdt_b"], writes=["bank4"])
            S.op("pe", lambda e: e.matmul(bank[5][:, 0:512], bt_b[:, 2, :], h_b[:], start=True, stop=True), reads=["bt_b", "h_b"], writes=["bank5"])
            S.op("dve", lambda e: e.tensor_tensor(out=ysc[:].rearrange("p (e q) -> p e q", e=8), in0=bank[5][:, 0:512].rearrange("p (e q) -> p e q", e=8),
                                                  in1=e_all[:, 0:8].unsqueeze(2).to_broadcast([128, 8, 64]), op=ALU.mult),
                 reads=["bank5", "e_all"], writes=["ysc"])
            S.op("dve", lambda e: e.tensor_tensor(out=ysc[:], in0=ysc[:], in1=bank[4][:, 0:512], op=ALU.add), reads=["ysc", "bank4"], writes=["ysc"])
            if d == 0:
                S.op("pool", lambda e: e.tensor_tensor(out=ytmp[:].rearrange("p (e q) -> p e q", e=8), in0=xs_c[:].rearrange("p (e q) -> p e q", e=8),
                                                       in1=small["dsk"][:].unsqueeze(2).to_broadcast([128, 8, 64]), op=ALU.mult),
                     reads=[f"xs_c{par}", "dsk"], writes=["ytmp"])
                S.op("dve", lambda e: e.tensor_tensor(out=ysc[:], in0=ysc[:], in1=ytmp[:], op=ALU.add), reads=["ysc", "ytmp"], writes=["ysc"])
                S.op("sp", lambda e: e.dma_start(out=y_tok[lat_r0:lat_r0 + 128, :], in_=ysc[:]), reads=["ysc"], writes=["yscr"], dma=True)
            else:
                S.op("sp", lambda e: e.dma_start(out=yprev[:], in_=y_tok[lat_r0:lat_r0 + 128, :]), reads=["yscr"], writes=[f"yprev{par}"], dma=True)
                S.op("sp", lambda e: e.dma_start(out=szc[:], in_=sz_tok[lat_r0:lat_r0 + 128, :]), writes=[f"szc{par}"], dma=True)
                S.op("dve", lambda e: e.tensor_tensor(out=ysc[:], in0=ysc[:], in1=yprev[:], op=ALU.add), reads=["ysc", f"yprev{par}"], writes=["ysc"])
                S.op("dve", lambda e: e.tensor_tensor(out=ysc[:], in0=ysc[:], in1=szc[:], op=ALU.mult), reads=["ysc", f"szc{par}"], writes=["ysc"])
                S.op("act", lambda e: e.activation(out=ytmp[:], in_=ysc[:], func=AF.Square, accum_out=ss1[:, 0:1]),
                     reads=["ysc"], writes=["ytmp", "ss1"])
                S.op("act", lambda e: e.activation(out=ss1[:, 1:2], in_=ss1[:, 0:1], func=AF.Sqrt, bias=eps5[:, 0:1], scale=1.0 / 512),
                     reads=["ss1", "eps5"], writes=["ss1b"])
                S.op("dve", lambda e: e.reciprocal(out=ss1[:, 1:2], in_=ss1[:, 1:2]), reads=["ss1b"], writes=["ss1b"])
                S.op("dve", lambda e: e.scalar_tensor_tensor(out=ytmp[:], in0=ysc[:], scalar=ss1[:, 1:2], in1=small["nw"][:],
                                                             op0=ALU.mult, op1=ALU.mult), reads=["ysc", "ss1b", "nw", "ytmp"], writes=["ytmp"])
                tok = S.op("sp", lambda e: e.dma_start(out=gy_out[lat_r0:lat_r0 + 128, :], in_=ytmp[:]), reads=["ytmp"], dma=True)
                S.final_tokens.append(tok)
        S.op("dve", lambda e: e.tensor_tensor(out=xw_b[:].rearrange("p (e q) -> p e q", e=8), in0=xdt_f[:].rearrange("p (e q) -> p e q", e=8),
                                              in1=e_all[:, 16:24].unsqueeze(2).to_broadcast([128, 8, 64]), op=ALU.mult),
             reads=["xdt_f", "e_all"], writes=["xw_b"])
        S.op("pe", lambda e: e.matmul(bank[6][:, 0:512], bt_b[:, 0, :], xw_b[:], start=True, stop=True), reads=["bt_b", "xw_b"], writes=["bank6"])
        S.op("dve", lambda e: e.tensor_tensor(out=h_f[:].rearrange("p (e q) -> p e q", e=8), in0=h_f[:].rearrange("p (e q) -> p e q", e=8),
                                              in1=e_all[:, 8:16].unsqueeze(2).to_broadcast([128, 8, 64]), op=ALU.mult),
             reads=["h_f", "e_all"], writes=["h_f"])
        S.op("dve", lambda e: e.tensor_tensor(out=h_f[:], in0=h_f[:], in1=bank[6][:, 0:512], op=ALU.add), reads=["h_f", "bank6"], writes=["h_f"])
        S.op("act", lambda e: e.copy(out=h_b[:], in_=h_f[:]), reads=["h_f"], writes=["h_b"])

    with nc.allow_low_precision("bf16 matmul operands, fp32 accumulation"):
        pass1_tile(cT, 0, 0, None, 1, True, True)
        nt = SEQ // MT
        for ti in range(nt):
            pass1_tile(xT, ti * MT, CTXL + ti * MT, ti * MT, 0, ti == 0, ti == nt - 1)
        S.barrier()
        for d in range(2):
            tri, trikey = (small["trif"], "trif") if d == 0 else (small["trib"], "trib")
            S.op("dve", lambda e: e.tensor_scalar(out=h_f[:], in0=small["nw"][:], scalar1=0.0, scalar2=None, op0=ALU.mult),
                 reads=["nw", "h_f"], writes=["h_f"])
            S.op("act", lambda e: e.copy(out=h_b[:], in_=h_f[:]), reads=["h_f"], writes=["h_b"])
            cchunks = [0, 1] if d == 0 else [1, 0]
            for c in cchunks:
                scan_chunk(d, c * 128, None, tri, trikey)
            lch = list(range(nlat_chunks)) if d == 0 else list(range(nlat_chunks - 1, -1, -1))
            for c in lch:
                scan_chunk(d, CTXL + c * 128, c * 128, tri, trikey)
        S.barrier()
        S.emit(S.final_tokens)
    return nc


def pad2(a):
    return np.concatenate([np.zeros((2, a.shape[1]), a.dtype), a, np.zeros((2, a.shape[1]), a.dtype)], 0)


def mamba_inputs(core, z, x1, xc1, layer=1):
    b, g = divmod(core, 4)
    w = z["ssm_w_in"][0]
    xcols = 2048 + g * 512 + np.arange(512)
    bcols = 2048 + 2048 + g * 128 + np.arange(128)
    ccols = 2048 + 2560 + g * 128 + np.arange(128)
    heads = g * 8 + np.arange(8)
    dtcols = np.concatenate([5120 + heads, 5120 + 32 + heads])
    conv_cols = np.concatenate([g * 512 + np.arange(512), 2048 + g * 128 + np.arange(128), 2560 + g * 128 + np.arange(128)])
    cwf = z["ssm_conv_w"][0][:, conv_cols]
    rep = lambda v: np.ascontiguousarray(np.broadcast_to(v.astype(np.float32), (128,) + v.shape))
    idx = np.arange(128)
    return {
        "xT": fm(pad2(x1[b])), "cT": fm(pad2(xc1[b])),
        "w_xbc": kch(np.ascontiguousarray(w[:, np.concatenate([xcols, bcols, ccols])])),
        "w_z": kch(np.ascontiguousarray(w[:, g * 512:(g + 1) * 512])),
        "w_dt": kch(np.ascontiguousarray(w[:, dtcols])),
        "ada_w": kch(z["ada_w"][layer]), "ada_b": vec_fm(z["ada_b"][layer]),
        "c2": np.ascontiguousarray(np.stack([vec_fm(z["c"][b]), vec_fm(z["c_ctx"])], axis=-1)),
        "nrm": vec_fm(z["norm_mix"][layer]),
        "cw": np.ascontiguousarray(cwf.T.reshape(6, 128, 5).transpose(1, 0, 2)),
        "cb": vec_fm(z["ssm_conv_b"][0][conv_cols]),
        "dtb": rep(np.concatenate([z["ssm_dt_bias"][0][0][heads], z["ssm_dt_bias"][0][1][heads]])),
        "alog": rep(np.concatenate([z["ssm_a_log"][0][0][heads], z["ssm_a_log"][0][1][heads]])),
        "dsk": rep(z["ssm_d"][0][heads]),
        "nw": rep(z["ssm_norm_w"][0][g * 512:(g + 1) * 512]),
        "ident": np.eye(128, dtype=np.float32),
        "trif": (idx[:, None] <= idx[None, :]).astype(np.float32),
        "trib": (idx[:, None] >= idx[None, :]).astype(np.float32),
    }


NCHK = NTOK // 128
RW_TILE = 510
LWC = -math.exp(-0.5)
GN_EPS = 64e-5


def build_rwkv(debug=None, nsteps=NCHK):
    nc = new_nc()
    S = Sched(nc)

    def din(name, shape):
        return nc.dram_tensor(name, shape, F32, kind="ExternalInput").ap()

    def dscr(name, shape):
        return nc.dram_tensor(name, shape, F32, kind="Internal").ap()

    xT = din("xT", [128, 8, SEQ + 2])
    cT = din("cT", [128, 8, CTXL + 2])
    w_in = din("w_in", [128, 8, 640])
    ada_w = din("ada_w", [128, 8, 6 * D])
    ada_b = din("ada_b", [128, 48])
    c2 = din("c2", [128, 8, 2])
    nrm = din("nrm", [128, 8])
    mu = din("mu", [128, 5, 2])
    pvec = din("pvec", [128, 8])
    lor = din("lor", [128, 2, 128])
    gup = din("gup", [128, 128])
    lnwb = din("lnwb", [128, 2, 128])
    ident = din("ident", [128, 128])
    bones = din("bones", [128, 128])
    masks = din("masks", [128, 2, 2, 128])
    mask3 = din("mask3", [128, 2, 128])
    rw_out = nc.dram_tensor("rw_out", [NTOK, 128], F32, kind="ExternalOutput").ap()
    SCR = [dscr(f"scr{d}", [128, 6, NTOK]) for d in range(2)]
    BN = dscr("bn_scr", [128, 2, NTOK])
    YS = dscr("ys_scr", [NTOK, 2, 128])

    bank = [nc.alloc_psum_tensor(f"bank{i}", [128, 512], F32) for i in range(8)]

    ones_bf = nc.alloc_sbuf_tensor("ones_bf", [128, 128], BF16)
    ones_f = nc.alloc_sbuf_tensor("ones_f", [128, 128], F32)
    S.op("pool", lambda e: e.memset(ones_bf[:], 1.0), writes=["ones_bf"])
    S.op("pool", lambda e: e.memset(ones_f[:], 1.0), writes=["ones_f"])
    eps6 = nc.alloc_sbuf_tensor("eps6", [128, 1], F32)
    S.op("pool", lambda e: e.memset(eps6[:], 1e-6), writes=["eps6"])
    epsg = nc.alloc_sbuf_tensor("epsg", [128, 1], F32)
    S.op("pool", lambda e: e.memset(epsg[:], GN_EPS), writes=["epsg"])
    small = {}
    for name, src, shape in (("nrm", nrm, [128, 8]), ("mu", mu, [128, 5, 2]), ("pvec", pvec, [128, 8]), ("lor", lor, [128, 2, 128]),
                             ("gup", gup, [128, 128]), ("lnwb", lnwb, [128, 2, 128]), ("ident", ident, [128, 128]),
                             ("bones", bones, [128, 128]), ("masks", masks, [128, 2, 2, 128]), ("mask3", mask3, [128, 2, 128])):
        t = nc.alloc_sbuf_tensor(name + "_sb", shape, F32)
        S.op("sp", lambda e, t=t, src=src: e.dma_start(out=t[:], in_=src), writes=[name], dma=True)
        small[name] = t
    pv = small["pvec"]
    muc = nc.alloc_sbuf_tensor("muc", [128, 5], F32)
    S.op("dve", lambda e: e.tensor_tensor(out=muc[:], in0=small["mu"][:, :, 0], in1=small["mu"][:, :, 1], op=ALU.add),
         reads=["mu"], writes=["muc"])
    S.op("dve", lambda e: e.tensor_scalar(out=muc[:], in0=muc[:], scalar1=-1.0, scalar2=1.0, op0=ALU.mult, op1=ALU.add),
         reads=["muc"], writes=["muc"])
    omka = nc.alloc_sbuf_tensor("omka", [128, 1], F32)
    S.op("dve", lambda e: e.tensor_scalar(out=omka[:], in0=pv[:, 1:2], scalar1=-1.0, scalar2=1.0, op0=ALU.mult, op1=ALU.add),
         reads=["pvec"], writes=["omka"])
    mtiles = mod_alloc(nc, 2)
    A1m = nc.alloc_sbuf_tensor("A1m", [128, 8, 2], F32)
    win_bf = nc.alloc_sbuf_tensor("win_bf", [128, 8, 640], BF16)
    x_t = nc.alloc_sbuf_tensor("x_t", [128, 8, 512], F32)
    h_bf = nc.alloc_sbuf_tensor("h_bf", [128, 8, 512], BF16)
    sq_bf = [nc.alloc_sbuf_tensor(f"sq_bf{i}", [128, 512], BF16) for i in range(2)]
    rs = nc.alloc_sbuf_tensor("rs", [128, 512], F32)
    t1 = [nc.alloc_sbuf_tensor(f"t1_{i}", [128, 512], F32) for i in range(3)]
    PP = nc.alloc_sbuf_tensor("PP", [128, 5, 512], F32)
    OUT = nc.alloc_sbuf_tensor("OUT", [128, 2, 6, 512], F32)
    BNO = nc.alloc_sbuf_tensor("BNO", [128, 2, 512], F32)
    AD = nc.alloc_sbuf_tensor("AD", [128, 512], F32)
    stA = nc.alloc_sbuf_tensor("stA", [128, 4096], F32)
    INs = [nc.alloc_sbuf_tensor(f"IN{i}", [128, 2, 6, 128], F32) for i in range(2)]
    CL = nc.alloc_sbuf_tensor("CL", [128, 2, 128], F32)
    CLX = nc.alloc_sbuf_tensor("CLX", [128, 2, 128], F32)
    TOT = nc.alloc_sbuf_tensor("TOT", [128, 2], F32)
    EG = nc.alloc_sbuf_tensor("EG", [128, 2], F32)
    E = [nc.alloc_sbuf_tensor(f"E{i}", [128, 2, 128], F32) for i in range(4)]
    RKp = [nc.alloc_sbuf_tensor(f"RKp{h}", [128, 2, 2, 128], F32) for h in range(2)]
    for h in range(2):
        S.op("pool", lambda e, h=h: e.memset(RKp[h][:], 0.0), writes=[f"RK{h}0", f"RK{h}1"])
    KB = nc.alloc_sbuf_tensor("KB", [128, 2, 2, 128], F32)
    HAT = nc.alloc_sbuf_tensor("HAT", [128, 2, 2, 128], F32)
    A1 = nc.alloc_sbuf_tensor("A1", [128, 2, 2, 2, 128], F32)
    A2 = nc.alloc_sbuf_tensor("A2", [128, 2, 2, 2, 128], F32)
    XK = [nc.alloc_sbuf_tensor(f"XK{i}", [128, 4, 128], F32) for i in range(2)]
    YK = [nc.alloc_sbuf_tensor(f"YK{i}", [128, 4, 128], F32) for i in range(2)]
    QT = nc.alloc_sbuf_tensor("QT", [128, 4, 128], F32)
    VT = nc.alloc_sbuf_tensor("VT", [128, 2, 128], F32)
    HT = nc.alloc_sbuf_tensor("HT", [128, 2, 2, 128], F32)
    Zs = nc.alloc_sbuf_tensor("Zs", [128, 256], F32)
    Us = nc.alloc_sbuf_tensor("Us", [128, 256], F32)
    Ys = nc.alloc_sbuf_tensor("Ys", [128, 256], F32)
    MST = nc.alloc_sbuf_tensor("MST", [128, 2, 64], F32)
    Y3 = nc.alloc_sbuf_tensor("Y3", [128, 2, 128], F32)
    B3 = nc.alloc_sbuf_tensor("B3", [128, 2, 128], F32)
    y3 = nc.alloc_sbuf_tensor("y3", [128, 128], F32)
    y3b = nc.alloc_sbuf_tensor("y3b", [128, 128], F32)
    st3 = nc.alloc_sbuf_tensor("st3", [128, 8], F32)
    print("sbuf remaining (rwkv)", nc.sbuf_bytes_remaining)

    mod = mod_vectors(nc, S, ada_w, ada_b, c2, [0, 1], bank[7], stA, "stA", mtiles)
    S.op("dve", lambda e: e.scalar_tensor_tensor(
        out=A1m[:], in0=mod[:, 8:16, :], scalar=1.0, in1=small["nrm"][:].unsqueeze(2).to_broadcast([128, 8, 2]),
        op0=ALU.add, op1=ALU.mult), reads=["mod_sb", "nrm"], writes=["A1m"])
    cast_weights(nc, S, w_in, win_bf, 8, 640, [(stA, "stA")], 640)
    S.barrier()

    def rms_stats(ncol):
        for k in range(8):
            b = k % 2
            S.op("act", lambda e, k=k, b=b: e.activation(out=sq_bf[b][:, 0:ncol], in_=x_t[:, k, 0:ncol], func=AF.Square),
                 reads=["x_t"], writes=[f"sq{b}"])
            S.op("pe", lambda e, k=k, b=b: e.matmul(bank[7][:, 0:ncol], ones_bf[:], sq_bf[b][:, 0:ncol], start=(k == 0), stop=(k == 7)),
                 reads=[f"sq{b}", "ones_bf"], writes=["bank7"])
        S.op("act", lambda e: e.activation(out=rs[:, 0:ncol], in_=bank[7][:, 0:ncol], func=AF.Sqrt, bias=eps6[:, 0:1], scale=1.0 / D),
             reads=["bank7", "eps6"], writes=["rs"])
        S.op("dve", lambda e: e.reciprocal(out=rs[:, 0:ncol], in_=rs[:, 0:ncol]), reads=["rs"], writes=["rs"])

    def pass1_tile(src, c0, no, tok0, v, first, last):
        n = no + 2
        S.op("sp", lambda e: e.dma_start(out=x_t[:, :, 0:n], in_=src[:, :, c0:c0 + n]), writes=["x_t"], dma=True)
        rms_stats(n)
        for k in range(8):
            b = k % 2
            S.op("dve", lambda e, k=k, b=b: e.tensor_tensor(out=t1[b][:, 0:n], in0=x_t[:, k, 0:n], in1=rs[:, 0:n], op=ALU.mult),
                 reads=["x_t", "rs"], writes=[f"t1_{b}"])
            S.op("act", lambda e, k=k, b=b: e.activation(out=h_bf[:, k, 0:n], in_=t1[b][:, 0:n], func=AF.Identity,
                                                         bias=mod[:, k, v:v + 1], scale=A1m[:, k, v:v + 1]),
                 reads=[f"t1_{b}", "A1m", "mod_sb"], writes=["h_bf"])
        if first:
            S.op("dve", lambda e: e.tensor_scalar(out=h_bf[:, :, 0:1], in0=h_bf[:, :, 0:1], scalar1=0.0, scalar2=None, op0=ALU.mult),
                 reads=["h_bf"], writes=["h_bf"])
        if last:
            S.op("dve", lambda e: e.tensor_scalar(out=h_bf[:, :, n - 1:n], in0=h_bf[:, :, n - 1:n], scalar1=0.0, scalar2=None, op0=ALU.mult),
                 reads=["h_bf"], writes=["h_bf"])
        dsts = [OUT[:, 0, 0, 0:no], PP[:, 1, 0:no], OUT[:, 0, 2, 0:no], PP[:, 3, 0:no], PP[:, 4, 0:no]]
        for c in range(5):
            ps = bank[c]
            for k in range(8):
                S.op("pe", lambda e, ps=ps, k=k, c=c: e.matmul(ps[:, 0:n], win_bf[:, k, c * 128:(c + 1) * 128], h_bf[:, k, 0:n],
                                                              start=(k == 0), stop=(k == 7)),
                     reads=[("w", "win_bf"), "h_bf"], writes=[f"bank{c}"])
            dst = dsts[c]
            S.op("act", lambda e, ps=ps, c=c, dst=dst: e.activation(out=dst, in_=ps[:, 1:1 + no], func=AF.Identity, scale=muc[:, c:c + 1]),
                 reads=[f"bank{c}", "muc"], writes=[("p", c)])
            S.op("dve", lambda e, ps=ps, c=c, dst=dst: e.scalar_tensor_tensor(out=dst, in0=ps[:, 0:no], scalar=small["mu"][:, c, 0:1], in1=dst,
                                                                              op0=ALU.mult, op1=ALU.add),
                 reads=[f"bank{c}", "mu", ("p", c)], writes=[("p", c)])
            S.op("dve", lambda e, ps=ps, c=c, dst=dst: e.scalar_tensor_tensor(out=dst, in0=ps[:, 2:2 + no], scalar=small["mu"][:, c, 1:2], in1=dst,
                                                                              op0=ALU.mult, op1=ALU.add),
                 reads=[f"bank{c}", "mu", ("p", c)], writes=[("p", c)])
        r_ = OUT[:, 0, 0, 0:no]
        k_ = PP[:, 1, 0:no]
        v_ = OUT[:, 0, 2, 0:no]
        kap = OUT[:, 0, 1, 0:no]
        T0, T1, T2 = t1[0][:, 0:no], t1[1][:, 0:no], t1[2][:, 0:no]
        S.op("dve", lambda e: e.tensor_scalar(out=T0, in0=k_, scalar1=pv[:, 0:1], scalar2=None, op0=ALU.mult),
             reads=[("p", 1), "pvec"], writes=["t1_0"])
        S.op("act", lambda e: e.activation(out=T1, in_=T0, func=AF.Square), reads=["t1_0"], writes=["t1_1"])
        S.op("pe", lambda e: e.matmul(bank[5][:, 0:no], small["bones"][:], T1, start=True, stop=True), reads=["bones", "t1_1"], writes=["bank5"])
        S.op("dve", lambda e: e.tensor_scalar(out=T1, in0=bank[5][:, 0:no], scalar1=1e-12, scalar2=None, op0=ALU.max),
             reads=["bank5", "t1_1"], writes=["t1_1"])
        S.op("act", lambda e: e.activation(out=T1, in_=T1, func=AF.Sqrt), reads=["t1_1"], writes=["t1_1"])
        S.op("dve", lambda e: e.reciprocal(out=T1, in_=T1), reads=["t1_1"], writes=["t1_1"])
        S.op("dve", lambda e: e.tensor_tensor(out=kap, in0=T0, in1=T1, op=ALU.mult), reads=["t1_0", "t1_1"], writes=["kap"])
        S.op("act", lambda e: e.activation(out=t1[2][0:64, 0:no], in_=PP[0:64, 3, 0:no], func=AF.Tanh), reads=[("p", 3)], writes=["t1_2"])
        for d in range(2):
            S.op("pe", lambda e, d=d: e.matmul(bank[5][:, 0:no], small["lor"][0:64, d, :], t1[2][0:64, 0:no], start=True, stop=True),
                 reads=["lor", "t1_2"], writes=["bank5"])
            S.op("act", lambda e, d=d: e.activation(out=OUT[:, d, 5, 0:no], in_=bank[5][:, 0:no], func=AF.Sigmoid, bias=pv[:, 3 + d:4 + d]),
                 reads=["bank5", "pvec"], writes=[("lw", d)])
            S.op("dve", lambda e, d=d: e.tensor_scalar(out=OUT[:, d, 5, 0:no], in0=OUT[:, d, 5, 0:no], scalar1=LWC, scalar2=None, op0=ALU.mult),
                 reads=[("lw", d)], writes=[("lw", d)])
            S.op("pe", lambda e, d=d: e.matmul(bank[6][:, 0:no], small["lor"][64:128, d, :], PP[64:128, 3, 0:no], start=True, stop=True),
                 reads=["lor", ("p", 3)], writes=["bank6"])
            S.op("act", lambda e, d=d: e.activation(out=AD[:, 0:no], in_=bank[6][:, 0:no], func=AF.Sigmoid, bias=pv[:, 5 + d:6 + d]),
                 reads=["bank6", "pvec"], writes=["AD"])
            S.op("dve", lambda e: e.tensor_scalar(out=T0, in0=AD[:, 0:no], scalar1=pv[:, 1:2], scalar2=omka[:, 0:1], op0=ALU.mult, op1=ALU.add),
                 reads=["AD", "pvec", "omka"], writes=["t1_0"])
            S.op("dve", lambda e, d=d: e.tensor_tensor(out=OUT[:, d, 3, 0:no], in0=k_, in1=T0, op=ALU.mult), reads=[("p", 1), "t1_0"], writes=[("kd", d)])
            S.op("dve", lambda e, d=d: e.tensor_tensor(out=OUT[:, d, 4, 0:no], in0=kap, in1=AD[:, 0:no], op=ALU.mult), reads=["kap", "AD"], writes=[("bd", d)])
        S.op("pool", lambda e: e.tensor_copy(out=OUT[:, 1, 0:3, 0:no], in_=OUT[:, 0, 0:3, 0:no]), reads=[("p", 0), ("p", 2), "kap"], writes=["out1c"])
        S.op("dve", lambda e: e.tensor_tensor(out=T0, in0=OUT[:, 0, 3, 0:no], in1=OUT[:, 1, 3, 0:no], op=ALU.add), reads=[("kd", 0), ("kd", 1)], writes=["t1_0"])
        S.op("dve", lambda e: e.scalar_tensor_tensor(out=T0, in0=T0, scalar=pv[:, 2:3], in1=r_, op0=ALU.mult, op1=ALU.mult),
             reads=["t1_0", "pvec", ("p", 0)], writes=["t1_0"])
        S.op("pe", lambda e: e.matmul(bank[5][:, 0:no], small["bones"][:], T0, start=True, stop=True), reads=["bones", "t1_0"], writes=["bank5"])
        S.op("dve", lambda e: e.tensor_tensor(out=BNO[:, 0, 0:no], in0=bank[5][:, 0:no], in1=v_, op=ALU.mult), reads=["bank5", ("p", 2)], writes=["bno0"])
        S.op("act", lambda e: e.activation(out=BNO[:, 1, 0:no], in_=PP[:, 4, 0:no], func=AF.Sigmoid), reads=[("p", 4)], writes=["bno1"])
        allk = [("p", 0), ("p", 2), "kap", ("lw", 0), ("lw", 1), ("kd", 0), ("kd", 1), ("bd", 0), ("bd", 1), "out1c"]
        S.op("sp", lambda e: e.dma_start(out=SCR[0][:, :, tok0:tok0 + no], in_=OUT[:, 0, :, 0:no]), reads=allk, dma=True)
        S.op("sp", lambda e: e.dma_start(out=SCR[1][:, :, tok0:tok0 + no], in_=OUT[:, 1, :, 0:no]), reads=allk, dma=True)
        S.op("sp", lambda e: e.dma_start(out=BN[:, :, tok0:tok0 + no], in_=BNO[:, :, 0:no]), reads=["bno0", "bno1"], dma=True)

    def mm(out, lhsT, rhs, start, stop, reads, wkey):
        S.op("pe", lambda e: e.matmul(out, lhsT, rhs, start=start, stop=stop), reads=reads, writes=[wkey])

    def scan_step(step):
        cF = step
        cB = (1 - step) if step < 2 else (NCHK - 1 - (step - 2))
        tF, tB = cF * 128, cB * 128
        par = step % 2
        IN = INs[par]
        S.op("sp", lambda e: e.dma_start(out=IN[:, 0], in_=SCR[0][:, :, tF:tF + 128]), writes=[f"IN0_{par}"], dma=True)
        S.op("sp", lambda e: e.dma_start(out=IN[:, 1], in_=SCR[1][:, :, tB:tB + 128]), writes=[f"IN1_{par}"], dma=True)
        INk = [f"IN0_{par}", f"IN1_{par}"]
        for d in range(2):
            S.op("dve", lambda e, d=d: e.tensor_tensor_scan(out=CL[:, d, :], data0=ones_f[:], data1=IN[:, d, 5, :], initial=0.0,
                                                            op0=ALU.mult, op1=ALU.add), reads=[INk[d], "ones_f"], writes=[("CL", d)])
        S.op("dve", lambda e: e.tensor_copy(out=TOT[:], in_=CL[:, :, 127]), reads=[("CL", 0), ("CL", 1)], writes=["TOT"])
        S.op("dve", lambda e: e.tensor_scalar(out=CL[:, 1, :], in0=CL[:, 1, :], scalar1=-1.0, scalar2=TOT[:, 1:2], op0=ALU.mult, op1=ALU.add),
             reads=[("CL", 1), "TOT"], writes=[("CL", 1)])
        S.op("dve", lambda e: e.tensor_tensor(out=CL[:, 1, :], in0=CL[:, 1, :], in1=IN[:, 1, 5, :], op=ALU.add), reads=[("CL", 1), INk[1]], writes=[("CL", 1)])
        CLk = [("CL", 0), ("CL", 1)]
        S.op("dve", lambda e: e.tensor_tensor(out=CLX[:], in0=CL[:], in1=IN[:, :, 5, :], op=ALU.subtract), reads=CLk + INk, writes=["CLX"])
        S.op("act", lambda e: e.activation(out=E[0][:], in_=CL[:], func=AF.Exp), reads=CLk, writes=["E0"])
        S.op("act", lambda e: e.activation(out=E[1][:], in_=CLX[:], func=AF.Exp), reads=["CLX"], writes=["E1"])
        S.op("act", lambda e: e.activation(out=E[2][:], in_=CL[:], func=AF.Exp, scale=-1.0), reads=CLk, writes=["E2"])
        for d in range(2):
            S.op("act", lambda e, d=d: e.activation(out=E[3][:, d, :], in_=CL[:, d, :], func=AF.Exp, scale=-1.0, bias=TOT[:, d:d + 1]),
                 reads=CLk + ["TOT"], writes=[("E3", d)])
        S.op("act", lambda e: e.activation(out=EG[:], in_=TOT[:], func=AF.Exp), reads=["TOT"], writes=["EG"])
        E3k = [("E3", 0), ("E3", 1)]
        for h in range(2):
            hs = slice(h * 64, (h + 1) * 64)
            S.op("dve" if h == 0 else "pool", lambda e, h=h, hs=hs: e.tensor_tensor(out=RKp[h][hs, :, 0, :], in0=IN[hs, :, 1, :], in1=E[1][hs], op=ALU.mult),
                 reads=INk + ["E1"], writes=[f"RK{h}0"])
            S.op("pool" if h == 0 else "dve", lambda e, h=h, hs=hs: e.tensor_tensor(out=RKp[h][hs, :, 1, :], in0=IN[hs, :, 0, :], in1=E[0][hs], op=ALU.mult),
                 reads=INk + ["E0"], writes=[f"RK{h}1"])
        S.op("dve", lambda e: e.tensor_tensor(out=KB[:, :, 0, :], in0=IN[:, :, 3, :], in1=E[2][:], op=ALU.mult), reads=INk + ["E2"], writes=["KB0"])
        S.op("pool", lambda e: e.tensor_tensor(out=KB[:, :, 1, :], in0=IN[:, :, 4, :], in1=E[2][:], op=ALU.mult), reads=INk + ["E2"], writes=["KB1"])
        S.op("pool", lambda e: e.tensor_tensor(out=HAT[:, :, 0, :], in0=IN[:, :, 3, :], in1=E[3][:], op=ALU.mult), reads=INk + E3k, writes=["HAT0"])
        S.op("dve", lambda e: e.scalar_tensor_tensor(out=HAT[:, :, 1, :], in0=IN[:, :, 4, :], scalar=-1.0, in1=E[3][:], op0=ALU.mult, op1=ALU.mult),
             reads=INk + E3k, writes=["HAT1"])
        RKk, KBk, HATk = ["RK00", "RK01", "RK10", "RK11"], ["KB0", "KB1"], ["HAT0", "HAT1"]
        for d in range(2):
            mm(bank[0][:, 256 + d * 128:256 + (d + 1) * 128], IN[:, d, 2, :], small["ident"][:], True, True, [INk[d], "ident"], "bank0")
        S.op("act", lambda e: e.copy(out=VT[:], in_=bank[0][:, 256:512].rearrange("p (d q) -> p d q", d=2)), reads=["bank0"], writes=["VT"])
        for d in range(2):
            for m_ in range(2):
                o = (d * 2 + m_) * 128
                mm(bank[4][:, o:o + 128], HAT[:, d, m_, :], small["ident"][:], True, True, HATk + ["ident"], "bank4")
        S.op("act", lambda e: e.copy(out=HT[:], in_=bank[4][:, 0:512].rearrange("p (d m q) -> p d m q", d=2, m=2)), reads=["bank4"], writes=["HT"])
        for h in range(2):
            hs = slice(h * 64, (h + 1) * 64)
            for d in range(2):
                rk2 = RKp[h][:, d, :, :].rearrange("p m q -> p (m q)")
                mm(bank[1][:, d * 256:(d + 1) * 256] if h == 0 else bank[5][:, d * 256:(d + 1) * 256],
                   KB[:, d, 0, :], rk2, True, True, KBk + RKk, "bank1" if h == 0 else "bank5")
                mm(bank[2][:, d * 256:(d + 1) * 256] if h == 0 else bank[3][:, d * 256:(d + 1) * 256],
                   KB[:, d, 1, :], rk2, True, True, KBk + RKk, "bank2" if h == 0 else "bank3")
                mm(bank[6][:, (h * 2 + d) * 128:(h * 2 + d + 1) * 128], RKp[h][:, d, 0, :], KB[:, d, 1, :], True, True, KBk + RKk, "bank6")
        g1b = [bank[1], bank[5]]
        g2b = [bank[2], bank[3]]
        mk = small["masks"]
        for h in range(2):
            S.op("dve", lambda e, h=h: e.tensor_tensor(out=A1[:, h].rearrange("p d m q -> p (d m) q"),
                                                       in0=g1b[h][:, 0:512].rearrange("p (a q) -> p a q", a=4),
                                                       in1=mk[:].rearrange("p d m q -> p (d m) q"), op=ALU.mult),
                 reads=[("bank1", "bank5")[h], "masks"], writes=[("A1", h)])
            S.op("dve", lambda e, h=h: e.scalar_tensor_tensor(out=A2[:, h].rearrange("p d m q -> p (d m) q"),
                                                              in0=g2b[h][:, 0:512].rearrange("p (a q) -> p a q", a=4), scalar=-1.0,
                                                              in1=mk[:].rearrange("p d m q -> p (d m) q"), op0=ALU.mult, op1=ALU.mult),
                 reads=[("bank2", "bank3")[h], "masks"], writes=[("A2", h)])
            S.op("dve", lambda e, h=h: e.scalar_tensor_tensor(out=XK[0][:, h * 2:(h + 1) * 2, :],
                                                              in0=bank[6][:, h * 256:(h + 1) * 256].rearrange("p (d q) -> p d q", d=2), scalar=-1.0,
                                                              in1=small["mask3"][:], op0=ALU.mult, op1=ALU.mult),
                 reads=["bank6", "mask3"], writes=["XK0"])
            S.op("pool", lambda e, h=h: e.tensor_copy(out=YK[0][:, h * 2:(h + 1) * 2, :], in_=A2[:, h, :, 0, :]), reads=[("A2", h)], writes=["YK0"])
            S.op("pool", lambda e, h=h: e.tensor_tensor(out=QT[:, h * 2:(h + 1) * 2, :], in0=A2[:, h, :, 0, :],
                                                        in1=small["ident"][:].unsqueeze(1).to_broadcast([128, 2, 128]), op=ALU.add),
                 reads=[("A2", h), "ident"], writes=["QT"])
        for lv in range(1, 7):
            xp, yp = XK[(lv - 1) % 2], YK[(lv - 1) % 2]
            xn, yn = XK[lv % 2], YK[lv % 2]
            xpk, ypk, xnk, ynk = f"XK{(lv - 1) % 2}", f"YK{(lv - 1) % 2}", f"XK{lv % 2}", f"YK{lv % 2}"
            for c in range(4):
                mm(bank[5][:, c * 128:(c + 1) * 128], yp[:, c, :], xp[:, c, :], True, True, [xpk, ypk], "bank5")
            S.op("act", lambda e, xn=xn: e.copy(out=xn[:], in_=bank[5][:, 0:512].rearrange("p (c q) -> p c q", c=4)), reads=["bank5"], writes=[xnk])
            if lv < 6:
                for c in range(4):
                    mm(bank[6][:, c * 128:(c + 1) * 128], xp[:, c, :], yp[:, c, :], True, True, [xpk, ypk], "bank6")
                S.op("dve", lambda e, yn=yn: e.tensor_copy(out=yn[:], in_=bank[6][:, 0:512].rearrange("p (c q) -> p c q", c=4)), reads=["bank6"], writes=[ynk])
            for c in range(4):
                mm(bank[7][:, c * 128:(c + 1) * 128], xn[:, c, :], QT[:, c, :], True, True, [xnk, "QT"], "bank7")
            S.op("dve", lambda e: e.tensor_tensor(out=QT[:], in0=QT[:], in1=bank[7][:, 0:512].rearrange("p (c q) -> p c q", c=4), op=ALU.add),
                 reads=["QT", "bank7"], writes=["QT"])
        for h in range(2):
            hs = slice(h * 64, (h + 1) * 64)
            for d in range(2):
                c = h * 2 + d
                mm(bank[0][:, c * 64:(c + 1) * 64], RKp[h][:, d, 0, :], MST[:, d, :], True, False, RKk + ["MST"], "bank0")
                mm(bank[0][:, c * 64:(c + 1) * 64], A1[:, h, d, 0, :], VT[:, d, hs], False, True, [("A1", h), "VT"], "bank0")
        S.op("act", lambda e: e.copy(out=Zs[:], in_=bank[0][:, 0:256]), reads=["bank0"], writes=["Zs"])
        for c in range(4):
            mm(bank[1][:, c * 64:(c + 1) * 64], QT[:, c, :], Zs[:, c * 64:(c + 1) * 64], True, True, ["QT", "Zs"], "bank1")
        S.op("dve", lambda e: e.tensor_copy(out=Us[:], in_=bank[1][:, 0:256]), reads=["bank1"], writes=["Us"])
        for h in range(2):
            hs = slice(h * 64, (h + 1) * 64)
            for d in range(2):
                c = h * 2 + d
                mm(bank[2][:, c * 64:(c + 1) * 64], RKp[h][:, d, 1, :], MST[:, d, :], True, False, RKk + ["MST"], "bank2")
                mm(bank[2][:, c * 64:(c + 1) * 64], A1[:, h, d, 1, :], VT[:, d, hs], False, False, [("A1", h), "VT"], "bank2")
                mm(bank[2][:, c * 64:(c + 1) * 64], A2[:, h, d, 1, :], Us[:, c * 64:(c + 1) * 64], False, True, [("A2", h), "Us"], "bank2")
        for h in range(2):
            hs = slice(h * 64, (h + 1) * 64)
            for d in range(2):
                c = h * 2 + d
                mm(bank[3][:, c * 64:(c + 1) * 64], HT[:, d, 0, :], VT[:, d, hs], True, False, ["HT", "VT"], "bank3")
                mm(bank[3][:, c * 64:(c + 1) * 64], HT[:, d, 1, :], Us[:, c * 64:(c + 1) * 64], False, True, ["HT", "Us"], "bank3")
        S.op("act", lambda e: e.copy(out=Ys[:], in_=bank[2][:, 0:256]), reads=["bank2"], writes=["Ys"])
        S.op("dve", lambda e: e.tensor_tensor(out=MST[:], in0=MST[:], in1=EG[:].unsqueeze(2).to_broadcast([128, 2, 64]), op=ALU.mult),
             reads=["MST", "EG"], writes=["MST"])
        for h in range(2):
            hs = slice(h * 64, (h + 1) * 64)
            S.op("dve", lambda e, h=h, hs=hs: e.tensor_tensor(out=MST[hs], in0=MST[hs],
                                                             in1=bank[3][hs, h * 128:(h + 1) * 128].rearrange("p (d q) -> p d q", d=2), op=ALU.add),
                 reads=["MST", "bank3"], writes=["MST"])
        Yv = Ys[:].rearrange("p (h d q) -> p h d q", h=2, d=2)
        S.op("sp", lambda e: e.dma_start(out=YS[tF:tF + 128, 0, :].rearrange("t (h q) -> t h q", h=2), in_=Yv[:, :, 0, :]), reads=["Ys"], writes=["ysf"], dma=True)
        S.op("sp", lambda e: e.dma_start(out=YS[tB:tB + 128, 1, :].rearrange("t (h q) -> t h q", h=2), in_=Yv[:, :, 1, :]), reads=["Ys"], writes=["ysb"], dma=True)

    def pass3_chunk(c):
        t0 = c * 128
        S.op("sp", lambda e: e.dma_start(out=Y3[:], in_=YS[t0:t0 + 128, :, :]), writes=["Y3"], dma=True)
        S.op("sp", lambda e: e.dma_start(out=B3[:], in_=BN[:, :, t0:t0 + 128]), writes=["B3"], dma=True)
        mm(bank[0][:, 0:128], B3[:, 0, :], small["ident"][:], True, True, ["B3", "ident"], "bank0")
        mm(bank[0][:, 128:256], B3[:, 1, :], small["gup"][:], True, True, ["B3", "gup"], "bank0")
        S.op("dve", lambda e: e.tensor_tensor(out=y3[:], in0=Y3[:, 0, :], in1=Y3[:, 1, :], op=ALU.add), reads=["Y3"], writes=["y3"])
        S.op("dve", lambda e: e.reduce_sum(out=st3[:, 0:2], in_=y3[:].rearrange("p (h q) -> p h q", h=2), axis=AX.X), reads=["y3"], writes=["st3a"])
        S.op("act", lambda e: e.activation(out=y3b[:], in_=y3[:], func=AF.Square), reads=["y3"], writes=["y3b"])
        S.op("dve", lambda e: e.reduce_sum(out=st3[:, 2:4], in_=y3b[:].rearrange("p (h q) -> p h q", h=2), axis=AX.X), reads=["y3b"], writes=["st3b"])
        S.op("dve", lambda e: e.tensor_scalar(out=st3[:, 0:4], in0=st3[:, 0:4], scalar1=1.0 / 64, scalar2=None, op0=ALU.mult),
             reads=["st3a", "st3b"], writes=["st3c"])
        S.op("dve", lambda e: e.tensor_tensor(out=st3[:, 4:6], in0=st3[:, 0:2], in1=st3[:, 0:2], op=ALU.mult), reads=["st3c"], writes=["st3d"])
        S.op("dve", lambda e: e.tensor_tensor(out=st3[:, 4:6], in0=st3[:, 2:4], in1=st3[:, 4:6], op=ALU.subtract), reads=["st3c", "st3d"], writes=["st3d"])
        S.op("act", lambda e: e.activation(out=st3[:, 6:8], in_=st3[:, 4:6], func=AF.Sqrt, bias=epsg[:, 0:1]), reads=["st3d", "epsg"], writes=["st3e"])
        S.op("dve", lambda e: e.reciprocal(out=st3[:, 6:8], in_=st3[:, 6:8]), reads=["st3e"], writes=["st3e"])
        y3v = y3[:].rearrange("p (h q) -> p h q", h=2)
        S.op("dve", lambda e: e.tensor_tensor(out=y3v, in0=y3v, in1=st3[:, 0:2].unsqueeze(2).to_broadcast([128, 2, 64]), op=ALU.subtract),
             reads=["y3", "st3c", "y3b"], writes=["y3"])
        S.op("dve", lambda e: e.tensor_tensor(out=y3v, in0=y3v, in1=st3[:, 6:8].unsqueeze(2).to_broadcast([128, 2, 64]), op=ALU.mult),
             reads=["y3", "st3e"], writes=["y3"])
        S.op("dve", lambda e: e.tensor_tensor(out=y3[:], in0=y3[:], in1=small["lnwb"][:, 0, :], op=ALU.mult), reads=["y3", "lnwb"], writes=["y3"])
        S.op("dve", lambda e: e.tensor_tensor(out=y3[:], in0=y3[:], in1=small["lnwb"][:, 1, :], op=ALU.add), reads=["y3", "lnwb"], writes=["y3"])
        S.op("dve", lambda e: e.tensor_tensor(out=y3[:], in0=y3[:], in1=bank[0][:, 0:128], op=ALU.add), reads=["y3", "bank0"], writes=["y3"])
        S.op("dve", lambda e: e.tensor_tensor(out=y3b[:], in0=y3[:], in1=bank[0][:, 128:256], op=ALU.mult), reads=["y3", "bank0", "y3b"], writes=["y3b"])
        tok = S.op("sp", lambda e: e.dma_start(out=rw_out[t0:t0 + 128, :], in_=y3b[:]), reads=["y3b"], dma=True)
        S.final_tokens.append(tok)

    with nc.allow_low_precision("bf16 matmul operands for the input projection, fp32 elsewhere"):
        pass1_tile(cT, 0, CTXL, 0, 1, True, True)
        nt = (SEQ + RW_TILE - 1) // RW_TILE
        for ti in range(nt):
            o0 = ti * RW_TILE
            no = min(RW_TILE, SEQ - o0)
            pass1_tile(xT, o0, no, CTXL + o0, 0, ti == 0, ti == nt - 1)
        S.barrier()
        if debug == "pass1":
            dbg = nc.dram_tensor("dbg", [128, 14, 512], F32, kind="ExternalOutput").ap()
            toks = [S.op("sp", lambda e: e.dma_start(out=dbg[:, 0:6, :], in_=SCR[0][:, :, 0:512]), dma=True),
                    S.op("sp", lambda e: e.dma_start(out=dbg[:, 6:12, :], in_=SCR[1][:, :, 0:512]), dma=True),
                    S.op("sp", lambda e: e.dma_start(out=dbg[:, 12:14, :], in_=BN[:, :, 0:512]), dma=True)]
            S.emit(toks)
            return nc
        S.op("dve", lambda e: e.tensor_scalar(out=MST[:], in0=small["ident"][:, 0:128].rearrange("p (d q) -> p d q", d=2), scalar1=0.0,
                                              scalar2=None, op0=ALU.mult), reads=["ident", "MST"], writes=["MST"])
        for step in range(nsteps):
            scan_step(step)
        S.barrier()
        for c in range(NCHK):
            pass3_chunk(c)
        S.emit(S.final_tokens)
    return nc


def pad1(a):
    return np.concatenate([np.zeros((1, a.shape[1]), a.dtype), a, np.zeros((1, a.shape[1]), a.dtype)], 0)


def rwkv_inputs(core, z, layer=0):
    b, g = divmod(core, 4)
    w = z["ev_w_in"][0]
    own = g * 128 + np.arange(128)
    rc = np.concatenate([own, 512 + own, 1024 + own, 1536 + np.arange(128), 1536 + 128 + np.arange(128)])
    cols = 1536 + rc
    mu = z["rwkv_mu"][0][:, rc]
    flat = lambda a: a.reshape(-1)
    pvec = np.stack([z["rwkv_k_k"][0][own], z["rwkv_k_a"][0][own], flat(z["rwkv_r_k"][0])[own],
                     z["rwkv_w0"][0][0][own], z["rwkv_w0"][0][1][own], z["rwkv_a0"][0][0][own], z["rwkv_a0"][0][1][own],
                     np.zeros(128, np.float32)], 1).astype(np.float32)
    lor = np.zeros((128, 2, 128), np.float32)
    for d in range(2):
        lor[0:64, d, :] = z["rwkv_w_up"][0][d][:, own]
        lor[64:128, d, :] = z["rwkv_a_up"][0][d][:, own]
    idx = np.arange(128)
    su = (idx[:, None] < idx[None, :]).astype(np.float32)
    iu = (idx[:, None] <= idx[None, :]).astype(np.float32)
    sl = (idx[:, None] > idx[None, :]).astype(np.float32)
    il = (idx[:, None] >= idx[None, :]).astype(np.float32)
    masks = np.stack([np.stack([su, iu], 1), np.stack([sl, il], 1)], 1)
    mask3 = np.stack([sl, su], 1)
    hh = idx // 64
    rep = lambda v: np.ascontiguousarray(np.broadcast_to(v.astype(np.float32), (128,) + v.shape))
    return {
        "xT": fm(pad1(z["x"][b])), "cT": fm(pad1(z["ctx"][b])),
        "w_in": kch(np.ascontiguousarray(w[:, cols])),
        "ada_w": kch(z["ada_w"][layer]), "ada_b": vec_fm(z["ada_b"][layer]),
        "c2": np.ascontiguousarray(np.stack([vec_fm(z["c"][b]), vec_fm(z["c_ctx"])], axis=-1)),
        "nrm": vec_fm(z["norm_mix"][layer]),
        "mu": np.ascontiguousarray(mu.T.reshape(5, 128, 2).transpose(1, 0, 2)),
        "pvec": pvec, "lor": lor,
        "gup": np.ascontiguousarray(z["rwkv_g_up"][0][:, own]),
        "lnwb": np.ascontiguousarray(np.stack([rep(z["rwkv_ln_w"][0][own]), rep(z["rwkv_ln_b"][0][own])], 1)),
        "ident": np.eye(128, dtype=np.float32),
        "bones": (hh[:, None] == hh[None, :]).astype(np.float32),
        "masks": np.ascontiguousarray(masks.astype(np.float32)), "mask3": np.ascontiguousarray(mask3.astype(np.float32)),
    }


def kernel(**z):
    z = {k: np.asarray(v, dtype=np.float32) for k, v in z.items()}
    B, T = 2, SEQ
    cores = list(range(8))
    ncA = build_even()
    resA = run_bass_kernel_spmd(ncA, [even_inputs(c, z) for c in cores], core_ids=cores)
    mix0 = np.zeros((B, T, 1024), np.float32)
    mixc0 = np.zeros((B, CTXL, 1024), np.float32)
    for c in cores:
        b, g = divmod(c, 4)
        mix0[b, :, g * 128:(g + 1) * 128] = resA.results[c]["attT"].T
        mixc0[b, :, g * 128:(g + 1) * 128] = resA.results[c]["attcT"].T
    del resA
    ncR = build_rwkv()
    resR = run_bass_kernel_spmd(ncR, [rwkv_inputs(c, z) for c in cores], core_ids=cores)
    for c in cores:
        b, g = divmod(c, 4)
        rw = resR.results[c]["rw_out"]
        mixc0[b, :, 512 + g * 128:512 + (g + 1) * 128] = rw[:CTXL]
        mix0[b, :, 512 + g * 128:512 + (g + 1) * 128] = rw[CTXL:]
    del resR
    ncB = build_ffn(8, True, False)
    mapsB = [ffn_inputs(c, z["x"], mix0, z["ctx"], mixc0, z["c"], z["c_ctx"], z["ada_w"][0], z["ada_b"][0],
                        z["norm_ffn"][0], z["ev_w_out"][0], z["ffn_w_up"][0], z["ffn_conv_w"][0], z["ffn_conv_b"][0],
                        z["ffn_w_down"][0], True, None) for c in cores]
    resB = run_bass_kernel_spmd(ncB, mapsB, core_ids=cores)
    x1 = np.zeros((B, T, 1024), np.float32)
    xc1 = np.zeros((B, CTXL, 1024), np.float32)
    for c in cores:
        b, q = divmod(c, 4)
        x1[b, q * TL:(q + 1) * TL] = fm_inv(resB.results[c]["xoT"])
        if q == 0:
            xc1[b] = fm_inv(resB.results[c]["xcoT"])
    del resB, mapsB
    ncC = build_mamba()
    resC = run_bass_kernel_spmd(ncC, [mamba_inputs(c, z, x1, xc1) for c in cores], core_ids=cores)
    mix1 = np.zeros((B, T, 2048), np.float32)
    for c in cores:
        b, g = divmod(c, 4)
        mix1[b, :, g * 512:(g + 1) * 512] = resC.results[c]["gy_out"]
    del resC
    ncD = build_ffn(16, False, True)
    mapsD = [ffn_inputs(c, x1, mix1, None, None, z["c"], z["c_ctx"], z["ada_w"][1], z["ada_b"][1],
                        z["norm_ffn"][1], z["ssm_w_out"][0], z["ffn_w_up"][1], z["ffn_conv_w"][1], z["ffn_conv_b"][1],
                        z["ffn_w_down"][1], False, z["final_norm"]) for c in cores]
    resD = run_bass_kernel_spmd(ncD, mapsD, core_ids=cores)
    out = np.zeros((B, T, 1024), np.float32)
    for c in cores:
        b, q = divmod(c, 4)
        out[b, q * TL:(q + 1) * TL] = fm_inv(resD.results[c]["xoT"])
    return out
```

```python
import numpy as np
import concourse.bass as bass
import concourse.mybir as mybir
from concourse.bass_utils import run_bass_kernel_spmd

F32 = mybir.dt.float32
BF16 = mybir.dt.bfloat16
AF = mybir.ActivationFunctionType
ALU = mybir.AluOpType
AX = mybir.AxisListType

EPOCH = 30000
NDMASEM = 6


class Sched:
    ENGS = ("pe", "act", "dve", "pool", "sp")

    def __init__(self, nc, same_engine_sync=True):
        self.nc = nc
        self.ops = {e: [] for e in self.ENGS}
        self.cnt = {e: 0 for e in self.ENGS}
        self.sems = {}
        self.dma_n = {e: 0 for e in self.ENGS}
        self.dma_last = {}
        self.waited = {e: {} for e in self.ENGS}
        self.last_w = {}
        self.readers = {}
        self.same = same_engine_sync
        self.final_tokens = []
        self.last_tok = {}

    def sem(self, name):
        if name not in self.sems:
            self.sems[name] = self.nc.alloc_semaphore(name) if hasattr(self.nc, "alloc_semaphore") else None
        return self.sems[name]

    def _need(self, eng, tok, waits):
        if tok is None:
            return
        name, val, src = tok
        if src == eng and (eng == "pe" or not self.same) and not name.startswith("dma"):
            return
        if self.waited[eng].get(name, 0) >= val:
            return
        waits[name] = max(waits.get(name, 0), val)

    def barrier(self):
        toks = list(self.last_tok.values()) + list(self.dma_last.values())
        for e in self.ENGS:
            self.op(e, lambda en: en.nop(), extra=toks)

    def op(self, eng, fn, reads=(), writes=(), dma=False, extra=()):
        waits = {}
        for tok in extra:
            self._need(eng, tok, waits)
        for b in reads:
            self._need(eng, self.last_w.get(b), waits)
        for b in writes:
            self._need(eng, self.last_w.get(b), waits)
            for tok in self.readers.get(b, {}).values():
                self._need(eng, tok, waits)
        if dma:
            i = self.dma_n[eng]
            self.dma_n[eng] += 1
            name = f"dma_{eng}_{i % NDMASEM}"
            val = 16 * (i // NDMASEM + 1)
            prev = self.dma_last.get(name)
            if prev is not None:
                self._need(eng, prev, waits)
            tok = (name, val, eng)
            self.dma_last[name] = tok
            inc = 16
        else:
            self.cnt[eng] += 1
            ep, v = divmod(self.cnt[eng] - 1, EPOCH)
            name = f"c_{eng}_{ep}"
            val = v + 1
            tok = (name, val, eng)
            self.last_tok[eng] = tok
            inc = 1
        for n, v in waits.items():
            self.waited[eng][n] = v
        self.ops[eng].append((list(waits.items()), fn, name, inc))
        for b in reads:
            self.readers.setdefault(b, {})[eng + ("d%d" % self.dma_n[eng] if dma else "")] = tok
        for b in writes:
            self.last_w[b] = tok
            self.readers[b] = {}
        return tok

    def emit(self, final_wait_tokens=()):
        nc = self.nc
        names = set()
        for e in self.ENGS:
            for waits, fn, name, inc in self.ops[e]:
                names.add(name)
                for n, _ in waits:
                    names.add(n)
        import contextlib
        with contextlib.ExitStack() as st:
            semh = {n: st.enter_context(nc.semaphore(n)) for n in sorted(names)}
            block = st.enter_context(nc.Block())

            def body(eng_name):
                def f(engine):
                    for waits, fn, name, inc in self.ops[eng_name]:
                        for n, v in waits:
                            engine.wait_ge(semh[n], v)
                        fn(engine).then_inc(semh[name], inc)
                    if eng_name == "sp":
                        for (n, v, _) in final_wait_tokens:
                            engine.wait_ge(semh[n], v)
                return f

            block.tensor(body("pe"))
            block.scalar(body("act"))
            block.vector(body("dve"))
            block.gpsimd(body("pool"))
            block.sync(body("sp"))


def new_nc():
    return bass.Bass("TRN2", target_bir_lowering=False)

import math

D = 1024
FH = 2816
NCH = 22
TL = 4096
CTXL = 256
WOUT = 510


def cast_weights(nc, S, dram, dst, nk, ncols, stages, piece):
    i = 0
    for k in range(nk):
        for c0 in range(0, ncols, piece):
            c1 = min(ncols, c0 + piece)
            st, key = stages[i % len(stages)]
            S.op("sp", lambda e, st=st, k=k, c0=c0, c1=c1: e.dma_start(out=st[:, 0:c1 - c0], in_=dram[:, k, c0:c1]),
                 writes=[key], dma=True)
            eng = "pool" if i % 2 == 0 else "act"
            if eng == "pool":
                S.op("pool", lambda e, st=st, k=k, c0=c0, c1=c1: e.tensor_copy(out=dst[:, k, c0:c1], in_=st[:, 0:c1 - c0]),
                     reads=[key], writes=[("w", dst.name)])
            else:
                S.op("act", lambda e, st=st, k=k, c0=c0, c1=c1: e.copy(out=dst[:, k, c0:c1], in_=st[:, 0:c1 - c0]),
                     reads=[key], writes=[("w", dst.name)])
            i += 1


def mod_alloc(nc, ns):
    return (nc.alloc_sbuf_tensor("c_sb", [128, 8, 2], F32), nc.alloc_sbuf_tensor("sc_sb", [128, 8, 2], F32),
            nc.alloc_sbuf_tensor("ab_sb", [128, 48], F32), nc.alloc_sbuf_tensor("mod_sb", [128, ns * 8, 2], F32))


def mod_vectors(nc, S, ada_w, ada_b, c2, sections, modps, stage, stage_key, tiles):
    ns = len(sections)
    c_sb, sc_sb, ab_sb, mod = tiles
    S.op("sp", lambda e: e.dma_start(out=c_sb[:], in_=c2[:, :, :]), writes=["c_sb"], dma=True)
    S.op("sp", lambda e: e.dma_start(out=ab_sb[:], in_=ada_b[:, :]), writes=["ab_sb"], dma=True)
    S.op("act", lambda e: e.activation(out=sc_sb[:], in_=c_sb[:], func=AF.Silu), reads=["c_sb"], writes=["sc_sb"])
    for si, s in enumerate(sections):
        for half in range(2):
            n0 = s * 1024 + half * 512
            S.op("sp", lambda e, n0=n0: e.dma_start(out=stage[:, 0:4096].rearrange("p (k n) -> p k n", k=8),
                                                    in_=ada_w[:, :, n0:n0 + 512]),
                 writes=[stage_key], dma=True)
            for nn in range(4):
                col = (si * 8 + half * 4 + nn) * 2
                for k in range(8):
                    S.op("pe", lambda e, nn=nn, k=k, col=col: e.matmul(
                        modps[:, col:col + 2], stage[:, k * 512 + nn * 128:k * 512 + (nn + 1) * 128], sc_sb[:, k, :],
                        start=(k == 0), stop=(k == 7)),
                        reads=[stage_key, "sc_sb"], writes=["modps"])
    for si, s in enumerate(sections):
        S.op("dve", lambda e, si=si, s=s: e.tensor_tensor(
            out=mod[:, si * 8:(si + 1) * 8, :],
            in0=modps[:, si * 16:(si + 1) * 16].rearrange("p (j v) -> p j v", v=2),
            in1=ab_sb[:, s * 8:(s + 1) * 8].unsqueeze(2).to_broadcast([128, 8, 2]), op=ALU.add),
            reads=["modps", "ab_sb"], writes=["mod_sb"])
    return mod


def build_ffn(KM, ctx_out, final, debug=None):
    nc = new_nc()
    S = Sched(nc)
    TP = TL + 2
    CP = CTXL + 2

    def din(name, shape):
        return nc.dram_tensor(name, shape, F32, kind="ExternalInput").ap()

    xT = din("xT", [128, 8, TP])
    mixT = din("mixT", [128, KM, TP])
    w_out = din("w_out", [128, KM, D])
    w_up = din("w_up", [128, 8, 2 * FH])
    w_dn = din("w_dn", [128, NCH, D])
    ada_w = din("ada_w", [128, 8, 6 * D])
    ada_b = din("ada_b", [128, 48])
    c2 = din("c2", [128, 8, 2])
    nrm = din("nrm", [128, 8])
    cw = din("cw", [128, NCH, 3])
    cb = din("cb", [128, NCH])
    emask = din("emask", [128, 2])
    if ctx_out:
        xcT = din("xcT", [128, 8, CP])
        mixcT = din("mixcT", [128, KM, CP])
        xcoT = nc.dram_tensor("xcoT", [128, 8, CTXL], F32, kind="ExternalOutput").ap()
    if final:
        fnw = din("fnw", [128, 8])
    xoT = nc.dram_tensor("xoT", [128, 8, TL], F32, kind="ExternalOutput").ap()
    xmid = nc.dram_tensor("xmid", [128, 8, TP], F32, kind="Internal").ap()
    xmidc = nc.dram_tensor("xmidc", [128, 8, CP], F32, kind="Internal").ap()

    ps_acc = [nc.alloc_psum_tensor(f"ps_acc{i}", [128, 512], F32) for i in range(2)]
    ps_ss = nc.alloc_psum_tensor("ps_ss", [128, 512], F32)
    ps_val = [nc.alloc_psum_tensor(f"ps_val{i}", [128, 512], F32) for i in range(2)]
    ps_gate = [nc.alloc_psum_tensor(f"ps_gate{i}", [128, 512], F32) for i in range(2)]
    modps = nc.alloc_psum_tensor("modps", [128, 512], F32)

    ones_bf = nc.alloc_sbuf_tensor("ones_bf", [128, 128], BF16)
    S.op("pool", lambda e: e.memset(ones_bf[:], 1.0), writes=["ones_bf"])
    small = {}
    for name, src, shape in (("nrm", nrm, [128, 8]), ("cw", cw, [128, NCH, 3]), ("cb", cb, [128, NCH]),
                             ("emask", emask, [128, 2])) + ((("fnw", fnw, [128, 8]),) if final else ()):
        t = nc.alloc_sbuf_tensor(name + "_sb", shape, F32)
        S.op("sp", lambda e, t=t, src=src: e.dma_start(out=t[:], in_=src), writes=[name], dma=True)
        small[name] = t

    import contextlib
    mtiles = mod_alloc(nc, 4)
    A2 = nc.alloc_sbuf_tensor("A2", [128, 8, 2], F32)
    x_t = nc.alloc_sbuf_tensor("x_t", [128, 8, 512], F32)
    eps_sb = nc.alloc_sbuf_tensor("eps_sb", [128, 1], F32)
    est = contextlib.ExitStack()
    stA = est.enter_context(nc.sbuf_tensor("stA", [128, 4096], F32))
    stB = est.enter_context(nc.sbuf_tensor("stB", [128, 2816], F32))
    mod = mod_vectors(nc, S, ada_w, ada_b, c2, [2, 3, 4, 5], modps, stA, "stA", mtiles)
    S.op("dve", lambda e: e.scalar_tensor_tensor(
        out=A2[:], in0=mod[:, 16:24, :], scalar=1.0, in1=small["nrm"][:].unsqueeze(2).to_broadcast([128, 8, 2]),
        op0=ALU.add, op1=ALU.mult), reads=["mod_sb", "nrm"], writes=["A2"])

    if debug == "mod":
        dbg = nc.dram_tensor("dbg", [128, 32, 2], F32, kind="ExternalOutput").ap()
        tok = S.op("sp", lambda e: e.dma_start(out=dbg[:, :, :], in_=mod[:]), reads=["mod_sb"], dma=True)
        S.emit([tok])
        return nc

    def g1(j, v):
        return mod[:, j, v:v + 1]

    def sh2(j, v):
        return mod[:, 8 + j, v:v + 1]

    def g2(j, v):
        return mod[:, 24 + j, v:v + 1]

    est1 = est
    wout_bf = est1.enter_context(nc.sbuf_tensor("wout_bf", [128, KM, D], BF16))
    cast_weights(nc, S, w_out, wout_bf, KM, D, [(stA, "stA"), (stB, "stB")], D)
    mix_f = est1.enter_context(nc.sbuf_tensor("mix_f", [128, KM, 512], F32))
    mix_b = est1.enter_context(nc.sbuf_tensor("mix_b", [128, KM, 512], BF16))
    acc_i = [0]

    def stage1(x_src, mix_src, dst, ncols_total, v):
        for c0 in range(0, ncols_total, 512):
            n = min(512, ncols_total - c0)
            S.op("sp", lambda e, c0=c0, n=n: e.dma_start(out=mix_f[:, :, 0:n], in_=mix_src[:, :, c0:c0 + n]),
                 writes=["mix_f"], dma=True)
            S.op("sp", lambda e, c0=c0, n=n: e.dma_start(out=x_t[:, :, 0:n], in_=x_src[:, :, c0:c0 + n]),
                 writes=["x_t"], dma=True)
            h = KM // 2
            S.op("pool", lambda e, n=n, h=h: e.tensor_copy(out=mix_b[:, 0:h, 0:n], in_=mix_f[:, 0:h, 0:n]),
                 reads=["mix_f"], writes=["mix_b0"])
            S.op("act", lambda e, n=n, h=h: e.copy(out=mix_b[:, h:KM, 0:n], in_=mix_f[:, h:KM, 0:n]),
                 reads=["mix_f"], writes=["mix_b1"])
            for j in range(8):
                ai = acc_i[0] % 2
                acc_i[0] += 1
                ps = ps_acc[ai]
                for k in range(KM):
                    S.op("pe", lambda e, ps=ps, k=k, j=j, n=n: e.matmul(
                        ps[:, 0:n], wout_bf[:, k, j * 128:(j + 1) * 128], mix_b[:, k, 0:n],
                        start=(k == 0), stop=(k == KM - 1)),
                        reads=[("w", "wout_bf"), "mix_b0", "mix_b1"], writes=[f"ps_acc{ai}"])
                S.op("dve", lambda e, ps=ps, j=j, n=n, v=v: e.scalar_tensor_tensor(
                    out=x_t[:, j, 0:n], in0=ps[:, 0:n], scalar=g1(j, v), in1=x_t[:, j, 0:n],
                    op0=ALU.mult, op1=ALU.add),
                    reads=[f"ps_acc{ai}", "mod_sb", "x_t"], writes=["x_t"])
            S.op("sp", lambda e, c0=c0, n=n: e.dma_start(out=dst[:, :, c0:c0 + n], in_=x_t[:, :, 0:n]),
                 reads=["x_t"], writes=["xmid_dram"], dma=True)

    stage1(xT, mixT, xmid, TP, 0)
    if ctx_out:
        stage1(xcT, mixcT, xmidc, CP, 1)

    if debug == "stage1":
        S.barrier()
        tok = S.op("sp", lambda e: e.dma_start(out=xoT[:, :, 0:512], in_=x_t[:]), reads=["x_t"], dma=True)
        S.emit([tok])
        return nc
    S.barrier()
    est.close()
    wup_bf = nc.alloc_sbuf_tensor("wup_bf", [128, 8, 2 * FH], BF16)
    wdn_bf = nc.alloc_sbuf_tensor("wdn_bf", [128, NCH, D], BF16)
    est = contextlib.ExitStack()
    stA = est.enter_context(nc.sbuf_tensor("stA2", [128, 2816], F32))
    stB = est.enter_context(nc.sbuf_tensor("stB2", [128, 2816], F32))
    cast_weights(nc, S, w_up, wup_bf, 8, 2 * FH, [(stA, "stA"), (stB, "stB")], FH)
    cast_weights(nc, S, w_dn, wdn_bf, NCH, D, [(stA, "stA"), (stB, "stB")], D)
    S.barrier()
    est.close()
    h_bf = nc.alloc_sbuf_tensor("h_bf", [128, 8, 512], BF16)
    sq_bf = [nc.alloc_sbuf_tensor(f"sq_bf{i}", [128, 512], BF16) for i in range(2)]
    rs = nc.alloc_sbuf_tensor("rs", [128, 512], F32)
    t1 = [nc.alloc_sbuf_tensor(f"t1_{i}", [128, 512], F32) for i in range(2)]
    u_bf = nc.alloc_sbuf_tensor("u_bf", [128, NCH, 512], BF16)
    print("sbuf remaining after ffn alloc", nc.sbuf_bytes_remaining)

    def rms_stats(src_tile, ncol, off):
        for k in range(8):
            b = k % 2
            S.op("act", lambda e, k=k, b=b: e.activation(out=sq_bf[b][:, 0:ncol], in_=src_tile[:, k, off:off + ncol],
                                                         func=AF.Square),
                 reads=["x_t"], writes=[f"sq{b}"])
            S.op("pe", lambda e, k=k, b=b: e.matmul(ps_ss[:, 0:ncol], ones_bf[:], sq_bf[b][:, 0:ncol],
                                                   start=(k == 0), stop=(k == 7)),
                 reads=[f"sq{b}", "ones_bf"], writes=["ps_ss"])
        S.op("act", lambda e: e.activation(out=rs[:, 0:ncol], in_=ps_ss[:, 0:ncol], func=AF.Sqrt,
                                           bias=eps_sb[:, 0:1], scale=1.0 / D),
             reads=["ps_ss", "eps"], writes=["rs"])
        S.op("dve", lambda e: e.reciprocal(out=rs[:, 0:ncol], in_=rs[:, 0:ncol]), reads=["rs"], writes=["rs"])

    S.op("pool", lambda e: e.memset(eps_sb[:], 1e-6), writes=["eps"])
    vg_i = [0]

    def stage2(src, dst, n_total, v, is_ctx):
        ntile = (n_total + WOUT - 1) // WOUT
        for ti in range(ntile):
            o0 = ti * WOUT
            no = min(WOUT, n_total - o0)
            n = no + 2
            S.op("sp", lambda e, o0=o0, n=n: e.dma_start(out=x_t[:, :, 0:n], in_=src[:, :, o0:o0 + n]),
                 reads=["xmid_dram"], writes=["x_t"], dma=True)
            rms_stats(x_t, n, 0)
            for k in range(8):
                b = k % 2
                S.op("dve", lambda e, k=k, b=b, n=n: e.tensor_tensor(out=t1[b][:, 0:n], in0=x_t[:, k, 0:n],
                                                                    in1=rs[:, 0:n], op=ALU.mult),
                     reads=["x_t", "rs"], writes=[f"t1_{b}"])
                S.op("act", lambda e, k=k, b=b, n=n, v=v: e.activation(
                    out=h_bf[:, k, 0:n], in_=t1[b][:, 0:n], func=AF.Identity,
                    bias=sh2(k, v), scale=A2[:, k, v:v + 1]),
                    reads=[f"t1_{b}", "A2", "mod_sb"], writes=["h_bf"])
            if is_ctx:
                S.op("dve", lambda e: e.tensor_scalar(out=h_bf[:, :, 0:1], in0=h_bf[:, :, 0:1], scalar1=0.0,
                                                      scalar2=None, op0=ALU.mult), reads=["h_bf"], writes=["h_bf"])
                S.op("dve", lambda e, n=n: e.tensor_scalar(out=h_bf[:, :, n - 1:n], in0=h_bf[:, :, n - 1:n], scalar1=0.0,
                                                           scalar2=None, op0=ALU.mult), reads=["h_bf"], writes=["h_bf"])
            else:
                if ti == 0:
                    S.op("dve", lambda e: e.tensor_scalar(out=h_bf[:, :, 0:1], in0=h_bf[:, :, 0:1],
                                                          scalar1=small["emask"][:, 0:1], scalar2=None, op0=ALU.mult),
                         reads=["emask", "h_bf"], writes=["h_bf"])
                if ti == ntile - 1:
                    S.op("dve", lambda e, n=n: e.tensor_scalar(out=h_bf[:, :, n - 1:n], in0=h_bf[:, :, n - 1:n],
                                                               scalar1=small["emask"][:, 1:2], scalar2=None, op0=ALU.mult),
                         reads=["emask", "h_bf"], writes=["h_bf"])
            if debug == "s2a":
                return
            for c in range(NCH if debug != "s2b" else 1):
                bi = vg_i[0] % 2
                vg_i[0] += 1
                pv, pg = ps_val[bi], ps_gate[bi]
                for k in range(8):
                    S.op("pe", lambda e, pv=pv, k=k, c=c, n=n: e.matmul(
                        pv[:, 0:n], wup_bf[:, k, c * 128:(c + 1) * 128], h_bf[:, k, 0:n], start=(k == 0), stop=(k == 7)),
                        reads=[("w", "wup_bf"), "h_bf"], writes=[f"ps_val{bi}"])
                for k in range(8):
                    S.op("pe", lambda e, pg=pg, k=k, c=c, n=n: e.matmul(
                        pg[:, 0:n], wup_bf[:, k, FH + c * 128:FH + (c + 1) * 128], h_bf[:, k, 0:n], start=(k == 0), stop=(k == 7)),
                        reads=[("w", "wup_bf"), "h_bf"], writes=[f"ps_gate{bi}"])
                ta, tb = t1[0], t1[1]
                S.op("act", lambda e, pg=pg, c=c, no=no: e.activation(out=ta[:, 0:no], in_=pg[:, 0:no], func=AF.Identity,
                                                                     scale=small["cw"][:, c, 0:1]),
                     reads=[f"ps_gate{bi}", "cw"], writes=["t1_0"])
                S.op("dve", lambda e, pg=pg, c=c, no=no: e.scalar_tensor_tensor(
                    out=ta[:, 0:no], in0=pg[:, 1:no + 1], scalar=small["cw"][:, c, 1:2], in1=ta[:, 0:no],
                    op0=ALU.mult, op1=ALU.add), reads=[f"ps_gate{bi}", "cw", "t1_0"], writes=["t1_0"])
                S.op("dve", lambda e, pg=pg, c=c, no=no: e.scalar_tensor_tensor(
                    out=ta[:, 0:no], in0=pg[:, 2:no + 2], scalar=small["cw"][:, c, 2:3], in1=ta[:, 0:no],
                    op0=ALU.mult, op1=ALU.add), reads=[f"ps_gate{bi}", "cw", "t1_0"], writes=["t1_0"])
                S.op("act", lambda e, c=c, no=no: e.activation(out=tb[:, 0:no], in_=ta[:, 0:no], func=AF.Silu,
                                                              bias=small["cb"][:, c:c + 1]),
                     reads=["t1_0", "cb"], writes=["t1_1"])
                S.op("dve", lambda e, pv=pv, c=c, no=no: e.tensor_tensor(out=u_bf[:, c, 0:no], in0=tb[:, 0:no],
                                                                        in1=pv[:, 1:no + 1], op=ALU.mult),
                     reads=["t1_1", f"ps_val{bi}"], writes=["u_bf"])
            if debug == "s2b":
                return
            for j in range(8):
                ai = acc_i[0] % 2
                acc_i[0] += 1
                ps = ps_acc[ai]
                for c in range(NCH):
                    S.op("pe", lambda e, ps=ps, c=c, j=j, no=no: e.matmul(
                        ps[:, 0:no], wdn_bf[:, c, j * 128:(j + 1) * 128], u_bf[:, c, 0:no],
                        start=(c == 0), stop=(c == NCH - 1)),
                        reads=[("w", "wdn_bf"), "u_bf"], writes=[f"ps_acc{ai}"])
                S.op("dve", lambda e, ps=ps, j=j, no=no, v=v: e.scalar_tensor_tensor(
                    out=x_t[:, j, 1:no + 1], in0=ps[:, 0:no], scalar=g2(j, v), in1=x_t[:, j, 1:no + 1],
                    op0=ALU.mult, op1=ALU.add),
                    reads=[f"ps_acc{ai}", "mod_sb", "x_t"], writes=["x_t"])
            if debug == "s2c":
                return
            if final and not is_ctx:
                rms_stats(x_t, no, 1)
                for j in range(8):
                    S.op("dve", lambda e, j=j, no=no: e.scalar_tensor_tensor(
                        out=x_t[:, j, 1:no + 1], in0=x_t[:, j, 1:no + 1], scalar=small["fnw"][:, j:j + 1],
                        in1=rs[:, 0:no], op0=ALU.mult, op1=ALU.mult),
                        reads=["x_t", "fnw", "rs"], writes=["x_t"])
            tok = S.op("sp", lambda e, o0=o0, no=no: e.dma_start(out=dst[:, :, o0:o0 + no], in_=x_t[:, :, 1:no + 1]),
                       reads=["x_t"], writes=["out_dram"], dma=True)
            S.final_tokens.append(tok)

    if debug in ("s2a", "s2b", "s2c", "s2d"):
        stage2(xmid, xoT, TL, 0, False)
        S.barrier()
        tok = S.op("sp", lambda e: e.dma_start(out=xoT[:, :, 0:512], in_=x_t[:]), reads=["x_t"], dma=True)
        S.emit([tok])
        return nc
    with nc.allow_low_precision("bf16 matmul operands, fp32 accumulation"):
        stage2(xmid, xoT, TL, 0, False)
        if ctx_out:
            stage2(xmidc, xcoT, CTXL, 1, True)
        S.emit(S.final_tokens)
    return nc


def fm(a):
    T, F = a.shape
    return np.ascontiguousarray(a.reshape(T, F // 128, 128).transpose(2, 1, 0))


def fm_inv(a):
    p, nch, T = a.shape
    return np.ascontiguousarray(a.transpose(2, 1, 0).reshape(T, nch * 128))


def kch(w):
    K, N = w.shape
    return np.ascontiguousarray(w.reshape(K // 128, 128, N).transpose(1, 0, 2))


def vec_fm(v):
    return np.ascontiguousarray(v.reshape(-1, 128).T)


def halo_slice(a, start, end):
    T, F = a.shape
    out = np.zeros((end - start + 2, F), a.dtype)
    lo, hi = max(start - 1, 0), min(end + 1, T)
    out[lo - (start - 1):hi - (start - 1)] = a[lo:hi]
    return out


def ffn_inputs(core, x, mix, xc, mixc, c, c_ctx, ada_w, ada_b, nrm, w_out, w_up, cw, cb, w_dn, ctx_out, fnw=None):
    b, q = divmod(core, 4)
    s, e = q * TL, (q + 1) * TL
    m = {
        "xT": fm(halo_slice(x[b], s, e)),
        "mixT": fm(halo_slice(mix[b], s, e)),
        "w_out": kch(w_out), "w_up": kch(w_up), "w_dn": kch(w_dn),
        "ada_w": kch(ada_w), "ada_b": vec_fm(ada_b),
        "c2": np.ascontiguousarray(np.stack([vec_fm(c[b]), vec_fm(c_ctx)], axis=-1)),
        "nrm": vec_fm(nrm),
        "cw": np.ascontiguousarray(cw.T.reshape(NCH, 128, 3).transpose(1, 0, 2)),
        "cb": vec_fm(cb),
        "emask": np.ascontiguousarray(np.broadcast_to(
            np.array([0.0 if q == 0 else 1.0, 0.0 if q == 3 else 1.0], np.float32), (128, 2))),
    }
    if ctx_out:
        m["xcT"] = fm(halo_slice(xc[b], 0, CTXL))
        m["mixcT"] = fm(halo_slice(mixc[b], 0, CTXL))
    if fnw is not None:
        m["fnw"] = vec_fm(fnw)
    return m


SEQ = 16384
NKT = (SEQ + CTXL) // 128
NPROJ = 10
LAM_INIT = 0.8 - 0.6 * math.exp(-0.3 * 0)


def build_even(debug=None, nq_tiles=32):
    nc = new_nc()
    S = Sched(nc)

    def din(name, shape):
        return nc.dram_tensor(name, shape, F32, kind="ExternalInput").ap()

    xT = din("xT", [128, 8, SEQ])
    cT = din("cT", [128, 8, CTXL])
    w_in = din("w_in", [128, 8, NPROJ * 128])
    ada_w = din("ada_w", [128, 8, 6 * D])
    ada_b = din("ada_b", [128, 48])
    c2 = din("c2", [128, 8, 2])
    nrm = din("nrm", [128, 8])
    lamv = din("lamv", [128, 4, 64])
    subw = din("subw", [128, 1])
    rfreq = din("rfreq", [128, 3])
    rpos = din("rpos", [128, 256])
    attT = nc.dram_tensor("attT", [128, SEQ], F32, kind="ExternalOutput").ap()
    attcT = nc.dram_tensor("attcT", [128, CTXL], F32, kind="ExternalOutput").ap()

    bank = [nc.alloc_psum_tensor(f"bank{i}", [128, 512], F32) for i in range(8)]

    ones_bf = nc.alloc_sbuf_tensor("ones_bf", [128, 128], BF16)
    S.op("pool", lambda e: e.memset(ones_bf[:], 1.0), writes=["ones_bf"])
    eps6 = nc.alloc_sbuf_tensor("eps6", [128, 1], F32)
    S.op("pool", lambda e: e.memset(eps6[:], 1e-6), writes=["eps6"])
    eps5 = nc.alloc_sbuf_tensor("eps5", [128, 1], F32)
    S.op("pool", lambda e: e.memset(eps5[:], 1e-5), writes=["eps5"])
    small = {}
    for name, src, shape in (("nrm", nrm, [128, 8]), ("lamv", lamv, [128, 4, 64]), ("subw", subw, [128, 1]),
                             ("rfreq", rfreq, [128, 3]), ("rpos", rpos, [128, 256])):
        t = nc.alloc_sbuf_tensor(name + "_sb", shape, F32)
        S.op("sp", lambda e, t=t, src=src: e.dma_start(out=t[:], in_=src), writes=[name], dma=True)
        small[name] = t
    mtiles = mod_alloc(nc, 2)
    A1 = nc.alloc_sbuf_tensor("A1", [128, 8, 2], F32)
    lam_t = nc.alloc_sbuf_tensor("lam_t", [128, 8], F32)
    subs = nc.alloc_sbuf_tensor("subs", [128, 1], F32)
    Crow = nc.alloc_sbuf_tensor("Crow", [128, 256], F32)
    Srow = nc.alloc_sbuf_tensor("Srow", [128, 256], F32)
    Ccol = nc.alloc_sbuf_tensor("Ccol", [128, 64], F32)
    Scol = nc.alloc_sbuf_tensor("Scol", [128, 64], F32)
    QT = nc.alloc_sbuf_tensor("QT", [128, SEQ], BF16)
    QcT = nc.alloc_sbuf_tensor("QcT", [128, CTXL], BF16)
    KT = nc.alloc_sbuf_tensor("KT", [128, SEQ + CTXL], BF16)
    Vt = nc.alloc_sbuf_tensor("Vt", [128, NKT, 128], BF16)
    win_bf = nc.alloc_sbuf_tensor("win_bf", [128, 8, NPROJ * 128], BF16)
    x_t = nc.alloc_sbuf_tensor("x_t", [128, 8, 512], F32)
    h_bf = nc.alloc_sbuf_tensor("h_bf", [128, 8, 512], BF16)
    sq_bf = [nc.alloc_sbuf_tensor(f"sq_bf{i}", [128, 512], BF16) for i in range(2)]
    rs = nc.alloc_sbuf_tensor("rs", [128, 512], F32)
    t1 = [nc.alloc_sbuf_tensor(f"t1_{i}", [128, 512], F32) for i in range(3)]
    cs_t = [nc.alloc_sbuf_tensor(f"cs_{i}", [128, 512], F32) for i in range(2)]
    PT = [nc.alloc_sbuf_tensor(f"PT{i}", [128, 512], BF16) for i in range(4)]
    stA = nc.alloc_sbuf_tensor("stA", [128, 4096], F32)
    Qpad = [[nc.alloc_sbuf_tensor(f"Qpad{a}{m}", [128, 512], BF16) for m in range(2)] for a in range(2)]
    accL = [nc.alloc_sbuf_tensor(f"accL{m}", [128, 512], F32) for m in range(2)]
    ones_f = nc.alloc_sbuf_tensor("ones_f", [128, 128], F32)
    S.op("pool", lambda e: e.memset(ones_f[:], 1.0), writes=["ones_f"])
    for a in range(2):
        for m in range(2):
            S.op("pool", lambda e, a=a, m=m: e.memset(Qpad[a][m][:], 0.0), writes=[f"Qpad{a}{m}"])
    print("sbuf remaining (even)", nc.sbuf_bytes_remaining)

    mod = mod_vectors(nc, S, ada_w, ada_b, c2, [0, 1], bank[7], stA, "stA", mtiles)
    S.op("dve", lambda e: e.scalar_tensor_tensor(
        out=A1[:], in0=mod[:, 8:16, :], scalar=1.0, in1=small["nrm"][:].unsqueeze(2).to_broadcast([128, 8, 2]),
        op0=ALU.add, op1=ALU.mult), reads=["mod_sb", "nrm"], writes=["A1"])
    cast_weights(nc, S, w_in, win_bf, 8, NPROJ * 128, [(stA, "stA")], NPROJ * 128)

    lv = small["lamv"]
    tmpl = t1[0]
    lv4 = lv[:].rearrange("p (a t) d -> p a t d", t=2)
    S.op("dve", lambda e: e.tensor_tensor(out=tmpl[:, 0:128].rearrange("p (a d) -> p a d", a=2),
                                          in0=lv4[:, :, 0, :], in1=lv4[:, :, 1, :], op=ALU.mult),
         reads=["lamv"], writes=["t1_0"])
    S.op("dve", lambda e: e.reduce_sum(out=lam_t[:, 0:2], in_=tmpl[:, 0:128].rearrange("p (a d) -> p a d", a=2),
                                       axis=AX.X), reads=["t1_0"], writes=["lam_t"])
    S.op("act", lambda e: e.activation(out=lam_t[:, 2:4], in_=lam_t[:, 0:2], func=AF.Exp), reads=["lam_t"], writes=["lam_t"])
    S.op("dve", lambda e: e.tensor_tensor(out=lam_t[:, 4:5], in0=lam_t[:, 3:4], in1=lam_t[:, 2:3], op=ALU.subtract),
         reads=["lam_t"], writes=["lam_t"])
    S.op("dve", lambda e: e.tensor_scalar(out=lam_t[:, 4:5], in0=lam_t[:, 4:5], scalar1=-LAM_INIT, scalar2=None, op0=ALU.add),
         reads=["lam_t"], writes=["lam_t"])
    S.op("dve", lambda e: e.tensor_scalar(out=subs[:], in0=small["subw"][:], scalar1=(1.0 - LAM_INIT), scalar2=None, op0=ALU.mult),
         reads=["subw"], writes=["subs"])

    TWO_PI = 2.0 * math.pi
    ti32 = nc.alloc_sbuf_tensor("ti32", [128, 256], mybir.dt.int32)

    def trig(dst, npos, fcol, phase, signed):
        ang = t1[1]
        k_f = t1[2]
        a = ang[:, 0:npos]
        S.op("dve", lambda e: e.tensor_scalar(out=a, in0=small["rpos"][:, 0:npos], scalar1=small["rfreq"][:, fcol:fcol + 1],
                                              scalar2=phase, op0=ALU.mult, op1=ALU.add),
             reads=["rpos", "rfreq"], writes=["t1_1"])
        S.op("dve", lambda e: e.tensor_scalar(out=k_f[:, 0:npos], in0=a, scalar1=1.0 / TWO_PI, scalar2=None, op0=ALU.mult),
             reads=["t1_1"], writes=["t1_2"])
        S.op("dve", lambda e: e.tensor_copy(out=ti32[:, 0:npos], in_=k_f[:, 0:npos]), reads=["t1_2"], writes=["ti32"])
        S.op("dve", lambda e: e.tensor_copy(out=k_f[:, 0:npos], in_=ti32[:, 0:npos]), reads=["ti32"], writes=["t1_2"])
        S.op("dve", lambda e: e.scalar_tensor_tensor(out=a, in0=k_f[:, 0:npos], scalar=-TWO_PI, in1=a, op0=ALU.mult, op1=ALU.add),
             reads=["t1_1", "t1_2"], writes=["t1_1"])
        S.op("dve", lambda e: e.tensor_scalar(out=k_f[:, 0:npos], in0=a, scalar1=math.pi, scalar2=-TWO_PI, op0=ALU.is_gt, op1=ALU.mult),
             reads=["t1_1"], writes=["t1_2"])
        S.op("dve", lambda e: e.tensor_tensor(out=a, in0=a, in1=k_f[:, 0:npos], op=ALU.add), reads=["t1_1", "t1_2"], writes=["t1_1"])
        S.op("dve", lambda e: e.tensor_scalar(out=k_f[:, 0:npos], in0=a, scalar1=-math.pi, scalar2=TWO_PI, op0=ALU.is_lt, op1=ALU.mult),
             reads=["t1_1"], writes=["t1_2"])
        S.op("dve", lambda e: e.tensor_tensor(out=a, in0=a, in1=k_f[:, 0:npos], op=ALU.add), reads=["t1_1", "t1_2"], writes=["t1_1"])
        S.op("act", lambda e: e.activation(out=dst[:, 0:npos], in_=a, func=AF.Sin), reads=["t1_1"], writes=[("tab", dst.name)])
        if signed:
            S.op("dve", lambda e: e.tensor_scalar(out=dst[:, 0:npos], in0=dst[:, 0:npos], scalar1=small["rfreq"][:, 2:3],
                                                  scalar2=None, op0=ALU.mult), reads=[("tab", dst.name), "rfreq"],
                 writes=[("tab", dst.name)])

    trig(Srow, 256, 0, 0.0, True)
    trig(Crow, 256, 0, math.pi / 2, False)
    trig(Scol, 64, 1, 0.0, True)
    trig(Ccol, 64, 1, math.pi / 2, False)
    tabs = [("tab", "Srow"), ("tab", "Crow"), ("tab", "Scol"), ("tab", "Ccol")]

    if debug == "tabs":
        dbg = nc.dram_tensor("dbg", [128, 1024], F32, kind="ExternalOutput").ap()
        toks = []
        for i, (t, n) in enumerate(((Crow, 256), (Srow, 256), (Ccol, 64), (Scol, 64))):
            toks.append(S.op("sp", lambda e, t=t, n=n, i=i: e.dma_start(out=dbg[:, i * 256:i * 256 + n], in_=t[:, 0:n]),
                             reads=tabs, dma=True))
        toks.append(S.op("sp", lambda e: e.dma_start(out=dbg[:, 1000:1008], in_=lam_t[:]), reads=["lam_t"], dma=True))
        S.emit(toks)
        return nc

    S.barrier()
    def rms_stats(ncol):
        for k in range(8):
            b = k % 2
            S.op("act", lambda e, k=k, b=b: e.activation(out=sq_bf[b][:, 0:ncol], in_=x_t[:, k, 0:ncol], func=AF.Square),
                 reads=["x_t"], writes=[f"sq{b}"])
            S.op("pe", lambda e, k=k, b=b: e.matmul(bank[7][:, 0:ncol], ones_bf[:], sq_bf[b][:, 0:ncol], start=(k == 0), stop=(k == 7)),
                 reads=[f"sq{b}", "ones_bf"], writes=["bank7"])
        S.op("act", lambda e: e.activation(out=rs[:, 0:ncol], in_=bank[7][:, 0:ncol], func=AF.Sqrt, bias=eps6[:, 0:1], scale=1.0 / D),
             reads=["bank7", "eps6"], writes=["rs"])
        S.op("dve", lambda e: e.reciprocal(out=rs[:, 0:ncol], in_=rs[:, 0:ncol]), reads=["rs"], writes=["rs"])

    pi = [0]

    def proj_chunk(ci, n):
        bi = pi[0] % 4
        pi[0] += 1
        ps = bank[bi]
        for k in range(8):
            S.op("pe", lambda e, ps=ps, k=k, ci=ci, n=n: e.matmul(ps[:, 0:n], win_bf[:, k, ci * 128:(ci + 1) * 128], h_bf[:, k, 0:n],
                                                             start=(k == 0), stop=(k == 7)),
                 reads=[("w", "win_bf"), "h_bf"], writes=[f"bank{bi}"])
        return ps, f"bank{bi}"

    def pass1_tile(src, c0, n, v, is_ctx, tok0):
        S.op("sp", lambda e: e.dma_start(out=x_t[:, :, 0:n], in_=src[:, :, c0:c0 + n]), writes=["x_t"], dma=True)
        rms_stats(n)
        for k in range(8):
            b = k % 2
            S.op("dve", lambda e, k=k, b=b: e.tensor_tensor(out=t1[b][:, 0:n], in0=x_t[:, k, 0:n], in1=rs[:, 0:n], op=ALU.mult),
                 reads=["x_t", "rs"], writes=[f"t1_{b}"])
            S.op("act", lambda e, k=k, b=b: e.activation(out=h_bf[:, k, 0:n], in_=t1[b][:, 0:n], func=AF.Identity,
                                                         bias=mod[:, k, v:v + 1], scale=A1[:, k, v:v + 1]),
                 reads=[f"t1_{b}", "A1", "mod_sb"], writes=["h_bf"])
        if not is_ctx:
            r0 = c0 // 64
            S.op("pool", lambda e: e.tensor_tensor(
                out=cs_t[0][:, 0:n].rearrange("p (r c) -> p r c", c=64),
                in0=Crow[:, r0:r0 + n // 64].unsqueeze(2).to_broadcast([128, n // 64, 64]),
                in1=Ccol[:, :].unsqueeze(1).to_broadcast([128, n // 64, 64]), op=ALU.mult), reads=tabs, writes=["cs0"])
            S.op("pool", lambda e: e.tensor_tensor(
                out=cs_t[1][:, 0:n].rearrange("p (r c) -> p r c", c=64),
                in0=Srow[:, r0:r0 + n // 64].unsqueeze(2).to_broadcast([128, n // 64, 64]),
                in1=Scol[:, :].unsqueeze(1).to_broadcast([128, n // 64, 64]), op=ALU.add), reads=tabs, writes=["cs1"])
        for qi, (ci, dst, dc0) in enumerate(((0, QcT if is_ctx else QT, c0), (2, KT, tok0))):
            ps, key = proj_chunk(ci, n)
            if is_ctx:
                S.op("act", lambda e, ps=ps, dst=dst, dc0=dc0: e.copy(out=dst[:, dc0:dc0 + n], in_=ps[:, 0:n]),
                     reads=[key], writes=[("qk", dst.name)])
            else:
                ps2, key2 = proj_chunk(ci + 1, n)
                S.op("dve", lambda e, ps=ps: e.tensor_tensor(out=t1[0][:, 0:n], in0=ps[:, 0:n], in1=cs_t[0][:, 0:n], op=ALU.mult),
                     reads=[key, "cs0"], writes=["t1_0"])
                S.op("dve", lambda e, ps2=ps2: e.tensor_tensor(out=t1[1][:, 0:n], in0=ps2[:, 0:n], in1=cs_t[1][:, 0:n], op=ALU.mult),
                     reads=[key2, "cs1"], writes=["t1_1"])
                S.op("dve", lambda e, dst=dst, dc0=dc0: e.tensor_tensor(out=dst[:, dc0:dc0 + n], in0=t1[0][:, 0:n], in1=t1[1][:, 0:n], op=ALU.add),
                     reads=["t1_0", "t1_1"], writes=[("qk", dst.name)])
        for tt in range(n // 128):
            bi = pi[0] % 4
            pi[0] += 1
            ps = bank[bi]
            for k in range(8):
                S.op("pe", lambda e, ps=ps, k=k, tt=tt: e.matmul(ps[:, 0:128], h_bf[:, k, tt * 128:(tt + 1) * 128], win_bf[:, k, 4 * 128:5 * 128],
                                                              start=(k == 0), stop=(k == 7)),
                     reads=[("w", "win_bf"), "h_bf"], writes=[f"bank{bi}"])
            kt = (tok0 + tt * 128) // 128
            S.op("act", lambda e, ps=ps, kt=kt: e.copy(out=Vt[:, kt, :], in_=ps[:, 0:128]), reads=[f"bank{bi}"], writes=["Vt"])

    with nc.allow_low_precision("bf16 matmul operands, fp32 accumulation"):
        pass1_tile(cT, 0, CTXL, 1, True, 0)
        for ti in range(SEQ // 512):
            pass1_tile(xT, ti * 512, 512, 0, False, CTXL + ti * 512)

        if debug == "qk":
            dbg = nc.dram_tensor("dbg", [128, 2048], BF16, kind="ExternalOutput").ap()
            toks = [S.op("sp", lambda e: e.dma_start(out=dbg[:, 0:512], in_=QT[:, 0:512]), reads=[("qk", "QT")], dma=True),
                    S.op("sp", lambda e: e.dma_start(out=dbg[:, 512:1024], in_=KT[:, 0:512]), reads=[("qk", "KT")], dma=True),
                    S.op("sp", lambda e: e.dma_start(out=dbg[:, 1024:1536], in_=Vt[:, 0:4, :]), reads=["Vt"], dma=True),
                    S.op("sp", lambda e: e.dma_start(out=dbg[:, 1536:2048], in_=QT[:, SEQ - 512:SEQ]), reads=[("qk", "QT")], dma=True)]
            S.emit(toks)
            return nc

        S.barrier()
        si = [0]
        pti = [0]
        qsi = [0]

        def attend(Qsrc, q0, nq, kt0, kt1, dst, d0):
            O = [bank[4], bank[5]]
            L = [bank[6], bank[7]]
            units = [(kt, m) for kt in range(kt0, kt1) for m in range(2)]
            PRE = 3
            slots = {}
            qs = qsi[0] % 2
            qsi[0] += 1
            S.op("dve", lambda e: e.tensor_copy(out=Qpad[qs][0][0:64, 0:nq], in_=Qsrc[0:64, q0:q0 + nq]),
                 reads=[("qk", Qsrc.name)], writes=[f"Qpad{qs}0"])
            S.op("act", lambda e: e.copy(out=Qpad[qs][1][64:128, 0:nq], in_=Qsrc[64:128, q0:q0 + nq]),
                 reads=[("qk", Qsrc.name)], writes=[f"Qpad{qs}1"])

            def emit_S(i):
                kt, m = units[i]
                sb = si[0] % 4
                si[0] += 1
                sps = bank[sb]
                S.op("pe", lambda e: e.matmul(
                    sps[:, 0:nq], KT[:, kt * 128:(kt + 1) * 128], Qpad[qs][m][:, 0:nq],
                    start=True, stop=True), reads=[("qk", "KT"), f"Qpad{qs}{m}"], writes=[f"bank{sb}"])
                pb = pti[0] % 4
                pti[0] += 1
                S.op("act", lambda e: e.activation(out=PT[pb][:, 0:nq], in_=sps[:, 0:nq], func=AF.Exp, scale=0.125),
                     reads=[f"bank{sb}"], writes=[f"PT{pb}"])
                slots[i] = pb

            def emit_AV(i):
                kt, m = units[i]
                pb = slots[i]
                S.op("pe", lambda e: e.matmul(O[m][:, 0:nq], Vt[:, kt, :], PT[pb][:, 0:nq],
                                              start=(kt == kt0), stop=(kt == kt1 - 1)),
                     reads=["Vt", f"PT{pb}"], writes=[f"bank{4 + m}"])
                if kt == kt0:
                    S.op("dve", lambda e: e.tensor_copy(out=accL[m][:, 0:nq], in_=PT[pb][:, 0:nq]),
                         reads=[f"PT{pb}"], writes=[f"accL{m}"])
                else:
                    S.op("dve", lambda e: e.tensor_tensor(out=accL[m][:, 0:nq], in0=accL[m][:, 0:nq], in1=PT[pb][:, 0:nq], op=ALU.add),
                         reads=[f"PT{pb}", f"accL{m}"], writes=[f"accL{m}"])

            for i in range(len(units) + PRE):
                if i < len(units):
                    emit_S(i)
                if i - PRE >= 0:
                    emit_AV(i - PRE)
            for m in range(2):
                S.op("pe", lambda e, m=m: e.matmul(L[m][:, 0:nq], ones_f[:], accL[m][:, 0:nq], start=True, stop=True),
                     reads=["ones_f", f"accL{m}"], writes=[f"bank{6 + m}"])
            for m in range(2):
                S.op("dve", lambda e, m=m: e.reciprocal(out=t1[2][:, 0:nq], in_=L[m][:, 0:nq]), reads=[f"bank{6 + m}"], writes=["t1_2"])
                S.op("dve", lambda e, m=m: e.tensor_tensor(out=t1[m][:, 0:nq], in0=O[m][:, 0:nq], in1=t1[2][:, 0:nq], op=ALU.mult),
                     reads=[f"bank{4 + m}", "t1_2"], writes=[f"t1_{m}"])
            S.op("dve", lambda e: e.scalar_tensor_tensor(out=t1[0][:, 0:nq], in0=t1[1][:, 0:nq], scalar=lam_t[:, 4:5], in1=t1[0][:, 0:nq],
                                                         op0=ALU.mult, op1=ALU.add), reads=["t1_0", "t1_1", "lam_t"], writes=["t1_0"])
            S.op("act", lambda e: e.activation(out=sq_bf[0][:, 0:nq], in_=t1[0][:, 0:nq], func=AF.Square), reads=["t1_0"], writes=["sq0"])
            S.op("pe", lambda e: e.matmul(bank[6][:, 0:nq], ones_bf[:], sq_bf[0][:, 0:nq], start=True, stop=True),
                 reads=["sq0", "ones_bf"], writes=["bank6"])
            S.op("act", lambda e: e.activation(out=rs[:, 0:nq], in_=bank[6][:, 0:nq], func=AF.Sqrt, bias=eps5[:, 0:1], scale=1.0 / 128),
                 reads=["bank6", "eps5"], writes=["rs"])
            S.op("dve", lambda e: e.reciprocal(out=rs[:, 0:nq], in_=rs[:, 0:nq]), reads=["rs"], writes=["rs"])
            S.op("dve", lambda e: e.scalar_tensor_tensor(out=t1[1][:, 0:nq], in0=t1[0][:, 0:nq], scalar=subs[:, 0:1], in1=rs[:, 0:nq],
                                                         op0=ALU.mult, op1=ALU.mult), reads=["t1_0", "subs", "rs"], writes=["t1_1"])
            tok = S.op("sp", lambda e: e.dma_start(out=dst[:, d0:d0 + nq], in_=t1[1][:, 0:nq]), reads=["t1_1"], dma=True)
            S.final_tokens.append(tok)

        attend(QcT, 0, CTXL, 0, 2, attcT, 0)
        for qt in range(nq_tiles):
            attend(QT, qt * 512, 512, 0, NKT, attT, qt * 512)
        S.emit(S.final_tokens)
    return nc


def even_inputs(core, z, layer=0):
    b, g = divmod(core, 4)
    w = z["ev_w_in"][0]
    qc = np.arange(g * 128, (g + 1) * 128)
    d = qc % 32
    partner = np.where(d < 16, qc + 16, qc - 16)
    cols = np.concatenate([qc, partner, 512 + qc, 512 + partner, 1024 + qc,
                           1536 + qc, 1536 + 512 + qc, 1536 + 1024 + qc,
                           1536 + 1536 + np.arange(128), 1536 + 1536 + 128 + np.arange(128)])
    pd = np.arange(128) % 64
    inv = (10000.0 ** (-(np.arange(16, dtype=np.float32)) / 16.0)).astype(np.float32)
    f = inv[pd % 16]
    rfreq = np.stack([np.where(pd < 32, f, 0.0), np.where(pd >= 32, f, 0.0), np.where(pd % 32 < 16, -1.0, 1.0)], 1).astype(np.float32)
    return {
        "xT": fm(z["x"][b]), "cT": fm(z["ctx"][b]),
        "w_in": kch(np.ascontiguousarray(w[:, cols])),
        "ada_w": kch(z["ada_w"][layer]), "ada_b": vec_fm(z["ada_b"][layer]),
        "c2": np.ascontiguousarray(np.stack([vec_fm(z["c"][b]), vec_fm(z["c_ctx"])], axis=-1)),
        "nrm": vec_fm(z["norm_mix"][layer]),
        "lamv": np.ascontiguousarray(np.broadcast_to(z["diff_lambda"][0], (128, 4, 64))),
        "subw": np.ascontiguousarray(z["diff_subln"][0].reshape(128, 1)),
        "rfreq": rfreq,
        "rpos": np.ascontiguousarray(np.broadcast_to(np.arange(256, dtype=np.float32), (128, 256))),
    }


NTOK = SEQ + CTXL
MT = 256


def build_mamba(debug=None, nlat_chunks=128):
    nc = new_nc()
    S = Sched(nc)

    def din(name, shape):
        return nc.dram_tensor(name, shape, F32, kind="ExternalInput").ap()

    def dscr(name, shape):
        return nc.dram_tensor(name, shape, F32, kind="Internal").ap()

    xT = din("xT", [128, 8, SEQ + 4])
    cT = din("cT", [128, 8, CTXL + 4])
    w_xbc = din("w_xbc", [128, 8, 768])
    w_z = din("w_z", [128, 8, 512])
    w_dt = din("w_dt", [128, 8, 16])
    ada_w = din("ada_w", [128, 8, 6 * D])
    ada_b = din("ada_b", [128, 48])
    c2 = din("c2", [128, 8, 2])
    nrm = din("nrm", [128, 8])
    cw = din("cw", [128, 6, 5])
    cb = din("cb", [128, 6])
    dtb = din("dtb", [128, 16])
    alog = din("alog", [128, 16])
    dsk = din("dsk", [128, 8])
    nw = din("nw", [128, 512])
    ident = din("ident", [128, 128])
    trif = din("trif", [128, 128])
    trib = din("trib", [128, 128])
    gy_out = nc.dram_tensor("gy_out", [SEQ, 512], F32, kind="ExternalOutput").ap()
    xs_tok = dscr("xs_tok", [NTOK, 512])
    B_tok = dscr("B_tok", [NTOK, 128])
    BT = dscr("BT", [128, NTOK])
    CT = dscr("CT", [128, NTOK])
    dt_tok = dscr("dt_tok", [NTOK, 16])
    sz_tok = dscr("sz_tok", [SEQ, 512])
    y_tok = dscr("y_tok", [SEQ, 512])

    bank = [nc.alloc_psum_tensor(f"bank{i}", [128, 512], F32) for i in range(8)]

    ones_bf = nc.alloc_sbuf_tensor("ones_bf", [128, 128], BF16)
    ones_f = nc.alloc_sbuf_tensor("ones_f", [128, 128], F32)
    S.op("pool", lambda e: e.memset(ones_bf[:], 1.0), writes=["ones_bf"])
    S.op("pool", lambda e: e.memset(ones_f[:], 1.0), writes=["ones_f"])
    eps6 = nc.alloc_sbuf_tensor("eps6", [128, 1], F32)
    S.op("pool", lambda e: e.memset(eps6[:], 1e-6), writes=["eps6"])
    eps5 = nc.alloc_sbuf_tensor("eps5", [128, 1], F32)
    S.op("pool", lambda e: e.memset(eps5[:], 1e-5), writes=["eps5"])
    small = {}
    for name, src, shape in (("nrm", nrm, [128, 8]), ("cw", cw, [128, 6, 5]), ("cb", cb, [128, 6]), ("dtb", dtb, [128, 16]),
                             ("alog", alog, [128, 16]), ("dsk", dsk, [128, 8]), ("nw", nw, [128, 512]),
                             ("ident", ident, [128, 128]), ("trif", trif, [128, 128]), ("trib", trib, [128, 128])):
        t = nc.alloc_sbuf_tensor(name + "_sb", shape, F32)
        S.op("sp", lambda e, t=t, src=src: e.dma_start(out=t[:], in_=src), writes=[name], dma=True)
        small[name] = t
    negA = nc.alloc_sbuf_tensor("negA", [128, 16], F32)
    S.op("act", lambda e: e.activation(out=negA[:], in_=small["alog"][:], func=AF.Exp), reads=["alog"], writes=["negA"])
    S.op("dve", lambda e: e.tensor_scalar(out=negA[:], in0=negA[:], scalar1=-1.0, scalar2=None, op0=ALU.mult),
         reads=["negA"], writes=["negA"])
    mtiles = mod_alloc(nc, 2)
    A1 = nc.alloc_sbuf_tensor("A1", [128, 8, 2], F32)
    wx_bf = nc.alloc_sbuf_tensor("wx_bf", [128, 8, 768], BF16)
    wz_bf = nc.alloc_sbuf_tensor("wz_bf", [128, 8, 512], BF16)
    wdt_bf = nc.alloc_sbuf_tensor("wdt_bf", [128, 8, 16], BF16)
    x_t = nc.alloc_sbuf_tensor("x_t", [128, 8, 512], F32)
    h_bf = nc.alloc_sbuf_tensor("h_bf", [128, 8, 512], BF16)
    sq_bf = [nc.alloc_sbuf_tensor(f"sq_bf{i}", [128, 512], BF16) for i in range(2)]
    rs = nc.alloc_sbuf_tensor("rs", [128, 512], F32)
    t1 = [nc.alloc_sbuf_tensor(f"t1_{i}", [128, 512], F32) for i in range(3)]
    xbcT = nc.alloc_sbuf_tensor("xbcT", [128, 6, MT], F32)
    tok_sb = [nc.alloc_sbuf_tensor(f"tok_sb{i}", [128, 512], F32) for i in range(2)]
    btok_sb = nc.alloc_sbuf_tensor("btok_sb", [128, 128], F32)
    sz_sb = nc.alloc_sbuf_tensor("sz_sb", [128, 512], F32)
    dt_sb = nc.alloc_sbuf_tensor("dt_sb", [128, 16], F32)
    stA = nc.alloc_sbuf_tensor("stA", [128, 4096], F32)
    xs_cs = [nc.alloc_sbuf_tensor(f"xs_c{i}", [128, 512], F32) for i in range(2)]
    bt_fs = [nc.alloc_sbuf_tensor(f"bt_f{i}", [128, 3, 128], F32) for i in range(2)]
    scn = [0]
    bt_b = nc.alloc_sbuf_tensor("bt_b", [128, 3, 128], BF16)
    dt_cs = [nc.alloc_sbuf_tensor(f"dt_c{i}", [128, 16], F32) for i in range(2)]
    da = nc.alloc_sbuf_tensor("da", [128, 8], F32)
    da_bc = nc.alloc_sbuf_tensor("da_bc", [128, 8, 128], F32)
    cum_sb = nc.alloc_sbuf_tensor("cum_sb", [128, 24], F32)
    e_all = nc.alloc_sbuf_tensor("e_all", [128, 24], F32)
    xdt_f = nc.alloc_sbuf_tensor("xdt_f", [128, 512], F32)
    xdt_b = nc.alloc_sbuf_tensor("xdt_b", [128, 512], BF16)
    xw_b = nc.alloc_sbuf_tensor("xw_b", [128, 512], BF16)
    mcb = nc.alloc_sbuf_tensor("mcb", [128, 128], F32)
    seg = nc.alloc_sbuf_tensor("seg", [128, 8, 128], F32)
    G_b = nc.alloc_sbuf_tensor("G_b", [128, 8, 128], BF16)
    ysc = nc.alloc_sbuf_tensor("ysc", [128, 512], F32)
    ytmp = nc.alloc_sbuf_tensor("ytmp", [128, 512], F32)
    yprevs = [nc.alloc_sbuf_tensor(f"yprev{i}", [128, 512], F32) for i in range(2)]
    szcs = [nc.alloc_sbuf_tensor(f"szc{i}", [128, 512], F32) for i in range(2)]
    h_f = nc.alloc_sbuf_tensor("h_f", [128, 512], F32)
    h_b = nc.alloc_sbuf_tensor("h_b", [128, 512], BF16)
    ss1 = nc.alloc_sbuf_tensor("ss1", [128, 2], F32)
    print("sbuf remaining (mamba)", nc.sbuf_bytes_remaining)

    mod = mod_vectors(nc, S, ada_w, ada_b, c2, [0, 1], bank[7], stA, "stA", mtiles)
    S.op("dve", lambda e: e.scalar_tensor_tensor(
        out=A1[:], in0=mod[:, 8:16, :], scalar=1.0, in1=small["nrm"][:].unsqueeze(2).to_broadcast([128, 8, 2]),
        op0=ALU.add, op1=ALU.mult), reads=["mod_sb", "nrm"], writes=["A1"])
    cast_weights(nc, S, w_xbc, wx_bf, 8, 768, [(stA, "stA")], 768)
    cast_weights(nc, S, w_z, wz_bf, 8, 512, [(stA, "stA")], 512)
    cast_weights(nc, S, w_dt, wdt_bf, 8, 16, [(stA, "stA")], 16)
    S.barrier()

    def rms_stats(ncol):
        for k in range(8):
            b = k % 2
            S.op("act", lambda e, k=k, b=b: e.activation(out=sq_bf[b][:, 0:ncol], in_=x_t[:, k, 0:ncol], func=AF.Square),
                 reads=["x_t"], writes=[f"sq{b}"])
            S.op("pe", lambda e, k=k, b=b: e.matmul(bank[7][:, 0:ncol], ones_bf[:], sq_bf[b][:, 0:ncol], start=(k == 0), stop=(k == 7)),
                 reads=[f"sq{b}", "ones_bf"], writes=["bank7"])
        S.op("act", lambda e: e.activation(out=rs[:, 0:ncol], in_=bank[7][:, 0:ncol], func=AF.Sqrt, bias=eps6[:, 0:1], scale=1.0 / D),
             reads=["bank7", "eps6"], writes=["rs"])
        S.op("dve", lambda e: e.reciprocal(out=rs[:, 0:ncol], in_=rs[:, 0:ncol]), reads=["rs"], writes=["rs"])

    pi = [0]

    def nbank():
        bi = pi[0] % 6
        pi[0] += 1
        return bank[bi], f"bank{bi}"

    def pass1_tile(src, c0, tok0, lat0, v, first, last):
        n = MT + 4
        S.op("sp", lambda e: e.dma_start(out=x_t[:, :, 0:n], in_=src[:, :, c0:c0 + n]), writes=["x_t"], dma=True)
        rms_stats(n)
        for k in range(8):
            b = k % 2
            S.op("dve", lambda e, k=k, b=b: e.tensor_tensor(out=t1[b][:, 0:n], in0=x_t[:, k, 0:n], in1=rs[:, 0:n], op=ALU.mult),
                 reads=["x_t", "rs"], writes=[f"t1_{b}"])
            S.op("act", lambda e, k=k, b=b: e.activation(out=h_bf[:, k, 0:n], in_=t1[b][:, 0:n], func=AF.Identity,
                                                         bias=mod[:, k, v:v + 1], scale=A1[:, k, v:v + 1]),
                 reads=[f"t1_{b}", "A1", "mod_sb"], writes=["h_bf"])
        if first:
            S.op("dve", lambda e: e.tensor_scalar(out=h_bf[:, :, 0:2], in0=h_bf[:, :, 0:2], scalar1=0.0, scalar2=None, op0=ALU.mult),
                 reads=["h_bf"], writes=["h_bf"])
        if last:
            S.op("dve", lambda e: e.tensor_scalar(out=h_bf[:, :, n - 2:n], in0=h_bf[:, :, n - 2:n], scalar1=0.0, scalar2=None, op0=ALU.mult),
                 reads=["h_bf"], writes=["h_bf"])
        for j in range(6):
            ps, key = nbank()
            for k in range(8):
                S.op("pe", lambda e, ps=ps, k=k, j=j: e.matmul(ps[:, 0:n], wx_bf[:, k, j * 128:(j + 1) * 128], h_bf[:, k, 0:n],
                                                              start=(k == 0), stop=(k == 7)),
                     reads=[("w", "wx_bf"), "h_bf"], writes=[key])
            ta = t1[2]
            S.op("act", lambda e, ps=ps, j=j: e.activation(out=ta[:, 0:MT], in_=ps[:, 0:MT], func=AF.Identity, scale=small["cw"][:, j, 0:1]),
                 reads=[key, "cw"], writes=["t1_2"])
            for kk in range(1, 5):
                S.op("dve", lambda e, ps=ps, j=j, kk=kk: e.scalar_tensor_tensor(
                    out=ta[:, 0:MT], in0=ps[:, kk:kk + MT], scalar=small["cw"][:, j, kk:kk + 1], in1=ta[:, 0:MT],
                    op0=ALU.mult, op1=ALU.add), reads=[key, "cw", "t1_2"], writes=["t1_2"])
            S.op("act", lambda e, j=j: e.activation(out=xbcT[:, j, :], in_=ta[:, 0:MT], func=AF.Silu, bias=small["cb"][:, j:j + 1]),
                 reads=["t1_2", "cb"], writes=[("xbc", j)])
        S.op("pool", lambda e: e.dma_start(out=BT[:, tok0:tok0 + MT], in_=xbcT[:, 4, :]), reads=[("xbc", 4)], dma=True)
        S.op("pool", lambda e: e.dma_start(out=CT[:, tok0:tok0 + MT], in_=xbcT[:, 5, :]), reads=[("xbc", 5)], dma=True)
        for tt in range(MT // 128):
            r0 = tok0 + tt * 128
            ps, key = nbank()
            for j in range(4):
                S.op("pe", lambda e, ps=ps, j=j, tt=tt: e.matmul(ps[:, j * 128:(j + 1) * 128], xbcT[:, j, tt * 128:(tt + 1) * 128],
                                                              small["ident"][:], start=True, stop=True),
                     reads=[("xbc", j), "ident"], writes=[key])
            tb = tok_sb[tt % 2]
            S.op("act", lambda e, ps=ps, tb=tb: e.copy(out=tb[:], in_=ps[:, 0:512]), reads=[key], writes=[f"tok_sb{tt % 2}"])
            S.op("pool", lambda e, tb=tb, r0=r0: e.dma_start(out=xs_tok[r0:r0 + 128, :], in_=tb[:]), reads=[f"tok_sb{tt % 2}"], dma=True)
            ps, key = nbank()
            S.op("pe", lambda e, ps=ps, tt=tt: e.matmul(ps[:, 0:128], xbcT[:, 4, tt * 128:(tt + 1) * 128], small["ident"][:],
                                                        start=True, stop=True), reads=[("xbc", 4), "ident"], writes=[key])
            S.op("act", lambda e, ps=ps: e.copy(out=btok_sb[:], in_=ps[:, 0:128]), reads=[key], writes=["btok_sb"])
            S.op("pool", lambda e, r0=r0: e.dma_start(out=B_tok[r0:r0 + 128, :], in_=btok_sb[:]), reads=["btok_sb"], dma=True)
            if lat0 is not None:
                ps, key = nbank()
                for k in range(8):
                    S.op("pe", lambda e, ps=ps, k=k, tt=tt: e.matmul(ps[:, 0:512], h_bf[:, k, 2 + tt * 128:2 + (tt + 1) * 128], wz_bf[:, k, :],
                                                                  start=(k == 0), stop=(k == 7)),
                         reads=[("w", "wz_bf"), "h_bf"], writes=[key])
                S.op("act", lambda e, ps=ps: e.activation(out=sz_sb[:], in_=ps[:, 0:512], func=AF.Silu), reads=[key], writes=["sz_sb"])
                l0 = lat0 + tt * 128
                S.op("pool", lambda e, l0=l0: e.dma_start(out=sz_tok[l0:l0 + 128, :], in_=sz_sb[:]), reads=["sz_sb"], dma=True)
            ps, key = nbank()
            for k in range(8):
                S.op("pe", lambda e, ps=ps, k=k, tt=tt: e.matmul(ps[:, 0:16], h_bf[:, k, 2 + tt * 128:2 + (tt + 1) * 128], wdt_bf[:, k, :],
                                                              start=(k == 0), stop=(k == 7)),
                     reads=[("w", "wdt_bf"), "h_bf"], writes=[key])
            S.op("dve", lambda e, ps=ps: e.tensor_tensor(out=dt_sb[:], in0=ps[:, 0:16], in1=small["dtb"][:], op=ALU.add),
                 reads=[key, "dtb"], writes=["dt_sb"])
            S.op("act", lambda e: e.activation(out=dt_sb[:], in_=dt_sb[:], func=AF.Exp), reads=["dt_sb"], writes=["dt_sb"])
            S.op("act", lambda e: e.activation(out=dt_sb[:], in_=dt_sb[:], func=AF.Ln, bias=ones_f[:, 0:1]), reads=["dt_sb"], writes=["dt_sb"])
            S.op("pool", lambda e, r0=r0: e.dma_start(out=dt_tok[r0:r0 + 128, :], in_=dt_sb[:]), reads=["dt_sb"], dma=True)

    def scan_chunk(d, r0, lat_r0, tri, trikey):
        is_ctx = lat_r0 is None
        par = scn[0] % 2
        scn[0] += 1
        xs_c, bt_f, dt_c, yprev, szc = xs_cs[par], bt_fs[par], dt_cs[par], yprevs[par], szcs[par]
        S.op("sp", lambda e: e.dma_start(out=xs_c[:], in_=xs_tok[r0:r0 + 128, :]), writes=[f"xs_c{par}"], dma=True)
        S.op("sp", lambda e: e.dma_start(out=bt_f[:, 0, :], in_=B_tok[r0:r0 + 128, :]), writes=[f"bt_f0{par}"], dma=True)
        S.op("sp", lambda e: e.dma_start(out=bt_f[:, 1, :], in_=BT[:, r0:r0 + 128]), writes=[f"bt_f1{par}"], dma=True)
        S.op("sp", lambda e: e.dma_start(out=bt_f[:, 2, :], in_=CT[:, r0:r0 + 128]), writes=[f"bt_f2{par}"], dma=True)
        S.op("sp", lambda e: e.dma_start(out=dt_c[:], in_=dt_tok[r0:r0 + 128, :]), writes=[f"dt_c{par}"], dma=True)
        S.op("act", lambda e: e.copy(out=bt_b[:], in_=bt_f[:]), reads=[f"bt_f0{par}", f"bt_f1{par}", f"bt_f2{par}"], writes=["bt_b"])
        S.op("dve", lambda e: e.tensor_tensor(out=da[:], in0=dt_c[:, d * 8:(d + 1) * 8], in1=negA[:, d * 8:(d + 1) * 8], op=ALU.mult),
             reads=[f"dt_c{par}", "negA"], writes=["da"])
        S.op("dve", lambda e: e.tensor_tensor(out=xdt_f[:].rearrange("p (e q) -> p e q", e=8), in0=xs_c[:].rearrange("p (e q) -> p e q", e=8),
                                              in1=dt_c[:, d * 8:(d + 1) * 8].unsqueeze(2).to_broadcast([128, 8, 64]), op=ALU.mult),
             reads=[f"xs_c{par}", f"dt_c{par}"], writes=["xdt_f"])
        S.op("pe", lambda e: e.matmul(bank[0][:, 0:8], tri[:], da[:], start=True, stop=True), reads=[trikey, "da"], writes=["bank0"])
        S.op("pe", lambda e: e.matmul(bank[0][:, 8:16], ones_f[:], da[:], start=True, stop=True), reads=["ones_f", "da"], writes=["bank0"])
        S.op("dve", lambda e: e.tensor_copy(out=cum_sb[:, 0:16], in_=bank[0][:, 0:16]), reads=["bank0"], writes=["cum_sb"])
        S.op("dve", lambda e: e.tensor_tensor(out=cum_sb[:, 16:24], in0=cum_sb[:, 8:16], in1=cum_sb[:, 0:8], op=ALU.subtract),
             reads=["cum_sb"], writes=["cum_sb"])
        S.op("act", lambda e: e.activation(out=e_all[:], in_=cum_sb[:], func=AF.Exp), reads=["cum_sb"], writes=["e_all"])
        if not is_ctx:
            S.op("pool", lambda e: e.tensor_copy(out=xdt_b[:], in_=xdt_f[:]), reads=["xdt_f"], writes=["xdt_b"])
            S.op("dve", lambda e: e.tensor_copy(out=da_bc[:], in_=da[:].unsqueeze(2).to_broadcast([128, 8, 128])), reads=["da"], writes=["da_bc"])
            for e_ in range(8):
                bk = 1 + e_ // 4
                S.op("pe", lambda e, e_=e_, bk=bk: e.matmul(bank[bk][:, (e_ % 4) * 128:(e_ % 4 + 1) * 128], da_bc[:, e_, :], tri[:],
                                                            start=True, stop=True), reads=["da_bc", trikey], writes=[f"bank{bk}"])
            S.op("pe", lambda e: e.matmul(bank[3][:, 0:128], bt_b[:, 1, :], bt_b[:, 2, :], start=True, stop=True),
                 reads=["bt_b"], writes=["bank3"])
            S.op("dve", lambda e: e.tensor_tensor(out=mcb[:], in0=bank[3][:, 0:128], in1=tri[:], op=ALU.mult),
                 reads=["bank3", trikey], writes=["mcb"])
            for half in range(2):
                S.op("dve", lambda e, half=half: e.tensor_tensor(
                    out=seg[:, half * 4:(half + 1) * 4, :], in0=bank[1 + half][:, 0:512].rearrange("p (e l) -> p e l", e=4),
                    in1=cum_sb[:, half * 4:(half + 1) * 4].unsqueeze(2).to_broadcast([128, 4, 128]), op=ALU.subtract),
                    reads=[f"bank{1 + half}", "cum_sb"], writes=["seg"])
            S.op("dve", lambda e: e.tensor_scalar(out=seg[:], in0=seg[:], scalar1=0.0, scalar2=None, op0=ALU.min), reads=["seg"], writes=["seg"])
            S.op("act", lambda e: e.activation(out=seg[:], in_=seg[:], func=AF.Exp), reads=["seg"], writes=["seg"])
            S.op("dve", lambda e: e.tensor_tensor(out=G_b[:], in0=seg[:], in1=mcb[:].unsqueeze(1).to_broadcast([128, 8, 128]), op=ALU.mult),
                 reads=["seg", "mcb"], writes=["G_b"])
            for e_ in range(8):
                S.op("pe", lambda e, e_=e_: e.matmul(bank[4][:, e_ * 64:(e_ + 1) * 64], G_b[:, e_, :], xdt_b[:, e_ * 64:(e_ + 1) * 64],
                                                     start=True, stop=True), reads=["G_b", "xdt_b"], writes=["bank4"])
            S.op("pe", lambda e: e.matmul(bank[5][:, 0:512], bt_b[:, 2, :], h_b[:], start=True, stop=True), reads=["bt_b", "h_b"], writes=["bank5"])
            S.op("dve", lambda e: e.tensor_tensor(out=ysc[:].rearrange("p (e q) -> p e q", e=8), in0=bank[5][:, 0:512].rearrange("p (e q) -> p e q", e=8),
                                                  in1=e_all[:, 0:8].unsqueeze(2).to_broadcast([128, 8, 64]), op=ALU.mult),
                 reads=["bank5", "e_all"], writes=["ysc"])
            S.op("dve", lambda e: e.tensor_tensor(out=ysc[:], in0=ysc[:], in1=bank[4][:, 0:512], op=ALU.add), reads=["ysc", "bank4"], writes=["ysc"])
            if d == 0:
                S.op("pool", lambda e: e.tensor_tensor(out=ytmp[:].rearrange("p (e q) -> p e q", e=8), in0=xs_c[:].rearrange("p (e q) -> p e q", e=8),
                                                       in1=small["dsk"][:].unsqueeze(2).to_broadcast([128, 8, 64]), op=ALU.mult),
                     reads=[f"xs_c{par}", "dsk"], writes=["ytmp"])
                S.op("dve", lambda e: e.tensor_tensor(out=ysc[:], in0=ysc[:], in1=ytmp[:], op=ALU.add), reads=["ysc", "ytmp"], writes=["ysc"])
                S.op("pool", lambda e: e.dma_start(out=y_tok[lat_r0:lat_r0 + 128, :], in_=ysc[:]), reads=["ysc"], writes=["yscr"], dma=True)
            else:
                S.op("sp", lambda e: e.dma_start(out=yprev[:], in_=y_tok[lat_r0:lat_r0 + 128, :]), reads=["yscr"], writes=[f"yprev{par}"], dma=True)
                S.op("sp", lambda e: e.dma_start(out=szc[:], in_=sz_tok[lat_r0:lat_r0 + 128, :]), writes=[f"szc{par}"], dma=True)
                S.op("dve", lambda e: e.tensor_tensor(out=ysc[:], in0=ysc[:], in1=yprev[:], op=ALU.add), reads=["ysc", f"yprev{par}"], writes=["ysc"])
                S.op("dve", lambda e: e.tensor_tensor(out=ysc[:], in0=ysc[:], in1=szc[:], op=ALU.mult), reads=["ysc", f"szc{par}"], writes=["ysc"])
                S.op("act", lambda e: e.activation(out=ytmp[:], in_=ysc[:], func=AF.Square, accum_out=ss1[:, 0:1]),
                     reads=["ysc"], writes=["ytmp", "ss1"])
                S.op("act", lambda e: e.activation(out=ss1[:, 1:2], in_=ss1[:, 0:1], func=AF.Sqrt, bias=eps5[:, 0:1], scale=1.0 / 512),
                     reads=["ss1", "eps5"], writes=["ss1b"])
                S.op("dve", lambda e: e.reciprocal(out=ss1[:, 1:2], in_=ss1[:, 1:2]), reads=["ss1b"], writes=["ss1b"])
                S.op("dve", lambda e: e.scalar_tensor_tensor(out=ytmp[:], in0=ysc[:], scalar=ss1[:, 1:2], in1=small["nw"][:],
                                                             op0=ALU.mult, op1=ALU.mult), reads=["ysc", "ss1b", "nw", "ytmp"], writes=["ytmp"])
                tok = S.op("pool", lambda e: e.dma_start(out=gy_out[lat_r0:lat_r0 + 128, :], in_=ytmp[:]), reads=["ytmp"], dma=True)
                S.final_tokens.append(tok)
        S.op("dve", lambda e: e.tensor_tensor(out=xw_b[:].rearrange("p (e q) -> p e q", e=8), in0=xdt_f[:].rearrange("p (e q) -> p e q", e=8),
                                              in1=e_all[:, 16:24].unsqueeze(2).to_broadcast([128, 8, 64]), op=ALU.mult),
             reads=["xdt_f", "e_all"], writes=["xw_b"])
        S.op("pe", lambda e: e.matmul(bank[6][:, 0:512], bt_b[:, 0, :], xw_b[:], start=True, stop=True), reads=["bt_b", "xw_b"], writes=["bank6"])
        S.op("dve", lambda e: e.tensor_tensor(out=h_f[:].rearrange("p (e q) -> p e q", e=8), in0=h_f[:].rearrange("p (e q) -> p e q", e=8),
                                              in1=e_all[:, 8:16].unsqueeze(2).to_broadcast([128, 8, 64]), op=ALU.mult),
             reads=["h_f", "e_all"], writes=["h_f"])
        S.op("dve", lambda e: e.tensor_tensor(out=h_f[:], in0=h_f[:], in1=bank[6][:, 0:512], op=ALU.add), reads=["h_f", "bank6"], writes=["h_f"])
        S.op("act", lambda e: e.copy(out=h_b[:], in_=h_f[:]), reads=["h_f"], writes=["h_b"])

    with nc.allow_low_precision("bf16 matmul operands, fp32 accumulation"):
        pass1_tile(cT, 0, 0, None, 1, True, True)
        nt = SEQ // MT
        for ti in range(nt):
            pass1_tile(xT, ti * MT, CTXL + ti * MT, ti * MT, 0, ti == 0, ti == nt - 1)
        S.barrier()
        for d in range(2):
            tri, trikey = (small["trif"], "trif") if d == 0 else (small["trib"], "trib")
            S.op("dve", lambda e: e.tensor_scalar(out=h_f[:], in0=small["nw"][:], scalar1=0.0, scalar2=None, op0=ALU.mult),
                 reads=["nw", "h_f"], writes=["h_f"])
            S.op("act", lambda e: e.copy(out=h_b[:], in_=h_f[:]), reads=["h_f"], writes=["h_b"])
            cchunks = [0, 1] if d == 0 else [1, 0]
            for c in cchunks:
                scan_chunk(d, c * 128, None, tri, trikey)
            lch = list(range(nlat_chunks)) if d == 0 else list(range(nlat_chunks - 1, -1, -1))
            for c in lch:
                scan_chunk(d, CTXL + c * 128, c * 128, tri, trikey)
        S.barrier()
        S.emit(S.final_tokens)
    return nc


def pad2(a):
    return np.concatenate([np.zeros((2, a.shape[1]), a.dtype), a, np.zeros((2, a.shape[1]), a.dtype)], 0)


def mamba_inputs(core, z, x1, xc1, layer=1):
    b, g = divmod(core, 4)
    w = z["ssm_w_in"][0]
    xcols = 2048 + g * 512 + np.arange(512)
    bcols = 2048 + 2048 + g * 128 + np.arange(128)
    ccols = 2048 + 2560 + g * 128 + np.arange(128)
    heads = g * 8 + np.arange(8)
    dtcols = np.concatenate([5120 + heads, 5120 + 32 + heads])
    conv_cols = np.concatenate([g * 512 + np.arange(512), 2048 + g * 128 + np.arange(128), 2560 + g * 128 + np.arange(128)])
    cwf = z["ssm_conv_w"][0][:, conv_cols]
    rep = lambda v: np.ascontiguousarray(np.broadcast_to(v.astype(np.float32), (128,) + v.shape))
    idx = np.arange(128)
    return {
        "xT": fm(pad2(x1[b])), "cT": fm(pad2(xc1[b])),
        "w_xbc": kch(np.ascontiguousarray(w[:, np.concatenate([xcols, bcols, ccols])])),
        "w_z": kch(np.ascontiguousarray(w[:, g * 512:(g + 1) * 512])),
        "w_dt": kch(np.ascontiguousarray(w[:, dtcols])),
        "ada_w": kch(z["ada_w"][layer]), "ada_b": vec_fm(z["ada_b"][layer]),
        "c2": np.ascontiguousarray(np.stack([vec_fm(z["c"][b]), vec_fm(z["c_ctx"])], axis=-1)),
        "nrm": vec_fm(z["norm_mix"][layer]),
        "cw": np.ascontiguousarray(cwf.T.reshape(6, 128, 5).transpose(1, 0, 2)),
        "cb": vec_fm(z["ssm_conv_b"][0][conv_cols]),
        "dtb": rep(np.concatenate([z["ssm_dt_bias"][0][0][heads], z["ssm_dt_bias"][0][1][heads]])),
        "alog": rep(np.concatenate([z["ssm_a_log"][0][0][heads], z["ssm_a_log"][0][1][heads]])),
        "dsk": rep(z["ssm_d"][0][heads]),
        "nw": rep(z["ssm_norm_w"][0][g * 512:(g + 1) * 512]),
        "ident": np.eye(128, dtype=np.float32),
        "trif": (idx[:, None] <= idx[None, :]).astype(np.float32),
        "trib": (idx[:, None] >= idx[None, :]).astype(np.float32),
    }


NCHK = NTOK // 128
RW_TILE = 510
LWC = -math.exp(-0.5)
GN_EPS = 64e-5


def build_rwkv(debug=None, nsteps=NCHK):
    nc = new_nc()
    S = Sched(nc)

    def din(name, shape):
        return nc.dram_tensor(name, shape, F32, kind="ExternalInput").ap()

    def dscr(name, shape):
        return nc.dram_tensor(name, shape, F32, kind="Internal").ap()

    xT = din("xT", [128, 8, SEQ + 2])
    cT = din("cT", [128, 8, CTXL + 2])
    w_in = din("w_in", [128, 8, 640])
    ada_w = din("ada_w", [128, 8, 6 * D])
    ada_b = din("ada_b", [128, 48])
    c2 = din("c2", [128, 8, 2])
    nrm = din("nrm", [128, 8])
    mu = din("mu", [128, 5, 2])
    pvec = din("pvec", [128, 8])
    lor = din("lor", [128, 2, 128])
    gup = din("gup", [128, 128])
    lnwb = din("lnwb", [128, 2, 128])
    ident = din("ident", [128, 128])
    bones = din("bones", [128, 128])
    masks = din("masks", [128, 2, 2, 128])
    mask3 = din("mask3", [128, 2, 128])
    rw_out = nc.dram_tensor("rw_out", [NTOK, 128], F32, kind="ExternalOutput").ap()
    SCR = [dscr(f"scr{d}", [128, 6, NTOK]) for d in range(2)]
    BN = dscr("bn_scr", [128, 2, NTOK])
    YS = dscr("ys_scr", [NTOK, 2, 128])

    bank = [nc.alloc_psum_tensor(f"bank{i}", [128, 512], F32) for i in range(8)]

    ones_bf = nc.alloc_sbuf_tensor("ones_bf", [128, 128], BF16)
    ones_f = nc.alloc_sbuf_tensor("ones_f", [128, 128], F32)
    S.op("pool", lambda e: e.memset(ones_bf[:], 1.0), writes=["ones_bf"])
    S.op("pool", lambda e: e.memset(ones_f[:], 1.0), writes=["ones_f"])
    eps6 = nc.alloc_sbuf_tensor("eps6", [128, 1], F32)
    S.op("pool", lambda e: e.memset(eps6[:], 1e-6), writes=["eps6"])
    epsg = nc.alloc_sbuf_tensor("epsg", [128, 1], F32)
    S.op("pool", lambda e: e.memset(epsg[:], GN_EPS), writes=["epsg"])
    small = {}
    for name, src, shape in (("nrm", nrm, [128, 8]), ("mu", mu, [128, 5, 2]), ("pvec", pvec, [128, 8]), ("lor", lor, [128, 2, 128]),
                             ("gup", gup, [128, 128]), ("lnwb", lnwb, [128, 2, 128]), ("ident", ident, [128, 128]),
                             ("bones", bones, [128, 128]), ("masks", masks, [128, 2, 2, 128]), ("mask3", mask3, [128, 2, 128])):
        t = nc.alloc_sbuf_tensor(name + "_sb", shape, F32)
        S.op("sp", lambda e, t=t, src=src: e.dma_start(out=t[:], in_=src), writes=[name], dma=True)
        small[name] = t
    pv = small["pvec"]
    muc = nc.alloc_sbuf_tensor("muc", [128, 5], F32)
    S.op("dve", lambda e: e.tensor_tensor(out=muc[:], in0=small["mu"][:, :, 0], in1=small["mu"][:, :, 1], op=ALU.add),
         reads=["mu"], writes=["muc"])
    S.op("dve", lambda e: e.tensor_scalar(out=muc[:], in0=muc[:], scalar1=-1.0, scalar2=1.0, op0=ALU.mult, op1=ALU.add),
         reads=["muc"], writes=["muc"])
    omka = nc.alloc_sbuf_tensor("omka", [128, 1], F32)
    S.op("dve", lambda e: e.tensor_scalar(out=omka[:], in0=pv[:, 1:2], scalar1=-1.0, scalar2=1.0, op0=ALU.mult, op1=ALU.add),
         reads=["pvec"], writes=["omka"])
    mtiles = mod_alloc(nc, 2)
    A1m = nc.alloc_sbuf_tensor("A1m", [128, 8, 2], F32)
    win_bf = nc.alloc_sbuf_tensor("win_bf", [128, 8, 640], BF16)
    x_t = nc.alloc_sbuf_tensor("x_t", [128, 8, 512], F32)
    h_bf = nc.alloc_sbuf_tensor("h_bf", [128, 8, 512], BF16)
    sq_bf = [nc.alloc_sbuf_tensor(f"sq_bf{i}", [128, 512], BF16) for i in range(2)]
    rs = nc.alloc_sbuf_tensor("rs", [128, 512], F32)
    t1 = [nc.alloc_sbuf_tensor(f"t1_{i}", [128, 512], F32) for i in range(3)]
    PP = nc.alloc_sbuf_tensor("PP", [128, 5, 512], F32)
    OUT = nc.alloc_sbuf_tensor("OUT", [128, 2, 6, 512], F32)
    BNO = nc.alloc_sbuf_tensor("BNO", [128, 2, 512], F32)
    AD = nc.alloc_sbuf_tensor("AD", [128, 512], F32)
    stA = nc.alloc_sbuf_tensor("stA", [128, 4096], F32)
    INs = [nc.alloc_sbuf_tensor(f"IN{i}", [128, 2, 6, 128], F32) for i in range(2)]
    CL = nc.alloc_sbuf_tensor("CL", [128, 2, 128], F32)
    CLX = nc.alloc_sbuf_tensor("CLX", [128, 2, 128], F32)
    TOT = nc.alloc_sbuf_tensor("TOT", [128, 2], F32)
    EG = nc.alloc_sbuf_tensor("EG", [128, 2], F32)
    E = [nc.alloc_sbuf_tensor(f"E{i}", [128, 2, 128], F32) for i in range(4)]
    RKp = [nc.alloc_sbuf_tensor(f"RKp{h}", [128, 2, 2, 128], F32) for h in range(2)]
    for h in range(2):
        S.op("pool", lambda e, h=h: e.memset(RKp[h][:], 0.0), writes=[f"RK{h}0", f"RK{h}1"])
    KB = nc.alloc_sbuf_tensor("KB", [128, 2, 2, 128], F32)
    HAT = nc.alloc_sbuf_tensor("HAT", [128, 2, 2, 128], F32)
    A1 = nc.alloc_sbuf_tensor("A1", [128, 2, 2, 2, 128], F32)
    A2 = nc.alloc_sbuf_tensor("A2", [128, 2, 2, 2, 128], F32)
    XK = [nc.alloc_sbuf_tensor(f"XK{i}", [128, 4, 128], F32) for i in range(2)]
    YK = [nc.alloc_sbuf_tensor(f"YK{i}", [128, 4, 128], F32) for i in range(2)]
    QT = nc.alloc_sbuf_tensor("QT", [128, 4, 128], F32)
    VT = nc.alloc_sbuf_tensor("VT", [128, 2, 128], F32)
    HT = nc.alloc_sbuf_tensor("HT", [128, 2, 2, 128], F32)
    Zs = nc.alloc_sbuf_tensor("Zs", [128, 256], F32)
    Us = nc.alloc_sbuf_tensor("Us", [128, 256], F32)
    Ys = nc.alloc_sbuf_tensor("Ys", [128, 256], F32)
    MST = nc.alloc_sbuf_tensor("MST", [128, 2, 64], F32)
    Y3 = nc.alloc_sbuf_tensor("Y3", [128, 2, 128], F32)
    B3 = nc.alloc_sbuf_tensor("B3", [128, 2, 128], F32)
    y3 = nc.alloc_sbuf_tensor("y3", [128, 128], F32)
    y3b = nc.alloc_sbuf_tensor("y3b", [128, 128], F32)
    st3 = nc.alloc_sbuf_tensor("st3", [128, 8], F32)
    print("sbuf remaining (rwkv)", nc.sbuf_bytes_remaining)

    mod = mod_vectors(nc, S, ada_w, ada_b, c2, [0, 1], bank[7], stA, "stA", mtiles)
    S.op("dve", lambda e: e.scalar_tensor_tensor(
        out=A1m[:], in0=mod[:, 8:16, :], scalar=1.0, in1=small["nrm"][:].unsqueeze(2).to_broadcast([128, 8, 2]),
        op0=ALU.add, op1=ALU.mult), reads=["mod_sb", "nrm"], writes=["A1m"])
    cast_weights(nc, S, w_in, win_bf, 8, 640, [(stA, "stA")], 640)
    S.barrier()

    def rms_stats(ncol):
        for k in range(8):
            b = k % 2
            S.op("act", lambda e, k=k, b=b: e.activation(out=sq_bf[b][:, 0:ncol], in_=x_t[:, k, 0:ncol], func=AF.Square),
                 reads=["x_t"], writes=[f"sq{b}"])
            S.op("pe", lambda e, k=k, b=b: e.matmul(bank[7][:, 0:ncol], ones_bf[:], sq_bf[b][:, 0:ncol], start=(k == 0), stop=(k == 7)),
                 reads=[f"sq{b}", "ones_bf"], writes=["bank7"])
        S.op("act", lambda e: e.activation(out=rs[:, 0:ncol], in_=bank[7][:, 0:ncol], func=AF.Sqrt, bias=eps6[:, 0:1], scale=1.0 / D),
             reads=["bank7", "eps6"], writes=["rs"])
        S.op("dve", lambda e: e.reciprocal(out=rs[:, 0:ncol], in_=rs[:, 0:ncol]), reads=["rs"], writes=["rs"])

    def pass1_tile(src, c0, no, tok0, v, first, last):
        n = no + 2
        S.op("sp", lambda e: e.dma_start(out=x_t[:, :, 0:n], in_=src[:, :, c0:c0 + n]), writes=["x_t"], dma=True)
        rms_stats(n)
        for k in range(8):
            b = k % 2
            S.op("dve", lambda e, k=k, b=b: e.tensor_tensor(out=t1[b][:, 0:n], in0=x_t[:, k, 0:n], in1=rs[:, 0:n], op=ALU.mult),
                 reads=["x_t", "rs"], writes=[f"t1_{b}"])
            S.op("act", lambda e, k=k, b=b: e.activation(out=h_bf[:, k, 0:n], in_=t1[b][:, 0:n], func=AF.Identity,
                                                         bias=mod[:, k, v:v + 1], scale=A1m[:, k, v:v + 1]),
                 reads=[f"t1_{b}", "A1m", "mod_sb"], writes=["h_bf"])
        if first:
            S.op("dve", lambda e: e.tensor_scalar(out=h_bf[:, :, 0:1], in0=h_bf[:, :, 0:1], scalar1=0.0, scalar2=None, op0=ALU.mult),
                 reads=["h_bf"], writes=["h_bf"])
        if last:
            S.op("dve", lambda e: e.tensor_scalar(out=h_bf[:, :, n - 1:n], in0=h_bf[:, :, n - 1:n], scalar1=0.0, scalar2=None, op0=ALU.mult),
                 reads=["h_bf"], writes=["h_bf"])
        dsts = [OUT[:, 0, 0, 0:no], PP[:, 1, 0:no], OUT[:, 0, 2, 0:no], PP[:, 3, 0:no], PP[:, 4, 0:no]]
        for c in range(5):
            ps = bank[c]
            for k in range(8):
                S.op("pe", lambda e, ps=ps, k=k, c=c: e.matmul(ps[:, 0:n], win_bf[:, k, c * 128:(c + 1) * 128], h_bf[:, k, 0:n],
                                                              start=(k == 0), stop=(k == 7)),
                     reads=[("w", "win_bf"), "h_bf"], writes=[f"bank{c}"])
            dst = dsts[c]
            S.op("act", lambda e, ps=ps, c=c, dst=dst: e.activation(out=dst, in_=ps[:, 1:1 + no], func=AF.Identity, scale=muc[:, c:c + 1]),
                 reads=[f"bank{c}", "muc"], writes=[("p", c)])
            S.op("dve", lambda e, ps=ps, c=c, dst=dst: e.scalar_tensor_tensor(out=dst, in0=ps[:, 0:no], scalar=small["mu"][:, c, 0:1], in1=dst,
                                                                              op0=ALU.mult, op1=ALU.add),
                 reads=[f"bank{c}", "mu", ("p", c)], writes=[("p", c)])
            S.op("dve", lambda e, ps=ps, c=c, dst=dst: e.scalar_tensor_tensor(out=dst, in0=ps[:, 2:2 + no], scalar=small["mu"][:, c, 1:2], in1=dst,
                                                                              op0=ALU.mult, op1=ALU.add),
                 reads=[f"bank{c}", "mu", ("p", c)], writes=[("p", c)])
        r_ = OUT[:, 0, 0, 0:no]
        k_ = PP[:, 1, 0:no]
        v_ = OUT[:, 0, 2, 0:no]
        kap = OUT[:, 0, 1, 0:no]
        T0, T1, T2 = t1[0][:, 0:no], t1[1][:, 0:no], t1[2][:, 0:no]
        S.op("dve", lambda e: e.tensor_scalar(out=T0, in0=k_, scalar1=pv[:, 0:1], scalar2=None, op0=ALU.mult),
             reads=[("p", 1), "pvec"], writes=["t1_0"])
        S.op("act", lambda e: e.activation(out=T1, in_=T0, func=AF.Square), reads=["t1_0"], writes=["t1_1"])
        S.op("pe", lambda e: e.matmul(bank[5][:, 0:no], small["bones"][:], T1, start=True, stop=True), reads=["bones", "t1_1"], writes=["bank5"])
        S.op("dve", lambda e: e.tensor_scalar(out=T1, in0=bank[5][:, 0:no], scalar1=1e-12, scalar2=None, op0=ALU.max),
             reads=["bank5", "t1_1"], writes=["t1_1"])
        S.op("act", lambda e: e.activation(out=T1, in_=T1, func=AF.Sqrt), reads=["t1_1"], writes=["t1_1"])
        S.op("dve", lambda e: e.reciprocal(out=T1, in_=T1), reads=["t1_1"], writes=["t1_1"])
        S.op("dve", lambda e: e.tensor_tensor(out=kap, in0=T0, in1=T1, op=ALU.mult), reads=["t1_0", "t1_1"], writes=["kap"])
        S.op("act", lambda e: e.activation(out=t1[2][0:64, 0:no], in_=PP[0:64, 3, 0:no], func=AF.Tanh), reads=[("p", 3)], writes=["t1_2"])
        for d in range(2):
            S.op("pe", lambda e, d=d: e.matmul(bank[5][:, 0:no], small["lor"][0:64, d, :], t1[2][0:64, 0:no], start=True, stop=True),
                 reads=["lor", "t1_2"], writes=["bank5"])
            S.op("act", lambda e, d=d: e.activation(out=OUT[:, d, 5, 0:no], in_=bank[5][:, 0:no], func=AF.Sigmoid, bias=pv[:, 3 + d:4 + d]),
                 reads=["bank5", "pvec"], writes=[("lw", d)])
            S.op("dve", lambda e, d=d: e.tensor_scalar(out=OUT[:, d, 5, 0:no], in0=OUT[:, d, 5, 0:no], scalar1=LWC, scalar2=None, op0=ALU.mult),
                 reads=[("lw", d)], writes=[("lw", d)])
            S.op("pe", lambda e, d=d: e.matmul(bank[6][:, 0:no], small["lor"][64:128, d, :], PP[64:128, 3, 0:no], start=True, stop=True),
                 reads=["lor", ("p", 3)], writes=["bank6"])
            S.op("act", lambda e, d=d: e.activation(out=AD[:, 0:no], in_=bank[6][:, 0:no], func=AF.Sigmoid, bias=pv[:, 5 + d:6 + d]),
                 reads=["bank6", "pvec"], writes=["AD"])
            S.op("dve", lambda e: e.tensor_scalar(out=T0, in0=AD[:, 0:no], scalar1=pv[:, 1:2], scalar2=omka[:, 0:1], op0=ALU.mult, op1=ALU.add),
                 reads=["AD", "pvec", "omka"], writes=["t1_0"])
            S.op("dve", lambda e, d=d: e.tensor_tensor(out=OUT[:, d, 3, 0:no], in0=k_, in1=T0, op=ALU.mult), reads=[("p", 1), "t1_0"], writes=[("kd", d)])
            S.op("dve", lambda e, d=d: e.tensor_tensor(out=OUT[:, d, 4, 0:no], in0=kap, in1=AD[:, 0:no], op=ALU.mult), reads=["kap", "AD"], writes=[("bd", d)])
        S.op("pool", lambda e: e.tensor_copy(out=OUT[:, 1, 0:3, 0:no], in_=OUT[:, 0, 0:3, 0:no]), reads=[("p", 0), ("p", 2), "kap"], writes=["out1c"])
        S.op("dve", lambda e: e.tensor_tensor(out=T0, in0=OUT[:, 0, 3, 0:no], in1=OUT[:, 1, 3, 0:no], op=ALU.add), reads=[("kd", 0), ("kd", 1)], writes=["t1_0"])
        S.op("dve", lambda e: e.scalar_tensor_tensor(out=T0, in0=T0, scalar=pv[:, 2:3], in1=r_, op0=ALU.mult, op1=ALU.mult),
             reads=["t1_0", "pvec", ("p", 0)], writes=["t1_0"])
        S.op("pe", lambda e: e.matmul(bank[5][:, 0:no], small["bones"][:], T0, start=True, stop=True), reads=["bones", "t1_0"], writes=["bank5"])
        S.op("dve", lambda e: e.tensor_tensor(out=BNO[:, 0, 0:no], in0=bank[5][:, 0:no], in1=v_, op=ALU.mult), reads=["bank5", ("p", 2)], writes=["bno0"])
        S.op("act", lambda e: e.activation(out=BNO[:, 1, 0:no], in_=PP[:, 4, 0:no], func=AF.Sigmoid), reads=[("p", 4)], writes=["bno1"])
        allk = [("p", 0), ("p", 2), "kap", ("lw", 0), ("lw", 1), ("kd", 0), ("kd", 1), ("bd", 0), ("bd", 1), "out1c"]
        S.op("pool", lambda e: e.dma_start(out=SCR[0][:, :, tok0:tok0 + no], in_=OUT[:, 0, :, 0:no]), reads=allk, dma=True)
        S.op("pool", lambda e: e.dma_start(out=SCR[1][:, :, tok0:tok0 + no], in_=OUT[:, 1, :, 0:no]), reads=allk, dma=True)
        S.op("pool", lambda e: e.dma_start(out=BN[:, :, tok0:tok0 + no], in_=BNO[:, :, 0:no]), reads=["bno0", "bno1"], dma=True)

    def mm(out, lhsT, rhs, start, stop, reads, wkey):
        S.op("pe", lambda e: e.matmul(out, lhsT, rhs, start=start, stop=stop), reads=reads, writes=[wkey])

    def scan_step(step):
        cF = step
        cB = (1 - step) if step < 2 else (NCHK - 1 - (step - 2))
        tF, tB = cF * 128, cB * 128
        par = step % 2
        IN = INs[par]
        S.op("sp", lambda e: e.dma_start(out=IN[:, 0], in_=SCR[0][:, :, tF:tF + 128]), writes=[f"IN0_{par}"], dma=True)
        S.op("sp", lambda e: e.dma_start(out=IN[:, 1], in_=SCR[1][:, :, tB:tB + 128]), writes=[f"IN1_{par}"], dma=True)
        INk = [f"IN0_{par}", f"IN1_{par}"]
        for d in range(2):
            S.op("dve", lambda e, d=d: e.tensor_tensor_scan(out=CL[:, d, :], data0=ones_f[:], data1=IN[:, d, 5, :], initial=0.0,
                                                            op0=ALU.mult, op1=ALU.add), reads=[INk[d], "ones_f"], writes=[("CL", d)])
        S.op("dve", lambda e: e.tensor_copy(out=TOT[:], in_=CL[:, :, 127]), reads=[("CL", 0), ("CL", 1)], writes=["TOT"])
        S.op("dve", lambda e: e.tensor_scalar(out=CL[:, 1, :], in0=CL[:, 1, :], scalar1=-1.0, scalar2=TOT[:, 1:2], op0=ALU.mult, op1=ALU.add),
             reads=[("CL", 1), "TOT"], writes=[("CL", 1)])
        S.op("dve", lambda e: e.tensor_tensor(out=CL[:, 1, :], in0=CL[:, 1, :], in1=IN[:, 1, 5, :], op=ALU.add), reads=[("CL", 1), INk[1]], writes=[("CL", 1)])
        CLk = [("CL", 0), ("CL", 1)]
        S.op("dve", lambda e: e.tensor_tensor(out=CLX[:], in0=CL[:], in1=IN[:, :, 5, :], op=ALU.subtract), reads=CLk + INk, writes=["CLX"])
        S.op("act", lambda e: e.activation(out=E[0][:], in_=CL[:], func=AF.Exp), reads=CLk, writes=["E0"])
        S.op("act", lambda e: e.activation(out=E[1][:], in_=CLX[:], func=AF.Exp), reads=["CLX"], writes=["E1"])
        S.op("act", lambda e: e.activation(out=E[2][:], in_=CL[:], func=AF.Exp, scale=-1.0), reads=CLk, writes=["E2"])
        for d in range(2):
            S.op("act", lambda e, d=d: e.activation(out=E[3][:, d, :], in_=CL[:, d, :], func=AF.Exp, scale=-1.0, bias=TOT[:, d:d + 1]),
                 reads=CLk + ["TOT"], writes=[("E3", d)])
        S.op("act", lambda e: e.activation(out=EG[:], in_=TOT[:], func=AF.Exp), reads=["TOT"], writes=["EG"])
        E3k = [("E3", 0), ("E3", 1)]
        for h in range(2):
            hs = slice(h * 64, (h + 1) * 64)
            S.op("dve" if h == 0 else "pool", lambda e, h=h, hs=hs: e.tensor_tensor(out=RKp[h][hs, :, 0, :], in0=IN[hs, :, 1, :], in1=E[1][hs], op=ALU.mult),
                 reads=INk + ["E1"], writes=[f"RK{h}0"])
            S.op("pool" if h == 0 else "dve", lambda e, h=h, hs=hs: e.tensor_tensor(out=RKp[h][hs, :, 1, :], in0=IN[hs, :, 0, :], in1=E[0][hs], op=ALU.mult),
                 reads=INk + ["E0"], writes=[f"RK{h}1"])
        S.op("dve", lambda e: e.tensor_tensor(out=KB[:, :, 0, :], in0=IN[:, :, 3, :], in1=E[2][:], op=ALU.mult), reads=INk + ["E2"], writes=["KB0"])
        S.op("pool", lambda e: e.tensor_tensor(out=KB[:, :, 1, :], in0=IN[:, :, 4, :], in1=E[2][:], op=ALU.mult), reads=INk + ["E2"], writes=["KB1"])
        S.op("pool", lambda e: e.tensor_tensor(out=HAT[:, :, 0, :], in0=IN[:, :, 3, :], in1=E[3][:], op=ALU.mult), reads=INk + E3k, writes=["HAT0"])
        S.op("dve", lambda e: e.scalar_tensor_tensor(out=HAT[:, :, 1, :], in0=IN[:, :, 4, :], scalar=-1.0, in1=E[3][:], op0=ALU.mult, op1=ALU.mult),
             reads=INk + E3k, writes=["HAT1"])
        RKk, KBk, HATk = ["RK00", "RK01", "RK10", "RK11"], ["KB0", "KB1"], ["HAT0", "HAT1"]
        for d in range(2):
            mm(bank[0][:, 256 + d * 128:256 + (d + 1) * 128], IN[:, d, 2, :], small["ident"][:], True, True, [INk[d], "ident"], "bank0")
        S.op("act", lambda e: e.copy(out=VT[:], in_=bank[0][:, 256:512].rearrange("p (d q) -> p d q", d=2)), reads=["bank0"], writes=["VT"])
        for d in range(2):
            for m_ in range(2):
                o = (d * 2 + m_) * 128
                mm(bank[4][:, o:o + 128], HAT[:, d, m_, :], small["ident"][:], True, True, HATk + ["ident"], "bank4")
        S.op("act", lambda e: e.copy(out=HT[:], in_=bank[4][:, 0:512].rearrange("p (d m q) -> p d m q", d=2, m=2)), reads=["bank4"], writes=["HT"])
        for h in range(2):
            hs = slice(h * 64, (h + 1) * 64)
            for d in range(2):
                rk2 = RKp[h][:, d, :, :].rearrange("p m q -> p (m q)")
                mm(bank[1][:, d * 256:(d + 1) * 256] if h == 0 else bank[5][:, d * 256:(d + 1) * 256],
                   KB[:, d, 0, :], rk2, True, True, KBk + RKk, "bank1" if h == 0 else "bank5")
                mm(bank[2][:, d * 256:(d + 1) * 256] if h == 0 else bank[3][:, d * 256:(d + 1) * 256],
                   KB[:, d, 1, :], rk2, True, True, KBk + RKk, "bank2" if h == 0 else "bank3")
                mm(bank[6][:, (h * 2 + d) * 128:(h * 2 + d + 1) * 128], RKp[h][:, d, 0, :], KB[:, d, 1, :], True, True, KBk + RKk, "bank6")
        g1b = [bank[1], bank[5]]
        g2b = [bank[2], bank[3]]
        mk = small["masks"]
        for h in range(2):
            S.op("dve", lambda e, h=h: e.tensor_tensor(out=A1[:, h].rearrange("p d m q -> p (d m) q"),
                                                       in0=g1b[h][:, 0:512].rearrange("p (a q) -> p a q", a=4),
                                                       in1=mk[:].rearrange("p d m q -> p (d m) q"), op=ALU.mult),
                 reads=[("bank1", "bank5")[h], "masks"], writes=[("A1", h)])
            S.op("dve", lambda e, h=h: e.scalar_tensor_tensor(out=A2[:, h].rearrange("p d m q -> p (d m) q"),
                                                              in0=g2b[h][:, 0:512].rearrange("p (a q) -> p a q", a=4), scalar=-1.0,
                                                              in1=mk[:].rearrange("p d m q -> p (d m) q"), op0=ALU.mult, op1=ALU.mult),
                 reads=[("bank2", "bank3")[h], "masks"], writes=[("A2", h)])
            S.op("dve", lambda e, h=h: e.scalar_tensor_tensor(out=XK[0][:, h * 2:(h + 1) * 2, :],
                                                              in0=bank[6][:, h * 256:(h + 1) * 256].rearrange("p (d q) -> p d q", d=2), scalar=-1.0,
                                                              in1=small["mask3"][:], op0=ALU.mult, op1=ALU.mult),
                 reads=["bank6", "mask3"], writes=["XK0"])
            S.op("pool", lambda e, h=h: e.tensor_copy(out=YK[0][:, h * 2:(h + 1) * 2, :], in_=A2[:, h, :, 0, :]), reads=[("A2", h)], writes=["YK0"])
            S.op("pool", lambda e, h=h: e.tensor_tensor(out=QT[:, h * 2:(h + 1) * 2, :], in0=A2[:, h, :, 0, :],
                                                        in1=small["ident"][:].unsqueeze(1).to_broadcast([128, 2, 128]), op=ALU.add),
                 reads=[("A2", h), "ident"], writes=["QT"])
        for lv in range(1, 7):
            xp, yp = XK[(lv - 1) % 2], YK[(lv - 1) % 2]
            xn, yn = XK[lv % 2], YK[lv % 2]
            xpk, ypk, xnk, ynk = f"XK{(lv - 1) % 2}", f"YK{(lv - 1) % 2}", f"XK{lv % 2}", f"YK{lv % 2}"
            for c in range(4):
                mm(bank[5][:, c * 128:(c + 1) * 128], yp[:, c, :], xp[:, c, :], True, True, [xpk, ypk], "bank5")
            S.op("act", lambda e, xn=xn: e.copy(out=xn[:], in_=bank[5][:, 0:512].rearrange("p (c q) -> p c q", c=4)), reads=["bank5"], writes=[xnk])
            if lv < 6:
                for c in range(4):
                    mm(bank[6][:, c * 128:(c + 1) * 128], xp[:, c, :], yp[:, c, :], True, True, [xpk, ypk], "bank6")
                S.op("dve", lambda e, yn=yn: e.tensor_copy(out=yn[:], in_=bank[6][:, 0:512].rearrange("p (c q) -> p c q", c=4)), reads=["bank6"], writes=[ynk])
            for c in range(4):
                mm(bank[7][:, c * 128:(c + 1) * 128], xn[:, c, :], QT[:, c, :], True, True, [xnk, "QT"], "bank7")
            S.op("dve", lambda e: e.tensor_tensor(out=QT[:], in0=QT[:], in1=bank[7][:, 0:512].rearrange("p (c q) -> p c q", c=4), op=ALU.add),
                 reads=["QT", "bank7"], writes=["QT"])
        for h in range(2):
            hs = slice(h * 64, (h + 1) * 64)
            for d in range(2):
                c = h * 2 + d
                mm(bank[0][:, c * 64:(c + 1) * 64], RKp[h][:, d, 0, :], MST[:, d, :], True, False, RKk + ["MST"], "bank0")
                mm(bank[0][:, c * 64:(c + 1) * 64], A1[:, h, d, 0, :], VT[:, d, hs], False, True, [("A1", h), "VT"], "bank0")
        S.op("act", lambda e: e.copy(out=Zs[:], in_=bank[0][:, 0:256]), reads=["bank0"], writes=["Zs"])
        for c in range(4):
            mm(bank[1][:, c * 64:(c + 1) * 64], QT[:, c, :], Zs[:, c * 64:(c + 1) * 64], True, True, ["QT", "Zs"], "bank1")
        S.op("dve", lambda e: e.tensor_copy(out=Us[:], in_=bank[1][:, 0:256]), reads=["bank1"], writes=["Us"])
        for h in range(2):
            hs = slice(h * 64, (h + 1) * 64)
            for d in range(2):
                c = h * 2 + d
                mm(bank[2][:, c * 64:(c + 1) * 64], RKp[h][:, d, 1, :], MST[:, d, :], True, False, RKk + ["MST"], "bank2")
                mm(bank[2][:, c * 64:(c + 1) * 64], A1[:, h, d, 1, :], VT[:, d, hs], False, False, [("A1", h), "VT"], "bank2")
                mm(bank[2][:, c * 64:(c + 1) * 64], A2[:, h, d, 1, :], Us[:, c * 64:(c + 1) * 64], False, True, [("A2", h), "Us"], "bank2")
        for h in range(2):
            hs = slice(h * 64, (h + 1) * 64)
            for d in range(2):
                c = h * 2 + d
                mm(bank[3][:, c * 64:(c + 1) * 64], HT[:, d, 0, :], VT[:, d, hs], True, False, ["HT", "VT"], "bank3")
                mm(bank[3][:, c * 64:(c + 1) * 64], HT[:, d, 1, :], Us[:, c * 64:(c + 1) * 64], False, True, ["HT", "Us"], "bank3")
        S.op("act", lambda e: e.copy(out=Ys[:], in_=bank[2][:, 0:256]), reads=["bank2"], writes=["Ys"])
        S.op("dve", lambda e: e.tensor_tensor(out=MST[:], in0=MST[:], in1=EG[:].unsqueeze(2).to_broadcast([128, 2, 64]), op=ALU.mult),
             reads=["MST", "EG"], writes=["MST"])
        for h in range(2):
            hs = slice(h * 64, (h + 1) * 64)
            S.op("dve", lambda e, h=h, hs=hs: e.tensor_tensor(out=MST[hs], in0=MST[hs],
                                                             in1=bank[3][hs, h * 128:(h + 1) * 128].rearrange("p (d q) -> p d q", d=2), op=ALU.add),
                 reads=["MST", "bank3"], writes=["MST"])
        Yv = Ys[:].rearrange("p (h d q) -> p h d q", h=2, d=2)
        S.op("pool", lambda e: e.dma_start(out=YS[tF:tF + 128, 0, :].rearrange("t (h q) -> t h q", h=2), in_=Yv[:, :, 0, :]), reads=["Ys"], writes=["ysf"], dma=True)
        S.op("pool", lambda e: e.dma_start(out=YS[tB:tB + 128, 1, :].rearrange("t (h q) -> t h q", h=2), in_=Yv[:, :, 1, :]), reads=["Ys"], writes=["ysb"], dma=True)

    def pass3_chunk(c):
        t0 = c * 128
        S.op("sp", lambda e: e.dma_start(out=Y3[:], in_=YS[t0:t0 + 128, :, :]), writes=["Y3"], dma=True)
        S.op("sp", lambda e: e.dma_start(out=B3[:], in_=BN[:, :, t0:t0 + 128]), writes=["B3"], dma=True)
        mm(bank[0][:, 0:128], B3[:, 0, :], small["ident"][:], True, True, ["B3", "ident"], "bank0")
        mm(bank[0][:, 128:256], B3[:, 1, :], small["gup"][:], True, True, ["B3", "gup"], "bank0")
        S.op("dve", lambda e: e.tensor_tensor(out=y3[:], in0=Y3[:, 0, :], in1=Y3[:, 1, :], op=ALU.add), reads=["Y3"], writes=["y3"])
        S.op("dve", lambda e: e.reduce_sum(out=st3[:, 0:2], in_=y3[:].rearrange("p (h q) -> p h q", h=2), axis=AX.X), reads=["y3"], writes=["st3a"])
        S.op("act", lambda e: e.activation(out=y3b[:], in_=y3[:], func=AF.Square), reads=["y3"], writes=["y3b"])
        S.op("dve", lambda e: e.reduce_sum(out=st3[:, 2:4], in_=y3b[:].rearrange("p (h q) -> p h q", h=2), axis=AX.X), reads=["y3b"], writes=["st3b"])
        S.op("dve", lambda e: e.tensor_scalar(out=st3[:, 0:4], in0=st3[:, 0:4], scalar1=1.0 / 64, scalar2=None, op0=ALU.mult),
             reads=["st3a", "st3b"], writes=["st3c"])
        S.op("dve", lambda e: e.tensor_tensor(out=st3[:, 4:6], in0=st3[:, 0:2], in1=st3[:, 0:2], op=ALU.mult), reads=["st3c"], writes=["st3d"])
        S.op("dve", lambda e: e.tensor_tensor(out=st3[:, 4:6], in0=st3[:, 2:4], in1=st3[:, 4:6], op=ALU.subtract), reads=["st3c", "st3d"], writes=["st3d"])
        S.op("act", lambda e: e.activation(out=st3[:, 6:8], in_=st3[:, 4:6], func=AF.Sqrt, bias=epsg[:, 0:1]), reads=["st3d", "epsg"], writes=["st3e"])
        S.op("dve", lambda e: e.reciprocal(out=st3[:, 6:8], in_=st3[:, 6:8]), reads=["st3e"], writes=["st3e"])
        y3v = y3[:].rearrange("p (h q) -> p h q", h=2)
        S.op("dve", lambda e: e.tensor_tensor(out=y3v, in0=y3v, in1=st3[:, 0:2].unsqueeze(2).to_broadcast([128, 2, 64]), op=ALU.subtract),
             reads=["y3", "st3c", "y3b"], writes=["y3"])
        S.op("dve", lambda e: e.tensor_tensor(out=y3v, in0=y3v, in1=st3[:, 6:8].unsqueeze(2).to_broadcast([128, 2, 64]), op=ALU.mult),
             reads=["y3", "st3e"], writes=["y3"])
        S.op("dve", lambda e: e.tensor_tensor(out=y3[:], in0=y3[:], in1=small["lnwb"][:, 0, :], op=ALU.mult), reads=["y3", "lnwb"], writes=["y3"])
        S.op("dve", lambda e: e.tensor_tensor(out=y3[:], in0=y3[:], in1=small["lnwb"][:, 1, :], op=ALU.add), reads=["y3", "lnwb"], writes=["y3"])
        S.op("dve", lambda e: e.tensor_tensor(out=y3[:], in0=y3[:], in1=bank[0][:, 0:128], op=ALU.add), reads=["y3", "bank0"], writes=["y3"])
        S.op("dve", lambda e: e.tensor_tensor(out=y3b[:], in0=y3[:], in1=bank[0][:, 128:256], op=ALU.mult), reads=["y3", "bank0", "y3b"], writes=["y3b"])
        tok = S.op("pool", lambda e: e.dma_start(out=rw_out[t0:t0 + 128, :], in_=y3b[:]), reads=["y3b"], dma=True)
        S.final_tokens.append(tok)

    with nc.allow_low_precision("bf16 matmul operands for the input projection, fp32 elsewhere"):
        pass1_tile(cT, 0, CTXL, 0, 1, True, True)
        nt = (SEQ + RW_TILE - 1) // RW_TILE
        for ti in range(nt):
            o0 = ti * RW_TILE
            no = min(RW_TILE, SEQ - o0)
            pass1_tile(xT, o0, no, CTXL + o0, 0, ti == 0, ti == nt - 1)
        S.barrier()
        if debug == "pass1":
            dbg = nc.dram_tensor("dbg", [128, 14, 512], F32, kind="ExternalOutput").ap()
            toks = [S.op("sp", lambda e: e.dma_start(out=dbg[:, 0:6, :], in_=SCR[0][:, :, 0:512]), dma=True),
                    S.op("sp", lambda e: e.dma_start(out=dbg[:, 6:12, :], in_=SCR[1][:, :, 0:512]), dma=True),
                    S.op("sp", lambda e: e.dma_start(out=dbg[:, 12:14, :], in_=BN[:, :, 0:512]), dma=True)]
            S.emit(toks)
            return nc
        S.op("dve", lambda e: e.tensor_scalar(out=MST[:], in0=small["ident"][:, 0:128].rearrange("p (d q) -> p d q", d=2), scalar1=0.0,
                                              scalar2=None, op0=ALU.mult), reads=["ident", "MST"], writes=["MST"])
        for step in range(nsteps):
            scan_step(step)
        S.barrier()
        for c in range(NCHK):
            pass3_chunk(c)
        S.emit(S.final_tokens)
    return nc


def pad1(a):
    return np.concatenate([np.zeros((1, a.shape[1]), a.dtype), a, np.zeros((1, a.shape[1]), a.dtype)], 0)


def rwkv_inputs(core, z, layer=0):
    b, g = divmod(core, 4)
    w = z["ev_w_in"][0]
    own = g * 128 + np.arange(128)
    rc = np.concatenate([own, 512 + own, 1024 + own, 1536 + np.arange(128), 1536 + 128 + np.arange(128)])
    cols = 1536 + rc
    mu = z["rwkv_mu"][0][:, rc]
    flat = lambda a: a.reshape(-1)
    pvec = np.stack([z["rwkv_k_k"][0][own], z["rwkv_k_a"][0][own], flat(z["rwkv_r_k"][0])[own],
                     z["rwkv_w0"][0][0][own], z["rwkv_w0"][0][1][own], z["rwkv_a0"][0][0][own], z["rwkv_a0"][0][1][own],
                     np.zeros(128, np.float32)], 1).astype(np.float32)
    lor = np.zeros((128, 2, 128), np.float32)
    for d in range(2):
        lor[0:64, d, :] = z["rwkv_w_up"][0][d][:, own]
        lor[64:128, d, :] = z["rwkv_a_up"][0][d][:, own]
    idx = np.arange(128)
    su = (idx[:, None] < idx[None, :]).astype(np.float32)
    iu = (idx[:, None] <= idx[None, :]).astype(np.float32)
    sl = (idx[:, None] > idx[None, :]).astype(np.float32)
    il = (idx[:, None] >= idx[None, :]).astype(np.float32)
    masks = np.stack([np.stack([su, iu], 1), np.stack([sl, il], 1)], 1)
    mask3 = np.stack([sl, su], 1)
    hh = idx // 64
    rep = lambda v: np.ascontiguousarray(np.broadcast_to(v.astype(np.float32), (128,) + v.shape))
    return {
        "xT": fm(pad1(z["x"][b])), "cT": fm(pad1(z["ctx"][b])),
        "w_in": kch(np.ascontiguousarray(w[:, cols])),
        "ada_w": kch(z["ada_w"][layer]), "ada_b": vec_fm(z["ada_b"][layer]),
        "c2": np.ascontiguousarray(np.stack([vec_fm(z["c"][b]), vec_fm(z["c_ctx"])], axis=-1)),
        "nrm": vec_fm(z["norm_mix"][layer]),
        "mu": np.ascontiguousarray(mu.T.reshape(5, 128, 2).transpose(1, 0, 2)),
        "pvec": pvec, "lor": lor,
        "gup": np.ascontiguousarray(z["rwkv_g_up"][0][:, own]),
        "lnwb": np.ascontiguousarray(np.stack([rep(z["rwkv_ln_w"][0][own]), rep(z["rwkv_ln_b"][0][own])], 1)),
        "ident": np.eye(128, dtype=np.float32),
        "bones": (hh[:, None] == hh[None, :]).astype(np.float32),
        "masks": np.ascontiguousarray(masks.astype(np.float32)), "mask3": np.ascontiguousarray(mask3.astype(np.float32)),
    }


def kernel(**z):
    z = {k: np.asarray(v, dtype=np.float32) for k, v in z.items()}
    B, T = 2, SEQ
    cores = list(range(8))
    ncA = build_even()
    resA = run_bass_kernel_spmd(ncA, [even_inputs(c, z) for c in cores], core_ids=cores)
    mix0 = np.zeros((B, T, 1024), np.float32)
    mixc0 = np.zeros((B, CTXL, 1024), np.float32)
    for c in cores:
        b, g = divmod(c, 4)
        mix0[b, :, g * 128:(g + 1) * 128] = resA.results[c]["attT"].T
        mixc0[b, :, g * 128:(g + 1) * 128] = resA.results[c]["attcT"].T
    del resA
    ncR = build_rwkv()
    resR = run_bass_kernel_spmd(ncR, [rwkv_inputs(c, z) for c in cores], core_ids=cores)
    for c in cores:
        b, g = divmod(c, 4)
        rw = resR.results[c]["rw_out"]
        mixc0[b, :, 512 + g * 128:512 + (g + 1) * 128] = rw[:CTXL]
        mix0[b, :, 512 + g * 128:512 + (g + 1) * 128] = rw[CTXL:]
    del resR
    ncB = build_ffn(8, True, False)
    mapsB = [ffn_inputs(c, z["x"], mix0, z["ctx"], mixc0, z["c"], z["c_ctx"], z["ada_w"][0], z["ada_b"][0],
                        z["norm_ffn"][0], z["ev_w_out"][0], z["ffn_w_up"][0], z["ffn_conv_w"][0], z["ffn_conv_b"][0],
                        z["ffn_w_down"][0], True, None) for c in cores]
    resB = run_bass_kernel_spmd(ncB, mapsB, core_ids=cores)
    x1 = np.zeros((B, T, 1024), np.float32)
    xc1 = np.zeros((B, CTXL, 1024), np.float32)
    for c in cores:
        b, q = divmod(c, 4)
        x1[b, q * TL:(q + 1) * TL] = fm_inv(resB.results[c]["xoT"])
        if q == 0:
            xc1[b] = fm_inv(resB.results[c]["xcoT"])
    del resB, mapsB
    ncC = build_mamba()
    resC = run_bass_kernel_spmd(ncC, [mamba_inputs(c, z, x1, xc1) for c in cores], core_ids=cores)
    mix1 = np.zeros((B, T, 2048), np.float32)
    for c in cores:
        b, g = divmod(c, 4)
        mix1[b, :, g * 512:(g + 1) * 512] = resC.results[c]["gy_out"]
    del resC
    ncD = build_ffn(16, False, True)
    mapsD = [ffn_inputs(c, x1, mix1, None, None, z["c"], z["c_ctx"], z["ada_w"][1], z["ada_b"][1],
                        z["norm_ffn"][1], z["ssm_w_out"][0], z["ffn_w_up"][1], z["ffn_conv_w"][1], z["ffn_conv_b"][1],
                        z["ffn_w_down"][1], False, z["final_norm"]) for c in cores]
    resD = run_bass_kernel_spmd(ncD, mapsD, core_ids=cores)
    out = np.zeros((B, T, 1024), np.float32)
    for c in cores:
        b, q = divmod(c, 4)
        out[b, q * TL:(q + 1) * TL] = fm_inv(resD.results[c]["xoT"])
    return out
```
